# Optimizing a Trainium2 kernel written in Bass

```python
import jax, jax.numpy as jnp
from jax import lax
import numpy as np


D_MODEL = 4096
BATCH = 4
SEQ = 2048
DEPTH = 1

D_CONV = D_MODEL // 2
CONV_WIDTH = 31
HEAD_DIM = 128
N_HEADS = D_MODEL // (2 * HEAD_DIM)
N_KV_GROUPS = 4
HEADS_PER_GROUP = N_HEADS // N_KV_GROUPS
CMP_BLOCK = 32
CMP_STRIDE = 16
CMP_HIDDEN = 2 * HEAD_DIM
SLC_BLOCK = 64
N_SELECT = 16
WINDOW = 512
WIN_BLOCK = 128
QUERY_CHUNK = 16
N_NSA_BRANCHES = 3
D_FF = 4 * D_MODEL
D_Q = N_HEADS * HEAD_DIM
D_KV = N_KV_GROUPS * HEAD_DIM
D_IN = 2 * D_CONV + D_Q + 6 * D_KV + N_NSA_BRANCHES * N_HEADS + 2 * D_MODEL

NORM_EPS = 1e-6
LN_EPS = 1e-5
FORCED_SCORE = 1e4
MASKED_SCORE = -1e4
NEG_INF = -1e30

kernel_name = 'hybrid_conformer_nsa_gated_block'


def rms_norm(x, g):
    xf = x.astype(jnp.float32)
    y = xf * lax.rsqrt(jnp.mean(xf * xf, axis=-1, keepdims=True) + NORM_EPS)
    return (y * g.astype(jnp.float32)).astype(x.dtype)


def layer_norm(x, g, b):
    xf = x.astype(jnp.float32)
    mu = jnp.mean(xf, axis=-1, keepdims=True)
    var = jnp.mean(jnp.square(xf - mu), axis=-1, keepdims=True)
    y = (xf - mu) * lax.rsqrt(var + LN_EPS)
    return (y * g.astype(jnp.float32) + b.astype(jnp.float32)).astype(x.dtype)


def masked_softmax(s, mask):
    s = jnp.where(mask, s.astype(jnp.float32), NEG_INF)
    m = jnp.max(s, axis=-1, keepdims=True)
    e = jnp.where(mask, jnp.exp(s - m), 0.0)
    return e / jnp.maximum(jnp.sum(e, axis=-1, keepdims=True), 1e-30)


def compress_blocks(kv, pos, w1, w2):
    B, S, G, d = kv.shape
    r = CMP_BLOCK // CMP_STRIDE
    chunks = kv.reshape(B, S // CMP_STRIDE, CMP_STRIDE, G, d)
    n_cmp = S // CMP_STRIDE - r + 1
    blocks = jnp.concatenate([chunks[:, i:i + n_cmp] for i in range(r)], axis=2)
    blocks = blocks + pos[None, None, :, None, :]
    blocks = blocks.transpose(0, 1, 3, 2, 4).reshape(B, n_cmp, G, CMP_BLOCK * d)
    return jax.nn.gelu(blocks @ w1) @ w2


def gather_blocks(blocks, idx):
    return blocks[idx]


def nsa_attention(q, kc, vc, ks, vs, kw, vw, gate_logits,
                  pos_cmp_k, w_cmp_k1, w_cmp_k2, pos_cmp_v, w_cmp_v1, w_cmp_v2):
    B, S = q.shape[:2]
    G, R, d = N_KV_GROUPS, HEADS_PER_GROUP, HEAD_DIM
    scale = HEAD_DIM ** -0.5
    q5 = q.reshape(B, S, G, R, d)
    kc, vc, ks, vs, kw, vw = [a.reshape(B, S, G, d) for a in (kc, vc, ks, vs, kw, vw)]
    t = np.arange(S)

    k_cmp = compress_blocks(kc, pos_cmp_k, w_cmp_k1, w_cmp_k2)
    v_cmp = compress_blocks(vc, pos_cmp_v, w_cmp_v1, w_cmp_v2)
    n_cmp = k_cmp.shape[1]
    cmp_start = np.arange(n_cmp) * CMP_STRIDE
    cmp_vis = (cmp_start[None, :] + CMP_BLOCK - 1) <= t[:, None]
    s_cmp = jnp.einsum('bsgrd,bngd->bgrsn', q5, k_cmp) * scale
    p_cmp = masked_softmax(s_cmp, cmp_vis)
    o_cmp = jnp.einsum('bgrsn,bngd->bsgrd', p_cmp.astype(v_cmp.dtype), v_cmp)

    n_slc = S // SLC_BLOCK
    slc_start = np.arange(n_slc) * SLC_BLOCK
    overlap = ((cmp_start[:, None] < slc_start[None, :] + SLC_BLOCK)
               & (cmp_start[:, None] + CMP_BLOCK > slc_start[None, :])).astype(np.float32)
    importance = jnp.einsum('bgsn,nm->bgsm', jnp.sum(p_cmp, axis=2), overlap)
    j = np.arange(n_slc)
    cur = t // SLC_BLOCK
    valid = slc_start[None, :] <= t[:, None]
    forced = (j[None, :] == 0) | (j[None, :] == cur[:, None]) | (j[None, :] == cur[:, None] - 1)
    sel_score = jnp.where(forced & valid, FORCED_SCORE, jnp.where(valid, importance, MASKED_SCORE))
    n_sel = min(N_SELECT, n_slc)
    _, sel_idx = lax.top_k(sel_score, n_sel)

    kb = ks.reshape(B, n_slc, SLC_BLOCK, G, d).transpose(0, 3, 1, 2, 4)
    vb = vs.reshape(B, n_slc, SLC_BLOCK, G, d).transpose(0, 3, 1, 2, 4)
    nq = S // QUERY_CHUNK
    q_chunks = q5.reshape(B, nq, QUERY_CHUNK, G, R, d).transpose(1, 0, 2, 3, 4, 5)
    idx_chunks = sel_idx.reshape(B, G, nq, QUERY_CHUNK, n_sel).transpose(2, 0, 1, 3, 4)
    t_chunks = jnp.arange(S, dtype=jnp.int32).reshape(nq, QUERY_CHUNK)
    gather2 = jax.vmap(jax.vmap(gather_blocks))

    def selected_chunk(args):
        qc, ic, tc = args
        kg = gather2(kb, ic)
        vg = gather2(vb, ic)
        s = jnp.einsum('bqgrd,bgqnld->bgrqnl', qc, kg) * scale
        key_pos = ic[..., None] * SLC_BLOCK + jnp.arange(SLC_BLOCK, dtype=jnp.int32)
        mask = key_pos <= tc[:, None, None]
        qc_len = qc.shape[1]
        p = masked_softmax(s.reshape(B, G, R, qc_len, n_sel * SLC_BLOCK),
                           mask[:, :, None].reshape(B, G, 1, qc_len, n_sel * SLC_BLOCK))
        p = p.reshape(B, G, R, qc_len, n_sel, SLC_BLOCK).astype(vg.dtype)
        return jnp.einsum('bgrqnl,bgqnld->bqgrd', p, vg)

    o_slc = lax.map(selected_chunk, (q_chunks, idx_chunks, t_chunks))
    o_slc = o_slc.transpose(1, 0, 2, 3, 4, 5).reshape(B, S, G, R, d)

    nb = S // WIN_BLOCK
    lb = WINDOW // WIN_BLOCK
    pad = ((0, 0), (lb * WIN_BLOCK, 0), (0, 0), (0, 0))
    kblk = jnp.pad(kw, pad).reshape(B, nb + lb, WIN_BLOCK, G, d)
    vblk = jnp.pad(vw, pad).reshape(B, nb + lb, WIN_BLOCK, G, d)
    k_win = jnp.concatenate([kblk[:, i:i + nb] for i in range(lb + 1)], axis=2)
    v_win = jnp.concatenate([vblk[:, i:i + nb] for i in range(lb + 1)], axis=2)
    qw = q5.reshape(B, nb, WIN_BLOCK, G, R, d)
    s_win = jnp.einsum('bnqgrd,bnkgd->bgrnqk', qw, k_win) * scale
    q_pos = t.reshape(nb, WIN_BLOCK)
    k_pos = np.arange(nb)[:, None] * WIN_BLOCK - lb * WIN_BLOCK + np.arange((lb + 1) * WIN_BLOCK)[None, :]
    win_mask = (k_pos[:, None, :] <= q_pos[:, :, None]) & (k_pos[:, None, :] > q_pos[:, :, None] - WINDOW)
    p_win = masked_softmax(s_win, win_mask).astype(v_win.dtype)
    o_win = jnp.einsum('bgrnqk,bnkgd->bnqgrd', p_win, v_win).reshape(B, S, G, R, d)

    g = jax.nn.sigmoid(gate_logits.astype(jnp.float32)).reshape(B, S, G, R, N_NSA_BRANCHES).astype(q.dtype)
    o = g[..., 0:1] * o_cmp + g[..., 1:2] * o_slc + g[..., 2:3] * o_win
    return o.reshape(B, S, D_Q)


def conformer_conv(glu_in, w_dw, b_dw, ln_g, ln_b):
    a, b = jnp.split(glu_in, 2, axis=-1)
    u = a * jax.nn.sigmoid(b)
    u = lax.conv_general_dilated(u, w_dw, window_strides=(1,), padding=[(CONV_WIDTH - 1, 0)],
                                 dimension_numbers=('NWC', 'WIO', 'NWC'),
                                 feature_group_count=D_CONV) + b_dw
    u = layer_norm(u, ln_g, ln_b)
    return jax.nn.silu(u)


def hybrid_layer(x, norm_mix_pre, w_in, pos_cmp_k, w_cmp_k1, w_cmp_k2, pos_cmp_v, w_cmp_v1, w_cmp_v2,
                 w_dw, b_dw, ln_conv_g, ln_conv_b, w_conv_out, w_attn_out, w_out, norm_mix_post,
                 norm_mlp_pre, w_up, w_down, norm_mlp_post):
    u = rms_norm(x, norm_mix_pre)
    proj = u @ w_in
    sizes = [2 * D_CONV, D_Q, D_KV, D_KV, D_KV, D_KV, D_KV, D_KV,
             N_NSA_BRANCHES * N_HEADS, D_MODEL, D_MODEL]
    offsets = np.cumsum(sizes)[:-1].tolist()
    (glu_in, q, kc, vc, ks, vs, kw, vw, nsa_gates, gate_a, gate_b) = jnp.split(proj, offsets, axis=-1)

    y_conv = conformer_conv(glu_in, w_dw, b_dw, ln_conv_g, ln_conv_b) @ w_conv_out
    y_attn = nsa_attention(q, kc, vc, ks, vs, kw, vw, nsa_gates,
                           pos_cmp_k, w_cmp_k1, w_cmp_k2, pos_cmp_v, w_cmp_v1, w_cmp_v2) @ w_attn_out
    merged = jax.nn.sigmoid(gate_a) * y_conv + jax.nn.sigmoid(gate_b) * y_attn
    x = x + rms_norm(merged @ w_out, norm_mix_post)

    h = rms_norm(x, norm_mlp_pre)
    h = jnp.square(jax.nn.relu(h @ w_up)) @ w_down
    return x + rms_norm(h, norm_mlp_post)


def setup_inputs(seed: int = 0) -> dict:
    key = jax.random.key(seed)
    k = jax.random.split(key, 21)

    def nrm(kk, shape, scale):
        return jax.random.normal(kk, shape, jnp.float32) * scale

    L = DEPTH
    return {
        'x': nrm(k[0], (BATCH, SEQ, D_MODEL), 1.0),
        'norm_mix_pre': 1.0 + nrm(k[1], (L, D_MODEL), 0.05),
        'w_in': nrm(k[2], (L, D_MODEL, D_IN), D_MODEL ** -0.5),
        'pos_cmp_k': nrm(k[3], (L, CMP_BLOCK, HEAD_DIM), 0.1),
        'w_cmp_k1': nrm(k[4], (L, CMP_BLOCK * HEAD_DIM, CMP_HIDDEN), (CMP_BLOCK * HEAD_DIM) ** -0.5),
        'w_cmp_k2': nrm(k[5], (L, CMP_HIDDEN, HEAD_DIM), CMP_HIDDEN ** -0.5),
        'pos_cmp_v': nrm(k[6], (L, CMP_BLOCK, HEAD_DIM), 0.1),
        'w_cmp_v1': nrm(k[7], (L, CMP_BLOCK * HEAD_DIM, CMP_HIDDEN), (CMP_BLOCK * HEAD_DIM) ** -0.5),
        'w_cmp_v2': nrm(k[8], (L, CMP_HIDDEN, HEAD_DIM), CMP_HIDDEN ** -0.5),
        'w_dw': nrm(k[9], (L, CONV_WIDTH, 1, D_CONV), CONV_WIDTH ** -0.5),
        'b_dw': nrm(k[10], (L, D_CONV), 0.02),
        'ln_conv_g': 1.0 + nrm(k[11], (L, D_CONV), 0.05),
        'ln_conv_b': nrm(k[12], (L, D_CONV), 0.02),
        'w_conv_out': nrm(k[13], (L, D_CONV, D_MODEL), D_CONV ** -0.5),
        'w_attn_out': nrm(k[14], (L, D_Q, D_MODEL), D_Q ** -0.5),
        'w_out': nrm(k[15], (L, D_MODEL, D_MODEL), D_MODEL ** -0.5),
        'norm_mix_post': 1.0 + nrm(k[16], (L, D_MODEL), 0.05),
        'norm_mlp_pre': 1.0 + nrm(k[17], (L, D_MODEL), 0.05),
        'w_up': nrm(k[18], (L, D_MODEL, D_FF), D_MODEL ** -0.5),
        'w_down': nrm(k[19], (L, D_FF, D_MODEL), D_FF ** -0.5),
        'norm_mlp_post': 1.0 + nrm(k[20], (L, D_MODEL), 0.05),
    }


def reference(x, norm_mix_pre, w_in, pos_cmp_k, w_cmp_k1, w_cmp_k2, pos_cmp_v, w_cmp_v1, w_cmp_v2,
              w_dw, b_dw, ln_conv_g, ln_conv_b, w_conv_out, w_attn_out, w_out, norm_mix_post,
              norm_mlp_pre, w_up, w_down, norm_mlp_post):
    for l in range(DEPTH):
        x = hybrid_layer(x, norm_mix_pre[l], w_in[l], pos_cmp_k[l], w_cmp_k1[l], w_cmp_k2[l],
                         pos_cmp_v[l], w_cmp_v1[l], w_cmp_v2[l], w_dw[l], b_dw[l], ln_conv_g[l],
                         ln_conv_b[l], w_conv_out[l], w_attn_out[l], w_out[l], norm_mix_post[l],
                         norm_mlp_pre[l], w_up[l], w_down[l], norm_mlp_post[l])
    return x
```

```python
from contextlib import ExitStack
import numpy as np
import ml_dtypes
import concourse.bass as bass
import concourse.mybir as mybir
from concourse.bass_utils import run_bass_kernel_spmd

F32 = mybir.dt.float32
BF16 = mybir.dt.bfloat16
AF = mybir.ActivationFunctionType
ALU = mybir.AluOpType
AX = mybir.AxisListType
NPBF = ml_dtypes.bfloat16

D = 4096
T = 1024
NT = 8
KC = 32
DFF = 16384
NCH_IN = 137
NORM_EPS = 1e-6
LN_EPS = 1e-5
SCALE = 128 ** -0.5
MASK_BIG = 32768.0
CH_GLU = 0
CH_Q = 32
CH_KV = 48
CH_GA = 72
CH_GB = 104
CH_NG = 136


class Sem:
    def __init__(self, nc, name):
        self.h = nc.alloc_semaphore(name)
        self.n = 0


class Buf:
    __slots__ = ("name", "w", "rs")

    def __init__(self, name=""):
        self.name = name
        self.w = None
        self.rs = {}


class Q:
    def __init__(self, nc, eng, name):
        self.eng = eng
        self.name = name
        self.sem = Sem(nc, "s_" + name)
        self.seen = {}

    def wait(self, t):
        if t is None:
            return
        s, v = t
        if self.seen.get(id(s), 0) >= v:
            return
        self.eng.wait_ge(s.h, v)
        self.seen[id(s)] = v


class KB:
    def __init__(self, nc):
        self.nc = nc
        self.pe = Q(nc, nc.tensor, "pe")
        self.act = Q(nc, nc.scalar, "act")
        self.dve = Q(nc, nc.vector, "dve")
        self.pool = Q(nc, nc.gpsimd, "pool")
        self.sp = Q(nc, nc.sync, "sp")
        self.qs = [self.pe, self.act, self.dve, self.pool, self.sp]
        self.dsems = {id(self.sp): [Sem(nc, f"dsp{i}") for i in range(24)],
                      id(self.pool): [Sem(nc, f"dpl{i}") for i in range(24)],
                      id(self.act): [Sem(nc, f"dac{i}") for i in range(8)]}
        self.dnext = {k: 0 for k in self.dsems}

    def _deps(self, q, r, w):
        for b in r:
            q.wait(b.w)
        for b in w:
            q.wait(b.w)
            for t in b.rs.values():
                q.wait(t)

    def _commit(self, t, r, w):
        for b in r:
            b.rs[id(t[0])] = t
        for b in w:
            b.w = t
            b.rs = {}

    def op(self, q, fn, r=(), w=()):
        self._deps(q, r, w)
        inst = fn()
        q.sem.n += 1
        inst.then_inc(q.sem.h, 1)
        t = (q.sem, q.sem.n)
        self._commit(t, r, w)
        return t

    def group(self, q, fns, r=(), w=()):
        self._deps(q, r, w)
        inst = None
        for fn in fns:
            inst = fn()
        q.sem.n += 1
        inst.then_inc(q.sem.h, 1)
        t = (q.sem, q.sem.n)
        self._commit(t, r, w)
        return t

    def dma(self, q, out, in_, r=(), w=(), **kw):
        self._deps(q, r, w)
        ring = self.dsems[id(q)]
        i = self.dnext[id(q)]
        self.dnext[id(q)] = (i + 1) % len(ring)
        s = ring[i]
        if s.n:
            q.wait((s, s.n))
        inst = q.eng.dma_start(out=out, in_=in_, **kw)
        s.n += 16
        inst.then_inc(s.h, 16)
        t = (s, s.n)
        self._commit(t, r, w)
        return t

    def barrier(self):
        ts = [(q.sem, q.sem.n) for q in self.qs if q.sem.n]
        for ring in self.dsems.values():
            ts += [(s, s.n) for s in ring if s.n]
        for q in self.qs:
            for t in ts:
                q.wait(t)


def build_nc(debug=False, upto="all"):
    nc = bass.Bass("TRN2", target_bir_lowering=False)
    order = ["A", "B", "C", "D", "E", "F", "all"]
    lvl = order.index(upto)

    def din(name, shape, dt=F32):
        return nc.dram_tensor(name, list(shape), dt, kind="ExternalInput").ap()

    def dscr(name, shape, dt=F32):
        kind = "ExternalOutput" if debug else "Internal"
        return nc.dram_tensor(name, list(shape), dt, kind=kind).ap()

    xo = din("xo", [T, D])
    xc = din("xc", [T, D])
    w_in_t = din("w_in_t", [NCH_IN, 128, D])
    w_co_t = din("w_co_t", [32, 128, 16 * 128])
    w_ao_t = din("w_ao_t", [32, 128, 16 * 128])
    w_out_t = din("w_out_t", [8, 4, 128, D])
    w_up_t = din("w_up_t", [128, 128, D])
    w_dn_t = din("w_dn_t", [4, 8, 4, 128, D])
    gains = din("gains", [4, D])
    convp = din("convp", [128, 16, 34])
    w1k_t = din("w1k_t", [128, 32 * 256])
    w1v_t = din("w1v_t", [128, 32 * 256])
    w2k_t = din("w2k_t", [128, 2 * 128])
    w2v_t = din("w2v_t", [128, 2 * 128])
    posT = din("posT", [128, 64])
    m8 = din("m8", [128, 8 * 512], BF16)
    cvis = din("cvis", [128, T], BF16)
    selv = din("selv", [128, NT * 32])
    selb = din("selb", [128, NT * 32])
    rcc = din("rcc", [128, 33], BF16)
    kval = din("kval", [128, 16], BF16)
    emat = din("emat", [32, 2048], BF16)
    identb = din("identb", [128, 128], BF16)
    identf = din("identf", [128, 128])
    out = nc.dram_tensor("out", [T, D], F32, kind="ExternalOutput").ap()

    p_glu = dscr("p_glu", [16, 128, 1152])
    p_q = dscr("p_q", [16, 128, T], BF16)
    p_kv = dscr("p_kv", [24, 128, 2048], BF16)
    p_ga = dscr("p_ga", [32, 128, T], BF16)
    p_gb = dscr("p_gb", [32, 128, T], BF16)
    p_attn = dscr("p_attn", [16, 128, T], BF16)
    p_conv = dscr("p_conv", [16, 128, T], BF16)
    p_z = dscr("p_z", [T, D])
    p_x1 = dscr("p_x1", [T, D])
    p_acc = dscr("p_acc", [T, D])
    p_y = dscr("p_y", [16, 128, T])

    cm = nc.cleanup_on_exit()
    cm.__enter__()
    _orig_sbuf_tensor = nc.sbuf_tensor
    _uid = [0]

    def _sbuf_unique(name, shape, dt):
        _uid[0] += 1
        return _orig_sbuf_tensor(f"{name}_{_uid[0]}", shape, dt)
    K = KB(nc)
    pe, act, dve, pool, sp = K.pe, K.act, K.dve, K.pool, K.sp

    banks = [nc.alloc_psum_tensor(f"bank{i}", [128, 512], F32) for i in range(8)]
    bankB = [Buf(f"bank{i}") for i in range(8)]
    wslots = [nc.alloc_sbuf_tensor(f"wslot{i}", [128, D], BF16) for i in range(4)]
    wslotB = [Buf(f"wslot{i}") for i in range(4)]
    wctr = [0]
    idb = nc.alloc_sbuf_tensor("idb", [128, 128], BF16)
    idf = nc.alloc_sbuf_tensor("idf", [128, 128], F32)
    gatesT = nc.alloc_sbuf_tensor("gatesT", [128, T], F32)
    gates = nc.alloc_sbuf_tensor("gates", [128, NT * 48], F32)
    constB = Buf("const")
    gatesTB = Buf("gatesT")
    gatesB = Buf("gates")
    K.dma(sp, idb[:], identb[:, :], w=[constB])
    K.dma(sp, idf[:], identf[:, :], w=[constB])

    def load_w(src):
        i = wctr[0] % 4
        wctr[0] += 1
        K.dma(pool, wslots[i][:], src, w=[wslotB[i]], max_dma_last_dim=8192)
        return wslots[i], wslotB[i]

    def norm_T(src, gB, gBb, dstT, dstB, ntiles, xt, xtB, ubs, ubBs, st, stB, pre=None):
        for i in range(ntiles):
            x_t, x_b = xt[i % 2], xtB[i % 2]
            K.dma(sp, x_t[:], src[i * 128:(i + 1) * 128, :], w=[x_b])
            if pre is not None:
                pre(i, x_t, x_b)
            s_t = st[:, (i % 2) * 4:(i % 2) * 4 + 4]
            s_b = stB[i % 2]
            ub, ubB = ubs[i % 2], ubBs[i % 2]
            K.op(act, lambda: nc.scalar.activation(out=ub[:], in_=x_t[:], func=AF.Square,
                                                   accum_out=s_t[:, 0:1]), r=[x_b], w=[ubB, s_b])
            K.op(act, lambda: nc.scalar.activation(out=s_t[:, 1:2], in_=s_t[:, 0:1], func=AF.Sqrt,
                                                   scale=1.0 / D, bias=epsn[:, 0:1]), r=[constB], w=[s_b])
            K.op(dve, lambda: nc.vector.reciprocal(out=s_t[:, 2:3], in_=s_t[:, 1:2]), w=[s_b])
            K.op(dve, lambda: nc.vector.scalar_tensor_tensor(out=ub[:], in0=x_t[:], scalar=s_t[:, 2:3],
                                                             op0=ALU.mult, in1=gB, op1=ALU.mult),
                 r=[x_b, s_b, gBb], w=[ubB])
            for g in range(4):
                bk = 4 + g
                pst = banks[bk][:].bitcast(BF16)
                fns = []
                for j in range(8):
                    kc = g * 8 + j
                    fns.append(lambda kc=kc, j=j, pst=pst: nc.tensor.transpose(
                        out=pst[:, j * 128:(j + 1) * 128], in_=ub[:, kc * 128:(kc + 1) * 128], identity=idb[:]))
                K.group(pe, fns, r=[ubB, constB], w=[bankB[bk]])
                dst = dstT[:, g * 8:(g + 1) * 8, i * 128:(i + 1) * 128]
                srcv = pst.rearrange("p (a b) -> p a b", a=8)
                if g % 2 == 0:
                    K.op(dve, lambda: nc.vector.tensor_copy(out=dst, in_=srcv), r=[bankB[bk]], w=[dstB])
                else:
                    K.op(act, lambda: nc.scalar.copy(out=dst, in_=srcv), r=[bankB[bk]], w=[dstB])

    def proj_fm(wsrcs, rhsT, rhsB, nkc, tok_groups, evac, bank0=0, after_chunk=None):
        cnt = 0
        for ci, src in enumerate(wsrcs):
            if nkc == 32:
                wt, wb = load_w(src)
            else:
                i = wctr[0] % 4
                wctr[0] += 1
                K.dma(pool, wslots[i][:, 0:nkc * 128], src, w=[wslotB[i]], max_dma_last_dim=8192)
                wt, wb = wslots[i], wslotB[i]
            for gi, grp in enumerate(tok_groups):
                lo, n = grp[0], grp[1]
                rT, rB = (grp[2], grp[3]) if len(grp) == 4 else (rhsT, rhsB)
                bk = bank0 + (cnt % 4)
                cnt += 1
                fns = []
                for kc in range(nkc):
                    fns.append(lambda kc=kc, bk=bk, wt=wt, lo=lo, n=n, rT=rT: nc.tensor.matmul(
                        banks[bk][:, 0:n], wt[:, kc * 128:(kc + 1) * 128], rT[:, kc, lo:lo + n],
                        start=(kc == 0), stop=(kc == nkc - 1)))
                K.group(pe, fns, r=[wb, rB], w=[bankB[bk]])
                evac(ci, gi, banks[bk], bankB[bk], lo, n)
            if after_chunk is not None:
                after_chunk()

    epsn = nc.alloc_sbuf_tensor("epsn", [128, 2], F32)
    K.op(dve, lambda: nc.vector.memset(epsn[:, 0:1], NORM_EPS), w=[constB])
    K.op(dve, lambda: nc.vector.memset(epsn[:, 1:2], LN_EPS), w=[constB])

    onesf = nc.alloc_sbuf_tensor("onesf", [128, 128], F32)
    cps = nc.alloc_sbuf_tensor("cps", [128, 16, 34], F32)
    ccB = Buf("convconst")
    K.op(dve, lambda: nc.vector.memset(onesf[:], 1.0), w=[ccB])
    K.dma(sp, cps[:], convp[:, :, :], w=[ccB])
    stg = None
    with _sbuf_unique("uT", [128, KC, T], BF16) as uT, \
            _sbuf_unique("xt0", [128, D], F32) as xt0, _sbuf_unique("xt1", [128, D], F32) as xt1, \
            _sbuf_unique("gB", [128, D], F32) as gB, _sbuf_unique("ub", [128, D], BF16) as ub, \
            _sbuf_unique("ubx", [128, D], BF16) as ubx, \
            _sbuf_unique("st", [128, 8], F32) as st, \
            _sbuf_unique("sgb", [128, 4, 512], BF16) as sgb, \
            _sbuf_unique("sgf", [128, 6, 512], F32) as sgf, \
            _sbuf_unique("utail", [128, KC, 128], BF16) as utail:
        uTB, gBb, ubB = Buf("uT"), Buf("gB"), Buf("ub")
        ubs, ubBs = [ub, ubx], [ubB, Buf("ubx")]
        xtB = [Buf("xt0"), Buf("xt1")]
        stB = [Buf("st0"), Buf("st1")]
        sgbB = [Buf(f"sgb{i}") for i in range(4)]
        sgfB = [Buf(f"sgf{i}") for i in range(6)]
        utailB = Buf("utail")
        sctr = [0, 0]
        act_only = [False]
        gluB = [Buf(f"glu{c}") for c in range(16)]
        K.dma(sp, gB[:], gains[0:1, :].to_broadcast([128, D]), w=[gBb])

        def evac_store(dst_of, func):
            def evac(ci, gi, bank, bb, lo, n):
                i = sctr[0] % 4
                sctr[0] += 1
                if func is None and (sctr[0] % 2 == 0) and not act_only[0]:
                    K.op(dve, lambda: nc.vector.tensor_copy(out=sgb[:, i, 0:n], in_=bank[:, 0:n]), r=[bb], w=[sgbB[i]])
                else:
                    f = func if func is not None else AF.Identity
                    K.op(act, lambda: nc.scalar.activation(out=sgb[:, i, 0:n], in_=bank[:, 0:n], func=f),
                         r=[bb], w=[sgbB[i]])
                K.dma(sp, dst_of(ci, lo, n), sgb[:, i, 0:n], r=[sgbB[i]])
            return evac

        def evac_glu(tok_off_of):
            hold = {}

            def evac(ci, gi, bank, bb, lo, n):
                tok_off = tok_off_of(gi)
                i = sctr[1] % 6
                sctr[1] += 1
                if ci % 2 == 0:
                    K.op(act, lambda: nc.scalar.activation(out=sgf[:, i, 0:n], in_=bank[:, 0:n], func=AF.Identity),
                         r=[bb], w=[sgfB[i]])
                    hold[gi] = i
                else:
                    ia = hold[gi]
                    K.op(act, lambda: nc.scalar.activation(out=sgf[:, i, 0:n], in_=bank[:, 0:n], func=AF.Sigmoid),
                         r=[bb], w=[sgfB[i]])
                    K.op(dve, lambda: nc.vector.tensor_tensor(out=sgf[:, i, 0:n], in0=sgf[:, i, 0:n], in1=sgf[:, ia, 0:n],
                                                              op=ALU.mult), r=[sgfB[ia]], w=[sgfB[i]])
                    K.dma(sp, p_glu[ci // 2, :, tok_off + lo:tok_off + lo + n], sgf[:, i, 0:n], r=[sgfB[i]], w=[gluB[ci // 2]])
            return evac

        def evac_gates(ci, gi, bank, bb, lo, n):
            K.op(act, lambda: nc.scalar.activation(out=gatesT[:, lo:lo + n], in_=bank[:, 0:n], func=AF.Sigmoid),
                 r=[bb], w=[gatesTB])

        tg2 = [(0, 512), (512, 512)]
        xts = [xt0, xt1]
        norm_T(xc, gB[:], gBb, uT, uTB, NT, xts, xtB, ubs, ubBs, st, stB)
        proj_fm([w_in_t[CH_KV + j] for j in range(16)], uT, uTB, KC, tg2,
                evac_store(lambda ci, lo, n: p_kv[ci, :, lo:lo + n], None))
        proj_fm([w_in_t[CH_KV + 16 + j] for j in range(8)], uT, uTB, KC, [(512, 512)],
                evac_store(lambda ci, lo, n: p_kv[16 + ci, :, lo:lo + n], None))
        K.op(dve, lambda: nc.vector.tensor_copy(out=utail[:], in_=uT[:, :, 896:1024]), r=[uTB], w=[utailB])
        norm_T(xo, gB[:], gBb, uT, uTB, NT, xts, xtB, ubs, ubBs, st, stB)
        proj_fm([w_in_t[CH_GLU + j] for j in range(32)], uT, uTB, KC, tg2 + [(0, 128, utail, utailB)],
                evac_glu(lambda gi: 128 if gi < 2 else 0))
        def inherit(olds):
            b = Buf()
            for o in olds:
                for t in list(o.rs.values()) + ([o.w] if o.w is not None else []):
                    k = id(t[0])
                    if k not in b.rs or b.rs[k][1] < t[1]:
                        b.rs[k] = t
            return b


        def conv_gen():
            ubuf = xt0[:, 0:2304].rearrange("p (a b) -> p a b", a=2)
            ybuf = xt1[:, 0:3072].rearrange("p (a b) -> p a b", a=3)
            sqb = sgf[:, 0:4, :].rearrange("p a b -> p (a b)").rearrange("p (a b) -> p a b", a=2)
            mean = gB[:, 0:T]
            rstd = gB[:, T:2 * T]
            cstg = ub[:, 0:2 * T].rearrange("p (a b) -> p a b", a=2)
            ubB_ = [inherit([xtB[0]]), inherit([xtB[0]])]
            yB_ = [inherit([xtB[1]]) for _ in range(3)]
            sqB_ = [inherit(sgfB), inherit(sgfB)]
            mB_ = inherit([gBb])
            csB_ = [inherit([ubBs[0]]), inherit([ubBs[0]])]
            pyB = [Buf() for _ in range(16)]

            def load_u(c):
                K.dma(sp, ubuf[:, c % 2, :], p_glu[c], r=[gluB[c]], w=[ubB_[c % 2]])

            load_u(0)
            for u in range(18):
                if u + 1 < 16:
                    load_u(u + 1)
                if u < 16:
                    c = u
                    u_, y_ = ubuf[:, c % 2, :], ybuf[:, c % 3, :]
                    K.op(dve, lambda: nc.vector.tensor_scalar(out=y_, in0=u_[:, 98:98 + T], scalar1=cps[:, c, 0:1],
                                                              scalar2=cps[:, c, 31:32], op0=ALU.mult, op1=ALU.add),
                         r=[ubB_[c % 2], ccB], w=[yB_[c % 3]])
                    for j in range(1, 31):
                        K.op(dve, lambda: nc.vector.scalar_tensor_tensor(out=y_, in0=u_[:, 98 + j:98 + j + T],
                                                                         scalar=cps[:, c, j:j + 1], in1=y_, op0=ALU.mult,
                                                                         op1=ALU.add), r=[ubB_[c % 2]], w=[yB_[c % 3]])
                if 0 <= u - 1 < 16:
                    c = u - 1
                    K.op(act, lambda: nc.scalar.activation(out=sqb[:, c % 2, :], in_=ybuf[:, c % 3, :], func=AF.Square),
                         r=[yB_[c % 3]], w=[sqB_[c % 2]])
                if 0 <= u - 2 < 16:
                    c = u - 2
                    for tg in range(2):
                        sl = slice(tg * 512, (tg + 1) * 512)
                        K.op(pe, lambda: nc.tensor.matmul(banks[4 + tg][:, 0:512], onesf[:], ybuf[:, c % 3, sl],
                                                          start=(c == 0), stop=(c == 15)), r=[yB_[c % 3], ccB], w=[bankB[4 + tg]])
                        K.op(pe, lambda: nc.tensor.matmul(banks[6 + tg][:, 0:512], onesf[:], sqb[:, c % 2, sl],
                                                          start=(c == 0), stop=(c == 15)), r=[sqB_[c % 2], ccB], w=[bankB[6 + tg]])
                    K.dma(sp, p_y[c], ybuf[:, c % 3, :], r=[yB_[c % 3]], w=[pyB[c]])
                yield
            for tg in range(2):
                sl = slice(tg * 512, (tg + 1) * 512)
                K.op(act, lambda: nc.scalar.activation(out=mean[:, sl], in_=banks[4 + tg][:, 0:512], func=AF.Identity,
                                                       scale=1.0 / 2048), r=[bankB[4 + tg]], w=[mB_])
                K.op(dve, lambda: nc.vector.tensor_tensor(out=rstd[:, sl], in0=mean[:, sl], in1=mean[:, sl], op=ALU.mult), w=[mB_])
                K.op(dve, lambda: nc.vector.scalar_tensor_tensor(out=rstd[:, sl], in0=banks[6 + tg][:, 0:512], scalar=1.0 / 2048,
                                                                 in1=rstd[:, sl], op0=ALU.mult, op1=ALU.subtract),
                     r=[bankB[6 + tg]], w=[mB_])
            yield
            for tg in range(2):
                sl = slice(tg * 512, (tg + 1) * 512)
                K.op(act, lambda: nc.scalar.activation(out=rstd[:, sl], in_=rstd[:, sl], func=AF.Sqrt, bias=epsn[:, 1:2]),
                     r=[constB], w=[mB_])
            yield
            for tg in range(2):
                sl = slice(tg * 512, (tg + 1) * 512)
                K.op(dve, lambda: nc.vector.reciprocal(out=rstd[:, sl], in_=rstd[:, sl]), w=[mB_])

            def load_y(c):
                K.dma(sp, ybuf[:, c % 3, :], p_y[c], r=[pyB[c]], w=[yB_[c % 3]])

            load_y(0)
            yield
            for v in range(18):
                if v + 1 < 16:
                    load_y(v + 1)
                if v < 16:
                    c = v
                    y_ = ybuf[:, c % 3, :]
                    K.op(dve, lambda: nc.vector.tensor_tensor(out=y_, in0=y_, in1=mean, op=ALU.subtract), r=[mB_], w=[yB_[c % 3]])
                    K.op(dve, lambda: nc.vector.tensor_tensor(out=y_, in0=y_, in1=rstd, op=ALU.mult), r=[mB_], w=[yB_[c % 3]])
                if 0 <= v - 1 < 16:
                    c = v - 1
                    K.op(act, lambda: nc.scalar.activation(out=cstg[:, c % 2, :], in_=ybuf[:, c % 3, :], func=AF.Silu,
                                                           scale=cps[:, c, 32:33], bias=cps[:, c, 33:34]),
                         r=[yB_[c % 3], ccB], w=[csB_[c % 2]])
                if 0 <= v - 2 < 16:
                    c = v - 2
                    K.dma(sp, p_conv[c], cstg[:, c % 2, :], r=[csB_[c % 2]])
                yield

        cgen = conv_gen()
        cstate = [0, 0]

        def conv_tick():
            cstate[0] += 1
            period = 3 if cstate[1] < 18 else 2
            if cstate[0] % period == 0:
                cstate[1] += 1
                next(cgen, None)

        act_only[0] = True
        proj_fm([w_in_t[CH_Q + j] for j in range(16)], uT, uTB, KC, tg2,
                evac_store(lambda ci, lo, n: p_q[ci, :, lo:lo + n], None), after_chunk=conv_tick)
        proj_fm([w_in_t[CH_KV + j] for j in range(24)], uT, uTB, KC, tg2,
                evac_store(lambda ci, lo, n: p_kv[ci, :, 1024 + lo:1024 + lo + n], None), after_chunk=conv_tick)
        proj_fm([w_in_t[CH_GA + j] for j in range(32)], uT, uTB, KC, tg2,
                evac_store(lambda ci, lo, n: p_ga[ci, :, lo:lo + n], AF.Sigmoid), after_chunk=conv_tick)
        proj_fm([w_in_t[CH_GB + j] for j in range(32)], uT, uTB, KC, tg2,
                evac_store(lambda ci, lo, n: p_gb[ci, :, lo:lo + n], AF.Sigmoid), after_chunk=conv_tick)
        proj_fm([w_in_t[CH_NG]], uT, uTB, KC, tg2, evac_gates)
        for _ in cgen:
            pass
        K.barrier()

    def phase_B():
        with ExitStack() as es:
            w1k = es.enter_context(_sbuf_unique("w1k", [128, 8192], BF16))
            w1v = es.enter_context(_sbuf_unique("w1v", [128, 8192], BF16))
            w2k = es.enter_context(_sbuf_unique("w2k", [128, 256], BF16))
            w2v = es.enter_context(_sbuf_unique("w2v", [128, 256], BF16))
            posb = es.enter_context(_sbuf_unique("posb", [128, 64], BF16))
            m8s = es.enter_context(_sbuf_unique("m8s", [128, 8, 512], BF16))
            cvs = es.enter_context(_sbuf_unique("cvs", [128, T], BF16))
            selvs = es.enter_context(_sbuf_unique("selvs", [128, NT * 32], F32))
            selbs = es.enter_context(_sbuf_unique("selbs", [128, NT * 32], F32))
            emats = es.enter_context(_sbuf_unique("emats", [32, 2048], BF16))
            kvals = es.enter_context(_sbuf_unique("kvals", [128, 16], BF16))
            cb = es.enter_context(_sbuf_unique("cb", [128, 4], F32))
            qT = es.enter_context(_sbuf_unique("qT", [128, 4, T], BF16))
            kvT = es.enter_context(_sbuf_unique("kvT", [128, 6, 2048], BF16))
            Rs = es.enter_context(_sbuf_unique("Rs", [128, 16, 129], BF16))
            Rw = es.enter_context(_sbuf_unique("Rw", [128, 16, 129], BF16))
            Rc = es.enter_context(_sbuf_unique("Rc", [128, 161], BF16))
            hid = es.enter_context(_sbuf_unique("hid", [128, 4, 128], BF16))
            gx = es.enter_context(_sbuf_unique("gx", [128, 3, 128], F32))
            kcmpT = es.enter_context(_sbuf_unique("kcmpT", [128, 128], BF16))
            msk = es.enter_context(_sbuf_unique("msk", [128, 16, 512], BF16))
            pvs = es.enter_context(_sbuf_unique("pvs", [128, 8, 161], F32))
            selm8 = es.enter_context(_sbuf_unique("selm8", [128, 8, 32], BF16))
            selmB = Buf("selm8")
            ptmp = es.enter_context(_sbuf_unique("ptmp", [128, 4, 160], F32))
            ptmpB = Buf("ptmp")
            m8c = es.enter_context(_sbuf_unique("m8c", [128, 4, 512], BF16))
            m8cB = Buf("m8c")
            emt = es.enter_context(_sbuf_unique("emt", [128, 8, 512], BF16))
            oacc = es.enter_context(_sbuf_unique("oacc", [128, NT, 512], F32))
            obf = es.enter_context(_sbuf_unique("obf", [128, NT, 512], BF16))
            imp = es.enter_context(_sbuf_unique("imp", [128, NT, 32], F32))
            selT = es.enter_context(_sbuf_unique("selT", [32, T], BF16))
            tk = es.enter_context(_sbuf_unique("tk", [128, 2, 32], F32))
            mx = es.enter_context(_sbuf_unique("mx", [128, 16], F32))
            selm = es.enter_context(_sbuf_unique("selm", [128, 32], BF16))
            sm = es.enter_context(_sbuf_unique("sm", [128, 16], F32))
            ostg = es.enter_context(_sbuf_unique("ostg", [128, 2, T], BF16))
            qTB, kvB = Buf("qT"), [Buf(f"kv{i}") for i in range(6)]
            kvalB, selvB, selbB, ematB, m8B, cvsB, cbB = Buf(), Buf(), Buf(), Buf(), Buf(), Buf(), Buf()
            grpW = [Buf() for _ in range(5)]
            RsB, RwB, RcB, hidB, gxB, kcmpB = Buf("Rs"), Buf("Rw"), Buf("Rc"), Buf("hid"), Buf("gx"), Buf("kcmp")
            mskB, oaccB, obfB, impB, selTB, tkB, smB = Buf(), Buf(), Buf(), Buf(), Buf(), Buf(), Buf()
            etB = [Buf() for _ in range(6)]
            emtB = [Buf() for _ in range(8)]
            pvsB = [Buf() for _ in range(2)]
            pctr, sbctr = [0], [0]
            ostgB = [Buf(), Buf()]
            ectr = [0]
            K.dma(sp, kvals[:], kval[:, :], w=[kvalB])
            K.dma(sp, Rc[:, 128:161], rcc[:, :], w=[RcB])
            K.dma(sp, selvs[:], selv[:, :], w=[selvB])
            K.dma(sp, selbs[:], selb[:, :], w=[selbB])
            K.dma(sp, emats[:], emat[:, :], w=[ematB])
            K.dma(sp, qT[:], p_q[0:4].rearrange("h p t -> p h t"), w=[qTB])
            for kd in (3, 5, 0, 1, 2, 4):
                K.dma(sp, kvT[:, kd, :], p_kv[kd * 4], w=[kvB[kd]])
            K.dma(pool, w1k[:], w1k_t[:, :], w=[grpW[0]], max_dma_last_dim=8192)
            K.dma(pool, w1v[:], w1v_t[:, :], w=[grpW[1]], max_dma_last_dim=8192)
            K.dma(pool, w2k[:], w2k_t[:, :], w=[grpW[2]])
            K.dma(pool, w2v[:], w2v_t[:, :], w=[grpW[3]])
            K.dma(pool, posb[:], posT[:, :], w=[grpW[4]])
            K.dma(sp, m8s[:].rearrange("p a b -> p (a b)"), m8[:, :], w=[m8B])
            K.dma(sp, cvs[:], cvis[:, :], w=[cvsB])
            K.op(dve, lambda: nc.vector.memset(hid[:], 0.0), w=[hidB])
            K.op(dve, lambda: nc.vector.memset(kcmpT[:], 0.0), w=[kcmpB])
            K.op(dve, lambda: nc.vector.memset(Rc[:, 0:128], 0.0), w=[RcB])
            for jt in range(16):
                K.op(dve, lambda: nc.vector.tensor_copy(out=Rs[:, jt, 128:129], in_=kvals[:, jt:jt + 1]), r=[kvalB], w=[RsB])
                K.op(dve, lambda: nc.vector.memset(Rw[:, jt, 128:129], 1.0), w=[RwB])
            for i in range(NT):
                K.op(pe, lambda: nc.tensor.transpose(out=banks[7][:, 0:128], in_=gatesT[:, i * 128:(i + 1) * 128],
                                                     identity=idf[:]), r=[gatesTB, constB], w=[bankB[7]])
                K.op(dve, lambda: nc.vector.tensor_copy(out=gates[:, i * 48:(i + 1) * 48], in_=banks[7][:, 0:48]),
                     r=[bankB[7]], w=[gatesB])
            def post_group(qg, hh, head, br, first, with_imp, ncol):
                pi = pctr[0] % 2
                pctr[0] += 1
                pv = pvs[:, pi * 4:(pi + 1) * 4, :]
                pB = pvsB[pi]
                for qt in range(4):
                    K.op(dve, lambda: nc.vector.tensor_copy(out=pv[:, qt, 0:ncol], in_=banks[2 + qt][:, 0:ncol]),
                         r=[bankB[2 + qt]], w=[pB])
                K.op(dve, lambda: nc.vector.tensor_scalar_max(out=sm[:, 0:4], in0=pv[:, :, 128], scalar1=1e-30),
                     r=[pB], w=[smB])
                K.op(dve, lambda: nc.vector.reciprocal(out=sm[:, 4:8], in_=sm[:, 0:4]), w=[smB])
                gv = gates[:].rearrange("p (t c) -> p t c", c=48)[:, qg * 4:(qg + 1) * 4, head * 3 + br]
                K.op(dve, lambda: nc.vector.tensor_tensor(out=sm[:, 8:12], in0=sm[:, 4:8], in1=gv, op=ALU.mult),
                     r=[gatesB], w=[smB])
                dst = oacc[:, qg * 4:(qg + 1) * 4, hh * 128:(hh + 1) * 128]
                wbc = sm[:, 8:12].unsqueeze(2).to_broadcast([128, 4, 128])
                if first:
                    K.op(dve, lambda: nc.vector.tensor_tensor(out=dst, in0=pv[:, :, 0:128], in1=wbc, op=ALU.mult),
                         r=[pB, smB], w=[oaccB])
                else:
                    K.op(dve, lambda: nc.vector.tensor_tensor(out=ptmp[:, :, 0:128], in0=pv[:, :, 0:128], in1=wbc, op=ALU.mult),
                         r=[pB, smB], w=[ptmpB])
                    K.op(dve, lambda: nc.vector.tensor_tensor(out=dst, in0=dst, in1=ptmp[:, :, 0:128], op=ALU.add),
                         r=[ptmpB], w=[oaccB])
                if with_imp:
                    idst = imp[:, qg * 4:(qg + 1) * 4, :]
                    rbc = sm[:, 4:8].unsqueeze(2).to_broadcast([128, 4, 32])
                    if hh == 0:
                        K.op(dve, lambda: nc.vector.tensor_tensor(out=idst, in0=pv[:, :, 129:161], in1=rbc, op=ALU.mult),
                             r=[pB, smB], w=[impB])
                    else:
                        K.op(dve, lambda: nc.vector.tensor_tensor(out=ptmp[:, :, 128:160], in0=pv[:, :, 129:161], in1=rbc,
                                                                  op=ALU.mult), r=[pB, smB], w=[ptmpB])
                        K.op(dve, lambda: nc.vector.tensor_tensor(out=idst, in0=idst, in1=ptmp[:, :, 128:160], op=ALU.add),
                             r=[ptmpB], w=[impB])

            SBK = [0, 1, 6, 7]
            NE = 8
            DEPTH = 4

            def attend(hh, head, qg, jts, kT, kB, R, RB, ncol, mask_of, br, first, with_imp, qrange_of=None):
                n = len(jts)
                rng = [qrange_of(jt) if qrange_of is not None else (0, 512) for jt in jts]
                first_ji, last_ji = {}, {}
                for ji, (lo, hi) in enumerate(rng):
                    for qt in range(lo // 128, hi // 128):
                        first_ji.setdefault(qt, ji)
                        last_ji[qt] = ji
                assert sorted(first_ji) == [0, 1, 2, 3]

                def front(ji):
                    jt = jts[ji]
                    lo, hi = rng[ji]
                    sb = SBK[sbctr[0] % 4]
                    sbctr[0] += 1
                    mk, mkB = mask_of(jt)
                    fns = [lambda: nc.tensor.matmul(banks[sb][:, lo:hi], kT(jt), G["qT"][:, hh, qg * 512 + lo:qg * 512 + hi],
                                                    start=True, stop=False),
                           lambda: nc.tensor.matmul(banks[sb][:, lo:hi], idb[:], mk[:, lo:hi], start=False, stop=True)]
                    K.group(pe, fns, r=[kB, G["qTB"], mkB, constB], w=[bankB[sb]])
                    ei = ectr[0] % NE
                    ectr[0] += 1
                    K.op(act, lambda: nc.scalar.activation(out=emt[:, ei, lo:hi], in_=banks[sb][:, lo:hi], func=AF.Exp,
                                                           scale=SCALE), r=[bankB[sb]], w=[emtB[ei]])
                    return ei

                eis = [front(ji) for ji in range(min(DEPTH, n))]
                for ji in range(n):
                    ei = eis[ji]
                    jt = jts[ji]
                    lo, hi = rng[ji]
                    qts = list(range(lo // 128, hi // 128))
                    fns = [lambda qt=qt, ei=ei, jt=jt, ji=ji: nc.tensor.matmul(
                        banks[2 + qt][:, 0:ncol], emt[:, ei, qt * 128:(qt + 1) * 128], R(jt),
                        start=(ji == first_ji[qt]), stop=(ji == last_ji[qt])) for qt in qts]
                    K.group(pe, fns, r=[emtB[ei], RB], w=[bankB[2 + qt] for qt in qts])
                    if ji + DEPTH < n:
                        eis.append(front(ji + DEPTH))
                post_group(qg, hh, head, br, first, with_imp, ncol)

            G = {}
            qTs = [qT[:], wslots[0][:].rearrange("p (h t) -> p h t", h=4)]
            qTBs = [qTB, Buf("qT1")]
            kalt = wslots[1][:].rearrange("p (a t) -> p a t", a=2)
            kaltB = [Buf(), Buf()]
            Rss = [Rs[:], wslots[2][:, 0:16 * 129].rearrange("p (j c) -> p j c", c=129)]
            Rws = [Rw[:], wslots[3][:, 0:16 * 129].rearrange("p (j c) -> p j c", c=129)]
            RsBs, RwBs = [RsB, Buf()], [RwB, Buf()]
            for jt in range(16):
                K.op(dve, lambda: nc.vector.tensor_copy(out=Rss[1][:, jt, 128:129], in_=kvals[:, jt:jt + 1]), r=[kvalB], w=[RsBs[1]])
                K.op(dve, lambda: nc.vector.memset(Rws[1][:, jt, 128:129], 1.0), w=[RwBs[1]])

            def kvsrc(par, kd):
                if par == 1 and kd in (2, 4):
                    return kalt[:, (kd - 2) // 2, :], kaltB[(kd - 2) // 2]
                return kvT[:, kd, :], kvB[kd]

            def loads(g):
                par = g % 2
                K.dma(sp, qTs[par], p_q[4 * g:4 * g + 4].rearrange("h p t -> p h t"), w=[qTBs[par]])
                for kd in range(6):
                    t_, b_ = kvsrc(par, kd)
                    K.dma(sp, t_, p_kv[kd * 4 + g], w=[b_])

            def compute_cb():
                for a, w1 in enumerate((w1k, w1v)):
                    for hc in range(2):
                        col = a * 2 + hc
                        fns = [lambda l=l, w1=w1, hc=hc, col=col, a=a: nc.tensor.matmul(
                            banks[7][:, col:col + 1], w1[:, l * 256 + hc * 128:l * 256 + hc * 128 + 128],
                            posb[:, a * 32 + l:a * 32 + l + 1], start=(l == 0), stop=(l == 31)) for l in range(32)]
                        K.group(pe, fns, r=grpW, w=[bankB[7]])
                        K.op(dve, lambda: nc.vector.tensor_copy(out=cb[:, col:col + 1], in_=banks[7][:, col:col + 1]),
                             r=[bankB[7]], w=[cbB])


            def stage1(g, between=None):
                par = g % 2
                for kd, R, RB in ((3, Rss[par], RsBs[par]), (5, Rws[par], RwBs[par])):
                    vsrc, vB = kvsrc(par, kd)
                    for jb in range(2):
                        pst = banks[6 + (jb % 2)][:].bitcast(BF16)
                        fns = [lambda j=j, jb=jb, vsrc=vsrc, pst=pst: nc.tensor.transpose(
                            out=pst[:, j * 128:(j + 1) * 128], in_=vsrc[:, (jb * 8 + j) * 128:(jb * 8 + j + 1) * 128],
                            identity=idb[:]) for j in range(8)]
                        K.group(pe, fns, r=[vB, constB], w=[bankB[6 + (jb % 2)]])
                        K.op(act, lambda: nc.scalar.copy(out=R[:, jb * 8:(jb + 1) * 8, 0:128],
                                                         in_=pst.rearrange("p (a b) -> p a b", a=8)),
                             r=[bankB[6 + (jb % 2)]], w=[RB])
                if between is not None:
                    between()
                for a, (kd, w1, w2) in enumerate(((0, w1k, w2k), (1, w1v, w2v))):
                    csrc, cBf = kvsrc(par, kd)
                    v3 = csrc.rearrange("p (n s) -> p n s", s=16)
                    for hc in range(2):
                        fns = []
                        for l in range(32):
                            rhs = v3[:, 0:127, l] if l < 16 else v3[:, 1:128, l - 16]
                            fns.append(lambda l=l, rhs=rhs, w1=w1, hc=hc: nc.tensor.matmul(
                                banks[7][:, 0:127], w1[:, l * 256 + hc * 128:l * 256 + hc * 128 + 128], rhs,
                                start=(l == 0), stop=(l == 31)))
                        K.group(pe, fns, r=[cBf] + grpW, w=[bankB[7]])
                        x_ = gx[:, 0, 0:127]
                        t_ = gx[:, 1, 0:127]
                        K.op(act, lambda: nc.scalar.activation(out=x_, in_=banks[7][:, 0:127], func=AF.Identity,
                                                               bias=cb[:, a * 2 + hc:a * 2 + hc + 1]), r=[bankB[7], cbB], w=[gxB])
                        K.op(dve, lambda: nc.vector.tensor_tensor(out=t_, in0=x_, in1=x_, op=ALU.mult), w=[gxB])
                        K.op(dve, lambda: nc.vector.tensor_scalar(out=t_, in0=t_, scalar1=0.044715, scalar2=1.0,
                                                                  op0=ALU.mult, op1=ALU.add), w=[gxB])
                        K.op(dve, lambda: nc.vector.tensor_tensor(out=t_, in0=t_, in1=x_, op=ALU.mult), w=[gxB])
                        K.op(act, lambda: nc.scalar.activation(out=t_, in_=t_, func=AF.Tanh, scale=0.7978845608028654), w=[gxB])
                        K.op(dve, lambda: nc.vector.tensor_scalar(out=t_, in0=t_, scalar1=1.0, scalar2=0.5,
                                                                  op0=ALU.add, op1=ALU.mult), w=[gxB])
                        K.op(dve, lambda: nc.vector.tensor_tensor(out=hid[:, a * 2 + hc, 0:127], in0=t_, in1=x_, op=ALU.mult),
                             r=[gxB], w=[hidB])
                    if a == 0:
                        fns = [lambda hc=hc: nc.tensor.matmul(banks[7][:, 128:255], w2k[:, hc * 128:(hc + 1) * 128],
                                                              hid[:, hc, 0:127], start=(hc == 0), stop=(hc == 1)) for hc in range(2)]
                        K.group(pe, fns, r=[hidB] + grpW, w=[bankB[7]])
                        K.op(act, lambda: nc.scalar.copy(out=kcmpT[:, 0:127], in_=banks[7][:, 128:255]), r=[bankB[7]], w=[kcmpB])
                    else:
                        fns = [lambda hc=hc: nc.tensor.matmul(banks[7][0:127, 256:384], hid[:, 2 + hc, 0:127],
                                                              w2v[:, hc * 128:(hc + 1) * 128], start=(hc == 0), stop=(hc == 1))
                               for hc in range(2)]
                        K.group(pe, fns, r=[hidB] + grpW, w=[bankB[7]])
                        K.op(act, lambda: nc.scalar.copy(out=Rc[0:127, 0:128], in_=banks[7][0:127, 256:384]), r=[bankB[7]], w=[RcB])

            stage1(0, between=compute_cb)
            K.op(dve, lambda: nc.vector.tensor_scalar(out=m8c[:], in0=m8s[:, 4:8, :], scalar1=-MASK_BIG, scalar2=None,
                                                      op0=ALU.add), r=[m8B], w=[m8cB])
            for g in range(4):
                par = g % 2
                G["qT"], G["qTB"] = qTs[par], qTBs[par]
                if g + 1 < 4:
                    loads(g + 1)
                for hh in range(4):
                    for qg in range(2):
                        attend(hh, 4 * g + hh, qg, [0], lambda jt: kcmpT[:, :], kcmpB, lambda jt: Rc[:, :], RcB, 161,
                               lambda jt: (cvs[:, qg * 512:(qg + 1) * 512], cvsB), 0, True, True)
                for i in range(NT):
                    sc = tk[:, 0, :]
                    sc2 = tk[:, 1, :]
                    K.op(dve, lambda: nc.vector.tensor_tensor(out=sc, in0=imp[:, i, :], in1=selvs[:, i * 32:(i + 1) * 32],
                                                              op=ALU.mult), r=[impB, selvB, selbB], w=[tkB])
                    K.op(dve, lambda: nc.vector.tensor_tensor(out=sc, in0=sc, in1=selbs[:, i * 32:(i + 1) * 32], op=ALU.add), w=[tkB])
                    K.op(dve, lambda: nc.vector.max(out=mx[:, 0:8], in_=sc), w=[tkB])
                    K.op(dve, lambda: nc.vector.match_replace(out=sc2, in_to_replace=mx[:, 0:8], in_values=sc, imm_value=-3.0e4), w=[tkB])
                    K.op(dve, lambda: nc.vector.max(out=mx[:, 8:16], in_=sc2), w=[tkB])
                    K.op(dve, lambda: nc.vector.tensor_scalar(out=selm8[:, i, :], in0=sc, scalar1=mx[:, 15:16], scalar2=None,
                                                              op0=ALU.is_ge), r=[tkB], w=[selmB])
                if g + 1 < 4:
                    stage1(g + 1)
                ksT, ksB = kvsrc(par, 2)
                kwT, kwB = kvsrc(par, 4)
                Rs_, RsB_, Rw_, RwB_ = Rss[par], RsBs[par], Rws[par], RwBs[par]
                for qg in range(2):
                    jt0 = 4 + 4 * qg
                    for hh in range(4):
                        attend(hh, 4 * g + hh, qg, list(range(jt0, jt0 + 8)), lambda jt: kwT[:, jt * 128:(jt + 1) * 128], kwB,
                               lambda jt: Rw_[:, jt, :], RwB_, 129, lambda jt: (m8s[:, jt - jt0, :], m8B), 2, False, False,
                               qrange_of=lambda jt: ((0, (jt - jt0 + 1) * 128) if jt - jt0 <= 3 else ((jt - jt0 - 4) * 128, 512)))
                for i in range(NT):
                    pst = banks[6][:].bitcast(BF16)
                    K.op(pe, lambda: nc.tensor.transpose(out=pst[0:32, 0:128], in_=selm8[:, i, :], identity=idb[:]),
                         r=[selmB, constB], w=[bankB[6]])
                    K.op(act, lambda: nc.scalar.copy(out=selT[:, i * 128:(i + 1) * 128], in_=pst[0:32, 0:128]), r=[bankB[6]], w=[selTB])
                for qg in range(2):
                    njt = 12 + 4 * qg
                    for jt in range(njt):
                        bk = 6 + (jt % 2)
                        K.op(pe, lambda: nc.tensor.matmul(banks[bk][:, 0:512], emats[:, jt * 128:(jt + 1) * 128],
                                                          selT[:, qg * 512:(qg + 1) * 512], start=True, stop=True),
                             r=[selTB, ematB], w=[bankB[bk]])
                        dl = jt * 128 - (1024 + qg * 512)
                        if dl >= 0:
                            K.op(dve, lambda: nc.vector.scalar_tensor_tensor(out=msk[:, jt, :], in0=banks[bk][:, 0:512],
                                                                             scalar=MASK_BIG, in1=m8c[:, dl // 128, :],
                                                                             op0=ALU.mult, op1=ALU.add),
                                 r=[bankB[bk], m8cB], w=[mskB])
                        else:
                            K.op(dve, lambda: nc.vector.tensor_scalar(out=msk[:, jt, :], in0=banks[bk][:, 0:512], scalar1=-1.0,
                                                                      scalar2=MASK_BIG, op0=ALU.add, op1=ALU.mult),
                                 r=[bankB[bk]], w=[mskB])
                    for hh in range(4):
                        attend(hh, 4 * g + hh, qg, list(range(njt)), lambda jt: ksT[:, jt * 128:(jt + 1) * 128], ksB,
                               lambda jt: Rs_[:, jt, :], RsB_, 129, lambda jt: (msk[:, jt, :], mskB), 1, False, False,
                               qrange_of=lambda jt: (max(0, jt * 128 - (1024 + qg * 512)), 512))
                K.op(dve, lambda: nc.vector.tensor_copy(out=obf[:, 0:4, :], in_=oacc[:, 0:4, :]), r=[oaccB], w=[obfB])
                K.op(act, lambda: nc.scalar.copy(out=obf[:, 4:8, :], in_=oacc[:, 4:8, :]), r=[oaccB], w=[obfB])
                for hh in range(4):
                    bk = 6 + (hh % 2)
                    pst = banks[bk][:].bitcast(BF16)
                    fns = [lambda i=i, hh=hh, pst=pst: nc.tensor.transpose(
                        out=pst[:, i * 128:(i + 1) * 128], in_=obf[:, i, hh * 128:(hh + 1) * 128], identity=idb[:]) for i in range(NT)]
                    K.group(pe, fns, r=[obfB, constB], w=[bankB[bk]])
                    K.op(act, lambda: nc.scalar.copy(out=ostg[:, hh % 2, :], in_=pst), r=[bankB[bk]], w=[ostgB[hh % 2]])
                    K.dma(sp, p_attn[4 * g + hh], ostg[:, hh % 2, :], r=[ostgB[hh % 2]])
            K.barrier()

    if lvl >= order.index("B"):
        phase_B()

    def phase_C():
        with ExitStack() as es:
            sbt = lambda n, s, d: es.enter_context(_sbuf_unique(n, s, d))
            yall = sbt("yall", [128, 16, T], F32)
            ubuf = sbt("ubuf", [128, 2, 1152], F32)
            sq = sbt("sq", [128, 2, T], F32)
            mean = sbt("mean", [128, T], F32)
            rstd = sbt("rstd", [128, T], F32)
            onesf = sbt("onesf", [128, 128], F32)
            cps = sbt("cps", [128, 16, 34], F32)
            cstg = sbt("cstg", [128, 2, T], BF16)
            yB = [Buf() for _ in range(16)]
            ubB, sqB, cstB = [Buf(), Buf()], [Buf(), Buf()], [Buf(), Buf()]
            cB, mB = Buf(), Buf()
            K.op(dve, lambda: nc.vector.memset(onesf[:], 1.0), w=[cB])
            K.dma(sp, cps[:], convp[:, :, :], w=[cB])
            for c in range(16):
                u_, uB_ = ubuf[:, c % 2, :], ubB[c % 2]
                K.dma(sp, u_, p_glu[c], w=[uB_])
                y_ = yall[:, c, :]
                K.op(dve, lambda: nc.vector.tensor_scalar(out=y_, in0=u_[:, 98:98 + T], scalar1=cps[:, c, 0:1],
                                                          scalar2=cps[:, c, 31:32], op0=ALU.mult, op1=ALU.add),
                     r=[uB_, cB], w=[yB[c]])
                for j in range(1, 31):
                    K.op(dve, lambda: nc.vector.scalar_tensor_tensor(out=y_, in0=u_[:, 98 + j:98 + j + T], scalar=cps[:, c, j:j + 1],
                                                                     in1=y_, op0=ALU.mult, op1=ALU.add), r=[uB_], w=[yB[c]])
                s_, sB_ = sq[:, c % 2, :], sqB[c % 2]
                K.op(act, lambda: nc.scalar.activation(out=s_, in_=y_, func=AF.Square), r=[yB[c]], w=[sB_])
                for tg in range(2):
                    K.op(pe, lambda: nc.tensor.matmul(banks[tg][:, 0:512], onesf[:], y_[:, tg * 512:(tg + 1) * 512],
                                                      start=(c == 0), stop=(c == 15)), r=[yB[c], cB], w=[bankB[tg]])
                    K.op(pe, lambda: nc.tensor.matmul(banks[2 + tg][:, 0:512], onesf[:], s_[:, tg * 512:(tg + 1) * 512],
                                                      start=(c == 0), stop=(c == 15)), r=[sB_, cB], w=[bankB[2 + tg]])
            for tg in range(2):
                sl = slice(tg * 512, (tg + 1) * 512)
                K.op(act, lambda: nc.scalar.activation(out=mean[:, sl], in_=banks[tg][:, 0:512], func=AF.Identity, scale=1.0 / 2048),
                     r=[bankB[tg]], w=[mB])
                K.op(dve, lambda: nc.vector.tensor_tensor(out=rstd[:, sl], in0=mean[:, sl], in1=mean[:, sl], op=ALU.mult), w=[mB])
                K.op(dve, lambda: nc.vector.scalar_tensor_tensor(out=rstd[:, sl], in0=banks[2 + tg][:, 0:512], scalar=1.0 / 2048,
                                                                 in1=rstd[:, sl], op0=ALU.mult, op1=ALU.subtract),
                     r=[bankB[2 + tg]], w=[mB])
                K.op(act, lambda: nc.scalar.activation(out=rstd[:, sl], in_=rstd[:, sl], func=AF.Sqrt, bias=epsn[:, 1:2]),
                     r=[constB], w=[mB])
                K.op(dve, lambda: nc.vector.reciprocal(out=rstd[:, sl], in_=rstd[:, sl]), w=[mB])
            for c in range(16):
                y_ = yall[:, c, :]
                if c % 2 == 0:
                    K.op(dve, lambda: nc.vector.tensor_tensor(out=y_, in0=y_, in1=mean[:], op=ALU.subtract), r=[mB], w=[yB[c]])
                else:
                    K.op(pool, lambda: nc.gpsimd.tensor_tensor(out=y_, in0=y_, in1=mean[:], op=ALU.subtract), r=[mB], w=[yB[c]])
                K.op(dve, lambda: nc.vector.tensor_tensor(out=y_, in0=y_, in1=rstd[:], op=ALU.mult), r=[mB], w=[yB[c]])
                K.op(act, lambda: nc.scalar.activation(out=cstg[:, c % 2, :], in_=y_, func=AF.Silu, scale=cps[:, c, 32:33],
                                                       bias=cps[:, c, 33:34]), r=[yB[c], cB], w=[cstB[c % 2]])
                K.dma(sp, p_conv[c], cstg[:, c % 2, :], r=[cstB[c % 2]])
            K.barrier()

    ssq = nc.alloc_sbuf_tensor("ssq", [128, NT, 8], F32)
    ssqB = Buf("ssq")

    def tm_proj(wsrc_of, lhsT, lhsB, nkb, evac, pre_cg=None):
        for cg in range(8):
            if pre_cg is not None:
                pre_cg(cg)
            for kb in range(nkb):
                wt, wb = load_w(wsrc_of(cg, kb))
                for tt in range(NT):
                    fns = [lambda k8=k8, tt=tt, wt=wt, kb=kb: nc.tensor.matmul(
                        banks[tt][:, 0:512], lhsT[:, kb * 8 + k8, tt * 128:(tt + 1) * 128], wt[:, k8 * 512:(k8 + 1) * 512],
                        start=(kb == 0 and k8 == 0), stop=(kb == nkb - 1 and k8 == 7)) for k8 in range(8)]
                    K.group(pe, fns, r=[wb, lhsB], w=[bankB[tt]])
            for tt in range(NT):
                evac(cg, tt, banks[tt], bankB[tt])

    def phase_DE():
        with ExitStack() as es:
            sbt = lambda n, s, d: es.enter_context(_sbuf_unique(n, s, d))
            mergedT = sbt("mergedT", [128, KC, T], BF16)
            mgB = Buf("merged")
            with ExitStack() as es2:
                sb2 = lambda n, s, d: es2.enter_context(_sbuf_unique(n, s, d))
                convT = sb2("convT", [128, 16, T], BF16)
                attnT = sb2("attnT", [128, 16, T], BF16)
                gat = sb2("gat", [128, 2, 2, T], BF16)
                tmp = sb2("tmpD", [128, 4, 512], F32)
                cvBs, atBs = [Buf() for _ in range(4)], [Buf() for _ in range(4)]
                gaB = [[Buf(), Buf()], [Buf(), Buf()]]
                tmpB = [Buf() for _ in range(4)]
                for c4 in range(4):
                    K.dma(sp, convT[:, c4 * 4:(c4 + 1) * 4, :], p_conv[c4 * 4:(c4 + 1) * 4].rearrange("c p t -> p c t"), w=[cvBs[c4]])
                    K.dma(sp, attnT[:, c4 * 4:(c4 + 1) * 4, :], p_attn[c4 * 4:(c4 + 1) * 4].rearrange("c p t -> p c t"), w=[atBs[c4]])
                tctr = 0
                for c in range(32):
                    K.dma(sp, gat[:, c % 2, 0, :], p_ga[c], w=[gaB[c % 2][0]])
                    K.dma(sp, gat[:, c % 2, 1, :], p_gb[c], w=[gaB[c % 2][1]])
                    ws = []
                    for src_ in (w_co_t[c], w_ao_t[c]):
                        i = wctr[0] % 4
                        wctr[0] += 1
                        K.dma(pool, wslots[i][:, 0:2048], src_, w=[wslotB[i]], max_dma_last_dim=8192)
                        ws.append((wslots[i], wslotB[i]))
                    for tg in range(2):
                        sl = slice(tg * 512, (tg + 1) * 512)
                        for a, (rhs, rB) in enumerate(((convT, cvBs), (attnT, atBs))):
                            bk = (tg * 2 + a) % 4
                            wt, wb = ws[a]
                            fns = [lambda kc=kc, wt=wt, rhs=rhs, bk=bk: nc.tensor.matmul(
                                banks[bk][:, 0:512], wt[:, kc * 128:(kc + 1) * 128], rhs[:, kc, sl],
                                start=(kc == 0), stop=(kc == 15)) for kc in range(16)]
                            K.group(pe, fns, r=[wb] + rB, w=[bankB[bk]])
                        t1, t2 = tctr % 4, (tctr + 1) % 4
                        tctr += 2
                        K.op(dve, lambda: nc.vector.tensor_tensor(out=tmp[:, t1, :], in0=banks[(tg * 2) % 4][:, 0:512],
                                                                  in1=gat[:, c % 2, 0, sl], op=ALU.mult),
                             r=[bankB[(tg * 2) % 4], gaB[c % 2][0]], w=[tmpB[t1]])
                        K.op(dve, lambda: nc.vector.tensor_tensor(out=tmp[:, t2, :], in0=banks[(tg * 2 + 1) % 4][:, 0:512],
                                                                  in1=gat[:, c % 2, 1, sl], op=ALU.mult),
                             r=[bankB[(tg * 2 + 1) % 4], gaB[c % 2][1]], w=[tmpB[t2]])
                        K.op(dve, lambda: nc.vector.tensor_tensor(out=mergedT[:, c, sl], in0=tmp[:, t1, :], in1=tmp[:, t2, :],
                                                                  op=ALU.add), r=[tmpB[t1], tmpB[t2]], w=[mgB])
                K.barrier()
            if lvl < order.index("E"):
                return
            with ExitStack() as es3:
                sb3 = lambda n, s, d: es3.enter_context(_sbuf_unique(n, s, d))
                zst = sb3("zst", [128, 4, 512], F32)
                junk = sb3("junk", [128, 2, 512], BF16)
                zB = [Buf() for _ in range(4)]
                jB = [Buf(), Buf()]
                zc = [0]

                def evacE(cg, tt, bank, bb):
                    i = zc[0] % 4
                    zc[0] += 1
                    K.op(dve, lambda: nc.vector.tensor_copy(out=zst[:, i, :], in_=bank[:, 0:512]), r=[bb], w=[zB[i]])
                    K.op(act, lambda: nc.scalar.activation(out=junk[:, i % 2, :], in_=zst[:, i, :], func=AF.Square,
                                                           accum_out=ssq[:, tt, cg:cg + 1]), r=[zB[i]], w=[jB[i % 2], ssqB])
                    K.dma(sp, p_z[tt * 128:(tt + 1) * 128, cg * 512:(cg + 1) * 512], zst[:, i, :], r=[zB[i]])

                tm_proj(lambda cg, kb: w_out_t[cg, kb], mergedT, mgB, 4, evacE)
                K.barrier()

    def make_pre(zsrc, gP, gPb, zts, ztBs, st2, st2B, store_to):
        def pre(i, x_t, x_b):
            zt, ztB = zts[i % len(zts)], ztBs[i % len(zts)]
            s2 = st2[:, (i % 2) * 4:(i % 2) * 4 + 4]
            s2B = st2B[i % 2]
            K.dma(sp, zt[:], zsrc[i * 128:(i + 1) * 128, :], w=[ztB])
            K.op(dve, lambda: nc.vector.tensor_reduce(out=s2[:, 0:1], in_=ssq[:, i, :], axis=AX.X, op=ALU.add),
                 r=[ssqB], w=[s2B])
            K.op(act, lambda: nc.scalar.activation(out=s2[:, 1:2], in_=s2[:, 0:1], func=AF.Sqrt, scale=1.0 / D,
                                                   bias=epsn[:, 0:1]), r=[constB], w=[s2B])
            K.op(dve, lambda: nc.vector.reciprocal(out=s2[:, 2:3], in_=s2[:, 1:2]), w=[s2B])
            K.op(dve, lambda: nc.vector.scalar_tensor_tensor(out=zt[:], in0=zt[:], scalar=s2[:, 2:3], in1=gP, op0=ALU.mult,
                                                             op1=ALU.mult), r=[s2B, gPb], w=[ztB])
            K.op(dve, lambda: nc.vector.tensor_tensor(out=x_t[:], in0=x_t[:], in1=zt[:], op=ALU.add), r=[ztB], w=[x_b])
            if store_to is not None:
                K.dma(pool, store_to[i * 128:(i + 1) * 128, :], x_t[:], r=[x_b])
        return pre

    def phase_F():
        with ExitStack() as es:
            sbt = lambda n, s, d: es.enter_context(_sbuf_unique(n, s, d))
            hT = sbt("hT", [128, KC, T], BF16)
            hTB = Buf("hT")
            with ExitStack() as es2:
                sb2 = lambda n, s, d: es2.enter_context(_sbuf_unique(n, s, d))
                xt0, xt1 = sb2("xt0", [128, D], F32), sb2("xt1", [128, D], F32)
                zt = sb2("zt", [128, D], F32)
                gP, gM = sb2("gP", [128, D], F32), sb2("gM", [128, D], F32)
                ub = sb2("ub", [128, D], BF16)
                ubx = sb2("ubx", [128, D], BF16)
                st, st2 = sb2("st", [128, 8], F32), sb2("st2", [128, 8], F32)
                gPb, gMb, ubB, ztB, st2B = Buf(), Buf(), Buf(), Buf(), [Buf(), Buf()]
                K.dma(sp, gP[:], gains[1:2, :].to_broadcast([128, D]), w=[gPb])
                K.dma(sp, gM[:], gains[2:3, :].to_broadcast([128, D]), w=[gMb])
                pre = make_pre(p_z, gP[:], gPb, [zt], [ztB], st2, st2B, p_x1)
                norm_T(xo, gM[:], gMb, hT, hTB, NT, [xt0, xt1], [Buf(), Buf()], [ub, ubx], [ubB, Buf()], st, [Buf(), Buf()], pre=pre)
                K.barrier()
            with ExitStack() as es3:
                sb3 = lambda n, s, d: es3.enter_context(_sbuf_unique(n, s, d))
                actq = sb3("actq", [128, 32, T], BF16)
                rl = sb3("rl", [128, 4, 512], F32)
                part = sb3("part", [128, NT, 512], F32)
                ost = sb3("ost", [128, 4, 512], F32)
                junk = sb3("junk2", [128, 2, 512], BF16)
                aqB = Buf("actq")
                rlB = [Buf() for _ in range(4)]
                partB = [Buf() for _ in range(NT)]
                ostB = [Buf() for _ in range(4)]
                jB = [Buf(), Buf()]
                accB = [[Buf() for _ in range(8)] for _ in range(NT)]
                ctr = [0, 0]
                for qf in range(4):
                    def evacF(ci, gi, bank, bb, lo, n):
                        i = ctr[0] % 4
                        ctr[0] += 1
                        K.op(act, lambda: nc.scalar.activation(out=rl[:, i, :], in_=bank[:, 0:512], func=AF.Relu), r=[bb], w=[rlB[i]])
                        K.op(dve, lambda: nc.vector.tensor_tensor(out=actq[:, ci, lo:lo + n], in0=rl[:, i, :], in1=rl[:, i, :],
                                                                  op=ALU.mult), r=[rlB[i]], w=[aqB])
                    proj_fm([w_up_t[qf * 32 + f] for f in range(32)], hT, hTB, KC, [(0, 512), (512, 512)], evacF)

                    def evacG(cg, tt, bank, bb, qf=qf):
                        i = ctr[1] % 4
                        ctr[1] += 1
                        dst = p_acc[tt * 128:(tt + 1) * 128, cg * 512:(cg + 1) * 512]
                        if qf == 0:
                            K.op(dve, lambda: nc.vector.tensor_copy(out=ost[:, i, :], in_=bank[:, 0:512]), r=[bb], w=[ostB[i]])
                        else:
                            K.op(dve, lambda: nc.vector.tensor_tensor(out=ost[:, i, :], in0=bank[:, 0:512], in1=part[:, tt, :],
                                                                      op=ALU.add), r=[bb, partB[tt]], w=[ostB[i]])
                        if qf == 3:
                            K.op(act, lambda: nc.scalar.activation(out=junk[:, i % 2, :], in_=ost[:, i, :], func=AF.Square,
                                                                   accum_out=ssq[:, tt, cg:cg + 1]), r=[ostB[i]], w=[jB[i % 2], ssqB])
                        K.dma(sp, dst, ost[:, i, :], r=[ostB[i]], w=[accB[tt][cg]])
                    def preG(cg, qf=qf):
                        if qf > 0:
                            for tt in range(NT):
                                K.dma(sp, part[:, tt, :], p_acc[tt * 128:(tt + 1) * 128, cg * 512:(cg + 1) * 512],
                                      r=[accB[tt][cg]], w=[partB[tt]])
                    tm_proj(lambda cg, kb, qf=qf: w_dn_t[qf, cg, kb], actq, aqB, 4, evacG, pre_cg=preG)
                K.barrier()
        with ExitStack() as es4:
            sb4 = lambda n, s, d: es4.enter_context(_sbuf_unique(n, s, d))
            xt = [sb4("xf0", [128, D], F32), sb4("xf1", [128, D], F32), sb4("xf2", [128, D], F32)]
            zts = [sb4("zf0", [128, D], F32), sb4("zf1", [128, D], F32), sb4("zf2", [128, D], F32)]
            gP = sb4("gF", [128, D], F32)
            st2 = sb4("stf", [128, 8], F32)
            gPb, ztBs, st2B = Buf(), [Buf(), Buf(), Buf()], [Buf(), Buf()]
            xB = [Buf(), Buf(), Buf()]
            K.dma(sp, gP[:], gains[3:4, :].to_broadcast([128, D]), w=[gPb])
            pre = make_pre(p_acc, gP[:], gPb, zts, ztBs, st2, st2B, out)
            for i in range(NT):
                K.dma(sp, xt[i % 3][:], p_x1[i * 128:(i + 1) * 128, :], w=[xB[i % 3]])
                pre(i, xt[i % 3], xB[i % 3])
            K.barrier()

    if lvl >= order.index("D"):
        phase_DE()
    if lvl >= order.index("F"):
        phase_F()

    K.barrier()
    nc.all_engine_barrier()
    cm.__exit__(None, None, None)
    return nc


def _col_index():
    idx = []
    for i in range(16):
        idx += list(range(128 * i, 128 * i + 128))
        idx += list(range(2048 + 128 * i, 2048 + 128 * i + 128))
    idx += list(range(4096, 6144))
    idx += list(range(6144, 9216))
    idx += list(range(9264, 13360))
    idx += list(range(13360, 17456))
    idx += list(range(9216, 9264))
    return np.asarray(idx, dtype=np.int64)


def _tile_cols(w, ncols_chunk=128):
    Kd, N = w.shape
    a = w.reshape(Kd // 128, 128, N // 128, 128).transpose(2, 1, 0, 3)
    return np.ascontiguousarray(a).reshape(N // 128, 128, (Kd // 128) * 128)


def _shared_inputs(inp):
    f32 = np.float32
    w_in = np.asarray(inp["w_in"], f32)[0]
    wp = np.zeros((D, NCH_IN * 128), f32)
    ci = _col_index()
    wp[:, :ci.size] = w_in[:, ci]
    sh = {}
    sh["w_in_t"] = _tile_cols(wp)
    del wp
    sh["w_co_t"] = _tile_cols(np.asarray(inp["w_conv_out"], f32)[0])
    sh["w_ao_t"] = _tile_cols(np.asarray(inp["w_attn_out"], f32)[0])
    w_out = np.asarray(inp["w_out"], f32)[0]
    sh["w_out_t"] = np.ascontiguousarray(
        w_out.reshape(4, 8, 128, 8, 512).transpose(3, 0, 2, 1, 4)).reshape(8, 4, 128, D)
    sh["w_up_t"] = _tile_cols(np.asarray(inp["w_up"], f32)[0])
    w_dn = np.asarray(inp["w_down"], f32)[0]
    sh["w_dn_t"] = np.ascontiguousarray(
        w_dn.reshape(4, 4, 8, 128, 8, 512).transpose(0, 4, 1, 3, 2, 5)).reshape(4, 8, 4, 128, D)
    sh["gains"] = np.ascontiguousarray(np.stack([np.asarray(inp[k], f32)[0] for k in
                                                 ("norm_mix_pre", "norm_mix_post", "norm_mlp_pre", "norm_mlp_post")]))
    cp = np.zeros((128, 16, 34), f32)
    cp[:, :, 0:31] = np.asarray(inp["w_dw"], f32)[0, :, 0, :].reshape(31, 16, 128).transpose(2, 1, 0)
    cp[:, :, 31] = np.asarray(inp["b_dw"], f32)[0].reshape(16, 128).T
    cp[:, :, 32] = np.asarray(inp["ln_conv_g"], f32)[0].reshape(16, 128).T
    cp[:, :, 33] = np.asarray(inp["ln_conv_b"], f32)[0].reshape(16, 128).T
    sh["convp"] = cp
    for nm, k1, k2 in (("k", "w_cmp_k1", "w_cmp_k2"), ("v", "w_cmp_v1", "w_cmp_v2")):
        w1 = np.asarray(inp[k1], f32)[0]
        sh[f"w1{nm}_t"] = np.ascontiguousarray(w1.reshape(32, 128, 256).transpose(1, 0, 2)).reshape(128, 8192)
        w2 = np.asarray(inp[k2], f32)[0]
        sh[f"w2{nm}_t"] = np.ascontiguousarray(w2.reshape(2, 128, 128).transpose(1, 0, 2)).reshape(128, 256)
    sh["posT"] = np.ascontiguousarray(np.concatenate([np.asarray(inp["pos_cmp_k"], f32)[0].T,
                                                      np.asarray(inp["pos_cmp_v"], f32)[0].T], axis=1))
    p = np.arange(128)[:, None, None]
    dl = (np.arange(8)[None, :, None] - 4) * 128
    qq = np.arange(512)[None, None, :]
    j = dl + p
    sh["m8"] = ((((j <= qq) & (j > qq - 512)).astype(np.float32) - 1.0) * MASK_BIG).astype(NPBF).reshape(128, 8 * 512)
    n = np.arange(128)[:, None]
    m = np.arange(32)[None, :]
    rc = np.zeros((128, 33), np.float32)
    rc[:127, 0] = 1.0
    rc[:, 1:] = ((n >= 4 * m - 1) & (n <= 4 * m + 3) & (n < 127))
    sh["rcc"] = rc.astype(NPBF)
    sh["emat"] = (np.arange(2048)[None, :] // 64 == np.arange(32)[:, None]).astype(NPBF)
    sh["identb"] = np.eye(128, dtype=np.float32).astype(NPBF)
    sh["identf"] = np.eye(128, dtype=np.float32)
    return sh


def _half_inputs(h):
    t = np.arange(T)
    t_real = t + 1024 * h
    n = np.arange(128)[:, None]
    cv = (16 * n + 31 <= t[None, :] + 1024) & (n <= 126)
    if h == 0:
        cv &= (n >= 64)
    m = np.arange(32)[None, :]
    jr = m - 16 * (1 - h)
    tr = t_real[:, None]
    valid = (jr >= 0) & (64 * jr <= tr)
    cur = tr // 64
    forced = (jr == 0) | (jr == cur) | (jr == cur - 1)
    sv = (valid & ~forced).astype(np.float32)
    sb = np.where(forced & valid, 1e4, np.where(valid, 0.0, -1e4)).astype(np.float32)
    lay = lambda a: np.ascontiguousarray(a.reshape(NT, 128, 32).transpose(1, 0, 2)).reshape(128, NT * 32)
    kv = np.ones((128, 16), np.float32)
    if h == 0:
        kv[:, :8] = 0.0
    cvb = ((cv.astype(np.float32) - 1.0) * MASK_BIG).astype(NPBF)
    return {"cvis": cvb, "selv": lay(sv), "selb": lay(sb), "kval": kv.astype(NPBF)}


def make_in_maps(inp):
    sh = _shared_inputs(inp)
    x = np.asarray(inp["x"], np.float32)
    halves = [_half_inputs(0), _half_inputs(1)]
    zeros = np.zeros((T, D), np.float32)
    maps = []
    for c in range(8):
        b, h = c // 2, c % 2
        mp = dict(sh)
        mp.update(halves[h])
        mp["xo"] = np.ascontiguousarray(x[b, h * T:(h + 1) * T])
        mp["xc"] = np.ascontiguousarray(x[b, 0:T]) if h == 1 else zeros
        maps.append(mp)
    return maps


def kernel(**inputs):
    maps = make_in_maps(inputs)
    nc = build_nc()
    res = run_bass_kernel_spmd(nc, maps, core_ids=list(range(8)))
    outp = np.zeros((4, 2 * T, D), np.float32)
    for c in range(8):
        b, h = c // 2, c % 2
        outp[b, h * T:(h + 1) * T] = np.asarray(res.results[c]["out"], np.float32)
    return outp
```

```python
from contextlib import ExitStack
import numpy as np
import ml_dtypes
import concourse.bass as bass
import concourse.mybir as mybir
from concourse.bass_utils import run_bass_kernel_spmd

F32 = mybir.dt.float32
BF16 = mybir.dt.bfloat16
AF = mybir.ActivationFunctionType
ALU = mybir.AluOpType
AX = mybir.AxisListType
NPBF = ml_dtypes.bfloat16

D = 4096
T = 1024
NT = 8
KC = 32
DFF = 16384
NCH_IN = 137
NORM_EPS = 1e-6
LN_EPS = 1e-5
SCALE = 128 ** -0.5
MASK_BIG = 32768.0
CH_GLU = 0
CH_Q = 32
CH_KV = 48
CH_GA = 72
CH_GB = 104
CH_NG = 136


class Sem:
    def __init__(self, nc, name):
        self.h = nc.alloc_semaphore(name)
        self.n = 0


class Buf:
    __slots__ = ("name", "w", "rs")

    def __init__(self, name=""):
        self.name = name
        self.w = None
        self.rs = {}


class Q:
    def __init__(self, nc, eng, name):
        self.eng = eng
        self.name = name
        self.sem = Sem(nc, "s_" + name)
        self.seen = {}

    def wait(self, t):
        if t is None:
            return
        s, v = t
        if self.seen.get(id(s), 0) >= v:
            return
        self.eng.wait_ge(s.h, v)
        self.seen[id(s)] = v


class KB:
    def __init__(self, nc):
        self.nc = nc
        self.pe = Q(nc, nc.tensor, "pe")
        self.act = Q(nc, nc.scalar, "act")
        self.dve = Q(nc, nc.vector, "dve")
        self.pool = Q(nc, nc.gpsimd, "pool")
        self.sp = Q(nc, nc.sync, "sp")
        self.qs = [self.pe, self.act, self.dve, self.pool, self.sp]
        self.dsems = {id(self.sp): [Sem(nc, f"dsp{i}") for i in range(24)],
                      id(self.pool): [Sem(nc, f"dpl{i}") for i in range(24)],
                      id(self.act): [Sem(nc, f"dac{i}") for i in range(8)]}
        self.dnext = {k: 0 for k in self.dsems}

    def _deps(self, q, r, w):
        for b in r:
            q.wait(b.w)
        for b in w:
            q.wait(b.w)
            for t in b.rs.values():
                q.wait(t)

    def _commit(self, t, r, w):
        for b in r:
            b.rs[id(t[0])] = t
        for b in w:
            b.w = t
            b.rs = {}

    def op(self, q, fn, r=(), w=()):
        self._deps(q, r, w)
        inst = fn()
        q.sem.n += 1
        inst.then_inc(q.sem.h, 1)
        t = (q.sem, q.sem.n)
        self._commit(t, r, w)
        return t

    def group(self, q, fns, r=(), w=()):
        self._deps(q, r, w)
        inst = None
        for fn in fns:
            inst = fn()
        q.sem.n += 1
        inst.then_inc(q.sem.h, 1)
        t = (q.sem, q.sem.n)
        self._commit(t, r, w)
        return t

    def dma(self, q, out, in_, r=(), w=(), **kw):
        self._deps(q, r, w)
        ring = self.dsems[id(q)]
        i = self.dnext[id(q)]
        self.dnext[id(q)] = (i + 1) % len(ring)
        s = ring[i]
        if s.n:
            q.wait((s, s.n))
        inst = q.eng.dma_start(out=out, in_=in_, **kw)
        s.n += 16
        inst.then_inc(s.h, 16)
        t = (s, s.n)
        self._commit(t, r, w)
        return t

    def barrier(self):
        ts = [(q.sem, q.sem.n) for q in self.qs if q.sem.n]
        for ring in self.dsems.values():
            ts += [(s, s.n) for s in ring if s.n]
        for q in self.qs:
            for t in ts:
                q.wait(t)


def build_nc(debug=False, upto="all"):
    nc = bass.Bass("TRN2", target_bir_lowering=False)
    order = ["A", "B", "C", "D", "E", "F", "all"]
    lvl = order.index(upto)

    def din(name, shape, dt=F32):
        return nc.dram_tensor(name, list(shape), dt, kind="ExternalInput").ap()

    def dscr(name, shape, dt=F32):
        kind = "ExternalOutput" if debug else "Internal"
        return nc.dram_tensor(name, list(shape), dt, kind=kind).ap()

    xo = din("xo", [T, D])
    xc = din("xc", [T, D])
    w_in_t = din("w_in_t", [NCH_IN, 128, D])
    w_co_t = din("w_co_t", [32, 128, 16 * 128])
    w_ao_t = din("w_ao_t", [32, 128, 16 * 128])
    w_out_t = din("w_out_t", [8, 4, 128, D])
    w_up_t = din("w_up_t", [128, 128, D])
    w_dn_t = din("w_dn_t", [4, 8, 4, 128, D])
    gains = din("gains", [4, D])
    convp = din("convp", [128, 16, 34])
    w1k_t = din("w1k_t", [128, 32 * 256])
    w1v_t = din("w1v_t", [128, 32 * 256])
    w2k_t = din("w2k_t", [128, 2 * 128])
    w2v_t = din("w2v_t", [128, 2 * 128])
    posT = din("posT", [128, 64])
    m8 = din("m8", [128, 8 * 512], BF16)
    cvis = din("cvis", [128, T], BF16)
    selv = din("selv", [128, NT * 32])
    selb = din("selb", [128, NT * 32])
    rcc = din("rcc", [128, 33], BF16)
    kval = din("kval", [128, 16], BF16)
    emat = din("emat", [32, 2048], BF16)
    identb = din("identb", [128, 128], BF16)
    identf = din("identf", [128, 128])
    out = nc.dram_tensor("out", [T, D], F32, kind="ExternalOutput").ap()

    p_glu = dscr("p_glu", [16, 128, 1152])
    p_q = dscr("p_q", [16, 128, T], BF16)
    p_kv = dscr("p_kv", [24, 128, 2048], BF16)
    p_ga = dscr("p_ga", [32, 128, T], BF16)
    p_gb = dscr("p_gb", [32, 128, T], BF16)
    p_attn = dscr("p_attn", [16, 128, T], BF16)
    p_conv = dscr("p_conv", [16, 128, T], BF16)
    p_z = dscr("p_z", [T, D])
    p_x1 = dscr("p_x1", [T, D])
    p_acc = dscr("p_acc", [T, D])
    p_y = dscr("p_y", [16, 128, T])

    cm = nc.cleanup_on_exit()
    cm.__enter__()
    _orig_sbuf_tensor = nc.sbuf_tensor
    _uid = [0]

    def _sbuf_unique(name, shape, dt):
        _uid[0] += 1
        return _orig_sbuf_tensor(f"{name}_{_uid[0]}", shape, dt)
    K = KB(nc)
    pe, act, dve, pool, sp = K.pe, K.act, K.dve, K.pool, K.sp

    banks = [nc.alloc_psum_tensor(f"bank{i}", [128, 512], F32) for i in range(8)]
    bankB = [Buf(f"bank{i}") for i in range(8)]
    wslots = [nc.alloc_sbuf_tensor(f"wslot{i}", [128, D], BF16) for i in range(4)]
    wslotB = [Buf(f"wslot{i}") for i in range(4)]
    wctr = [0]
    idb = nc.alloc_sbuf_tensor("idb", [128, 128], BF16)
    idf = nc.alloc_sbuf_tensor("idf", [128, 128], F32)
    gatesT = nc.alloc_sbuf_tensor("gatesT", [128, T], F32)
    gates = nc.alloc_sbuf_tensor("gates", [128, NT * 48], F32)
    constB = Buf("const")
    gatesTB = Buf("gatesT")
    gatesB = Buf("gates")
    K.dma(sp, idb[:], identb[:, :], w=[constB])
    K.dma(sp, idf[:], identf[:, :], w=[constB])

    def load_w(src):
        i = wctr[0] % 4
        wctr[0] += 1
        K.dma(pool, wslots[i][:], src, w=[wslotB[i]], max_dma_last_dim=8192)
        return wslots[i], wslotB[i]

    def norm_T(src, gB, gBb, dstT, dstB, ntiles, xt, xtB, ubs, ubBs, st, stB, pre=None):
        for i in range(ntiles):
            x_t, x_b = xt[i % 2], xtB[i % 2]
            K.dma(sp, x_t[:], src[i * 128:(i + 1) * 128, :], w=[x_b])
            if pre is not None:
                pre(i, x_t, x_b)
            s_t = st[:, (i % 2) * 4:(i % 2) * 4 + 4]
            s_b = stB[i % 2]
            ub, ubB = ubs[i % 2], ubBs[i % 2]
            K.op(act, lambda: nc.scalar.activation(out=ub[:], in_=x_t[:], func=AF.Square,
                                                   accum_out=s_t[:, 0:1]), r=[x_b], w=[ubB, s_b])
            K.op(act, lambda: nc.scalar.activation(out=s_t[:, 1:2], in_=s_t[:, 0:1], func=AF.Sqrt,
                                                   scale=1.0 / D, bias=epsn[:, 0:1]), r=[constB], w=[s_b])
            K.op(dve, lambda: nc.vector.reciprocal(out=s_t[:, 2:3], in_=s_t[:, 1:2]), w=[s_b])
            K.op(dve, lambda: nc.vector.scalar_tensor_tensor(out=ub[:], in0=x_t[:], scalar=s_t[:, 2:3],
                                                             op0=ALU.mult, in1=gB, op1=ALU.mult),
                 r=[x_b, s_b, gBb], w=[ubB])
            for g in range(4):
                bk = 4 + g
                pst = banks[bk][:].bitcast(BF16)
                fns = []
                for j in range(8):
                    kc = g * 8 + j
                    fns.append(lambda kc=kc, j=j, pst=pst: nc.tensor.transpose(
                        out=pst[:, j * 128:(j + 1) * 128], in_=ub[:, kc * 128:(kc + 1) * 128], identity=idb[:]))
                K.group(pe, fns, r=[ubB, constB], w=[bankB[bk]])
                dst = dstT[:, g * 8:(g + 1) * 8, i * 128:(i + 1) * 128]
                srcv = pst.rearrange("p (a b) -> p a b", a=8)
                dB_ = dstB[g] if isinstance(dstB, list) else dstB
                if g % 2 == 0:
                    K.op(dve, lambda: nc.vector.tensor_copy(out=dst, in_=srcv), r=[bankB[bk]], w=[dB_])
                else:
                    K.op(act, lambda: nc.scalar.copy(out=dst, in_=srcv), r=[bankB[bk]], w=[dB_])

    def proj_fm(wsrcs, rhsT, rhsB, nkc, tok_groups, evac, bank0=0, after_chunk=None):
        cnt = 0
        for ci, src in enumerate(wsrcs):
            if nkc == 32:
                wt, wb = load_w(src)
            else:
                i = wctr[0] % 4
                wctr[0] += 1
                K.dma(pool, wslots[i][:, 0:nkc * 128], src, w=[wslotB[i]], max_dma_last_dim=8192)
                wt, wb = wslots[i], wslotB[i]
            for gi, grp in enumerate(tok_groups):
                lo, n = grp[0], grp[1]
                rT, rB = (grp[2], grp[3]) if len(grp) == 4 else (rhsT, rhsB)
                bk = bank0 + (cnt % 4)
                cnt += 1
                fns = []
                for kc in range(nkc):
                    fns.append(lambda kc=kc, bk=bk, wt=wt, lo=lo, n=n, rT=rT: nc.tensor.matmul(
                        banks[bk][:, 0:n], wt[:, kc * 128:(kc + 1) * 128], rT[:, kc, lo:lo + n],
                        start=(kc == 0), stop=(kc == nkc - 1)))
                K.group(pe, fns, r=[wb] + (rB if isinstance(rB, list) else [rB]), w=[bankB[bk]])
                evac(ci, gi, banks[bk], bankB[bk], lo, n)
            if after_chunk is not None:
                after_chunk()

    epsn = nc.alloc_sbuf_tensor("epsn", [128, 2], F32)
    K.op(dve, lambda: nc.vector.memset(epsn[:, 0:1], NORM_EPS), w=[constB])
    K.op(dve, lambda: nc.vector.memset(epsn[:, 1:2], LN_EPS), w=[constB])

    onesf = nc.alloc_sbuf_tensor("onesf", [128, 128], F32)
    cps = nc.alloc_sbuf_tensor("cps", [128, 16, 34], F32)
    ccB = Buf("convconst")
    K.op(dve, lambda: nc.vector.memset(onesf[:], 1.0), w=[ccB])
    K.dma(sp, cps[:], convp[:, :, :], w=[ccB])
    stg = None
    with _sbuf_unique("uT", [128, KC, T], BF16) as uT, \
            _sbuf_unique("xt0", [128, D], F32) as xt0, _sbuf_unique("xt1", [128, D], F32) as xt1, \
            _sbuf_unique("gB", [128, D], F32) as gB, _sbuf_unique("ub", [128, D], BF16) as ub, \
            _sbuf_unique("ubx", [128, D], BF16) as ubx, \
            _sbuf_unique("st", [128, 8], F32) as st, \
            _sbuf_unique("sgb", [128, 4, 512], BF16) as sgb, \
            _sbuf_unique("sgf", [128, 6, 512], F32) as sgf, \
            _sbuf_unique("utail", [128, KC, 128], BF16) as utail:
        uTB, gBb, ubB = [Buf(f"uT{i}") for i in range(4)], Buf("gB"), Buf("ub")
        ubs, ubBs = [ub, ubx], [ubB, Buf("ubx")]
        xtB = [Buf("xt0"), Buf("xt1")]
        stB = [Buf("st0"), Buf("st1")]
        sgbB = [Buf(f"sgb{i}") for i in range(4)]
        sgfB = [Buf(f"sgf{i}") for i in range(6)]
        utailB = Buf("utail")
        sctr = [0, 0]
        act_only = [False]
        gluB = [Buf(f"glu{c}") for c in range(16)]
        K.dma(sp, gB[:], gains[0:1, :].to_broadcast([128, D]), w=[gBb])

        def evac_store(dst_of, func):
            def evac(ci, gi, bank, bb, lo, n):
                i = sctr[0] % 4
                sctr[0] += 1
                if func is None and (sctr[0] % 2 == 0) and not act_only[0]:
                    K.op(dve, lambda: nc.vector.tensor_copy(out=sgb[:, i, 0:n], in_=bank[:, 0:n]), r=[bb], w=[sgbB[i]])
                else:
                    f = func if func is not None else AF.Identity
                    K.op(act, lambda: nc.scalar.activation(out=sgb[:, i, 0:n], in_=bank[:, 0:n], func=f),
                         r=[bb], w=[sgbB[i]])
                K.dma(sp, dst_of(ci, lo, n), sgb[:, i, 0:n], r=[sgbB[i]])
            return evac

        def evac_glu(tok_off_of):
            hold = {}

            def evac(ci, gi, bank, bb, lo, n):
                tok_off = tok_off_of(gi)
                i = sctr[1] % 6
                sctr[1] += 1
                if ci % 2 == 0:
                    K.op(act, lambda: nc.scalar.activation(out=sgf[:, i, 0:n], in_=bank[:, 0:n], func=AF.Identity),
                         r=[bb], w=[sgfB[i]])
                    hold[gi] = i
                else:
                    ia = hold[gi]
                    K.op(act, lambda: nc.scalar.activation(out=sgf[:, i, 0:n], in_=bank[:, 0:n], func=AF.Sigmoid),
                         r=[bb], w=[sgfB[i]])
                    K.op(dve, lambda: nc.vector.tensor_tensor(out=sgf[:, i, 0:n], in0=sgf[:, i, 0:n], in1=sgf[:, ia, 0:n],
                                                              op=ALU.mult), r=[sgfB[ia]], w=[sgfB[i]])
                    K.dma(sp, p_glu[ci // 2, :, tok_off + lo:tok_off + lo + n], sgf[:, i, 0:n], r=[sgfB[i]], w=[gluB[ci // 2]])
            return evac

        def evac_gates(ci, gi, bank, bb, lo, n):
            K.op(act, lambda: nc.scalar.activation(out=gatesT[:, lo:lo + n], in_=bank[:, 0:n], func=AF.Sigmoid),
                 r=[bb], w=[gatesTB])

        tg2 = [(0, 512), (512, 512)]
        xts = [xt0, xt1]
        norm_T(xc, gB[:], gBb, uT, uTB, NT, xts, xtB, ubs, ubBs, st, stB)
        proj_fm([w_in_t[CH_KV + j] for j in range(16)], uT, uTB, KC, tg2,
                evac_store(lambda ci, lo, n: p_kv[ci, :, lo:lo + n], None))
        proj_fm([w_in_t[CH_KV + 16 + j] for j in range(8)], uT, uTB, KC, [(512, 512)],
                evac_store(lambda ci, lo, n: p_kv[16 + ci, :, lo:lo + n], None))
        K.op(dve, lambda: nc.vector.tensor_copy(out=utail[:], in_=uT[:, :, 896:1024]), r=uTB, w=[utailB])
        norm_T(xo, gB[:], gBb, uT, uTB, NT, xts, xtB, ubs, ubBs, st, stB)
        proj_fm([w_in_t[CH_GLU + j] for j in range(32)], uT, uTB, KC, tg2 + [(0, 128, utail, utailB)],
                evac_glu(lambda gi: 128 if gi < 2 else 0))
        def inherit(olds):
            b = Buf()
            for o in olds:
                for t in list(o.rs.values()) + ([o.w] if o.w is not None else []):
                    k = id(t[0])
                    if k not in b.rs or b.rs[k][1] < t[1]:
                        b.rs[k] = t
            return b


        def conv_gen():
            ubuf = xt0[:, 0:2304].rearrange("p (a b) -> p a b", a=2)
            ybuf = xt1[:, 0:3072].rearrange("p (a b) -> p a b", a=3)
            sqb = sgf[:, 0:4, :].rearrange("p a b -> p (a b)").rearrange("p (a b) -> p a b", a=2)
            mean = gB[:, 0:T]
            rstd = gB[:, T:2 * T]
            cstg = ub[:, 0:2 * T].rearrange("p (a b) -> p a b", a=2)
            ubB_ = [inherit([xtB[0]]), inherit([xtB[0]])]
            yB_ = [inherit([xtB[1]]) for _ in range(3)]
            sqB_ = [inherit(sgfB), inherit(sgfB)]
            mB_ = inherit([gBb])
            csB_ = [inherit([ubBs[0]]), inherit([ubBs[0]])]
            pyB = [Buf() for _ in range(16)]

            def load_u(c):
                K.dma(sp, ubuf[:, c % 2, :], p_glu[c], r=[gluB[c]], w=[ubB_[c % 2]])

            load_u(0)
            for u in range(18):
                if u + 1 < 16:
                    load_u(u + 1)
                if u < 16:
                    c = u
                    u_, y_ = ubuf[:, c % 2, :], ybuf[:, c % 3, :]
                    K.op(dve, lambda: nc.vector.tensor_scalar(out=y_, in0=u_[:, 98:98 + T], scalar1=cps[:, c, 0:1],
                                                              scalar2=cps[:, c, 31:32], op0=ALU.mult, op1=ALU.add),
                         r=[ubB_[c % 2], ccB], w=[yB_[c % 3]])
                    for j in range(1, 31):
                        K.op(dve, lambda: nc.vector.scalar_tensor_tensor(out=y_, in0=u_[:, 98 + j:98 + j + T],
                                                                         scalar=cps[:, c, j:j + 1], in1=y_, op0=ALU.mult,
                                                                         op1=ALU.add), r=[ubB_[c % 2]], w=[yB_[c % 3]])
                if 0 <= u - 1 < 16:
                    c = u - 1
                    K.op(act, lambda: nc.scalar.activation(out=sqb[:, c % 2, :], in_=ybuf[:, c % 3, :], func=AF.Square),
                         r=[yB_[c % 3]], w=[sqB_[c % 2]])
                if 0 <= u - 2 < 16:
                    c = u - 2
                    for tg in range(2):
                        sl = slice(tg * 512, (tg + 1) * 512)
                        K.op(pe, lambda: nc.tensor.matmul(banks[4 + tg][:, 0:512], onesf[:], ybuf[:, c % 3, sl],
                                                          start=(c == 0), stop=(c == 15)), r=[yB_[c % 3], ccB], w=[bankB[4 + tg]])
                        K.op(pe, lambda: nc.tensor.matmul(banks[6 + tg][:, 0:512], onesf[:], sqb[:, c % 2, sl],
                                                          start=(c == 0), stop=(c == 15)), r=[sqB_[c % 2], ccB], w=[bankB[6 + tg]])
                    K.dma(sp, p_y[c], ybuf[:, c % 3, :], r=[yB_[c % 3]], w=[pyB[c]])
                yield
            for tg in range(2):
                sl = slice(tg * 512, (tg + 1) * 512)
                K.op(act, lambda: nc.scalar.activation(out=mean[:, sl], in_=banks[4 + tg][:, 0:512], func=AF.Identity,
                                                       scale=1.0 / 2048), r=[bankB[4 + tg]], w=[mB_])
                K.op(dve, lambda: nc.vector.tensor_tensor(out=rstd[:, sl], in0=mean[:, sl], in1=mean[:, sl], op=ALU.mult), w=[mB_])
                K.op(dve, lambda: nc.vector.scalar_tensor_tensor(out=rstd[:, sl], in0=banks[6 + tg][:, 0:512], scalar=1.0 / 2048,
                                                                 in1=rstd[:, sl], op0=ALU.mult, op1=ALU.subtract),
                     r=[bankB[6 + tg]], w=[mB_])
            yield
            for tg in range(2):
                sl = slice(tg * 512, (tg + 1) * 512)
                K.op(act, lambda: nc.scalar.activation(out=rstd[:, sl], in_=rstd[:, sl], func=AF.Sqrt, bias=epsn[:, 1:2]),
                     r=[constB], w=[mB_])
            yield
            for tg in range(2):
                sl = slice(tg * 512, (tg + 1) * 512)
                K.op(dve, lambda: nc.vector.reciprocal(out=rstd[:, sl], in_=rstd[:, sl]), w=[mB_])

            def load_y(c):
                K.dma(sp, ybuf[:, c % 3, :], p_y[c], r=[pyB[c]], w=[yB_[c % 3]])

            load_y(0)
            yield
            for v in range(18):
                if v + 1 < 16:
                    load_y(v + 1)
                if v < 16:
                    c = v
                    y_ = ybuf[:, c % 3, :]
                    K.op(dve, lambda: nc.vector.tensor_tensor(out=y_, in0=y_, in1=mean, op=ALU.subtract), r=[mB_], w=[yB_[c % 3]])
                    K.op(dve, lambda: nc.vector.tensor_tensor(out=y_, in0=y_, in1=rstd, op=ALU.mult), r=[mB_], w=[yB_[c % 3]])
                if 0 <= v - 1 < 16:
                    c = v - 1
                    K.op(act, lambda: nc.scalar.activation(out=cstg[:, c % 2, :], in_=ybuf[:, c % 3, :], func=AF.Silu,
                                                           scale=cps[:, c, 32:33], bias=cps[:, c, 33:34]),
                         r=[yB_[c % 3], ccB], w=[csB_[c % 2]])
                if 0 <= v - 2 < 16:
                    c = v - 2
                    K.dma(sp, p_conv[c], cstg[:, c % 2, :], r=[csB_[c % 2]])
                yield

        cgen = conv_gen()
        cstate = [0, 0]

        def conv_tick():
            cstate[0] += 1
            period = 3 if cstate[1] < 18 else 2
            if cstate[0] % period == 0:
                cstate[1] += 1
                next(cgen, None)

        act_only[0] = True
        proj_fm([w_in_t[CH_Q + j] for j in range(16)], uT, uTB, KC, tg2,
                evac_store(lambda ci, lo, n: p_q[ci, :, lo:lo + n], None), after_chunk=conv_tick)
        proj_fm([w_in_t[CH_KV + j] for j in range(24)], uT, uTB, KC, tg2,
                evac_store(lambda ci, lo, n: p_kv[ci, :, 1024 + lo:1024 + lo + n], None), after_chunk=conv_tick)
        proj_fm([w_in_t[CH_GA + j] for j in range(32)], uT, uTB, KC, tg2,
                evac_store(lambda ci, lo, n: p_ga[ci, :, lo:lo + n], AF.Sigmoid), after_chunk=conv_tick)
        proj_fm([w_in_t[CH_GB + j] for j in range(32)], uT, uTB, KC, tg2,
                evac_store(lambda ci, lo, n: p_gb[ci, :, lo:lo + n], AF.Sigmoid), after_chunk=conv_tick)
        proj_fm([w_in_t[CH_NG]], uT, uTB, KC, tg2, evac_gates)
        for _ in cgen:
            pass
        K.barrier()

    def phase_B():
        with ExitStack() as es:
            w1k = es.enter_context(_sbuf_unique("w1k", [128, 8192], BF16))
            w1v = es.enter_context(_sbuf_unique("w1v", [128, 8192], BF16))
            w2k = es.enter_context(_sbuf_unique("w2k", [128, 256], BF16))
            w2v = es.enter_context(_sbuf_unique("w2v", [128, 256], BF16))
            posb = es.enter_context(_sbuf_unique("posb", [128, 64], BF16))
            m8s = es.enter_context(_sbuf_unique("m8s", [128, 8, 512], BF16))
            cvs = es.enter_context(_sbuf_unique("cvs", [128, T], BF16))
            selvs = es.enter_context(_sbuf_unique("selvs", [128, NT * 32], F32))
            selbs = es.enter_context(_sbuf_unique("selbs", [128, NT * 32], F32))
            emats = es.enter_context(_sbuf_unique("emats", [32, 2048], BF16))
            kvals = es.enter_context(_sbuf_unique("kvals", [128, 16], BF16))
            cb = es.enter_context(_sbuf_unique("cb", [128, 4], F32))
            qT = es.enter_context(_sbuf_unique("qT", [128, 4, T], BF16))
            kvT = es.enter_context(_sbuf_unique("kvT", [128, 6, 2048], BF16))
            Rs = es.enter_context(_sbuf_unique("Rs", [128, 16, 129], BF16))
            Rw = es.enter_context(_sbuf_unique("Rw", [128, 16, 129], BF16))
            Rc = es.enter_context(_sbuf_unique("Rc", [128, 161], BF16))
            hid = es.enter_context(_sbuf_unique("hid", [128, 4, 128], BF16))
            gx = es.enter_context(_sbuf_unique("gx", [128, 3, 128], F32))
            kcmpT = es.enter_context(_sbuf_unique("kcmpT", [128, 128], BF16))
            msk = es.enter_context(_sbuf_unique("msk", [128, 16, 512], BF16))
            pvs = es.enter_context(_sbuf_unique("pvs", [128, 8, 161], F32))
            selm8 = es.enter_context(_sbuf_unique("selm8", [128, 8, 32], BF16))
            selmB = Buf("selm8")
            ptmp = es.enter_context(_sbuf_unique("ptmp", [128, 4, 160], F32))
            ptmpB = Buf("ptmp")
            m8c = es.enter_context(_sbuf_unique("m8c", [128, 4, 512], BF16))
            m8cB = Buf("m8c")
            emt = es.enter_context(_sbuf_unique("emt", [128, 8, 512], BF16))
            oacc = es.enter_context(_sbuf_unique("oacc", [128, NT, 512], F32))
            obf = es.enter_context(_sbuf_unique("obf", [128, NT, 512], BF16))
            imp = es.enter_context(_sbuf_unique("imp", [128, NT, 32], F32))
            selT = es.enter_context(_sbuf_unique("selT", [32, T], BF16))
            tk = es.enter_context(_sbuf_unique("tk", [128, 2, 32], F32))
            mx = es.enter_context(_sbuf_unique("mx", [128, 16], F32))
            selm = es.enter_context(_sbuf_unique("selm", [128, 32], BF16))
            sm = es.enter_context(_sbuf_unique("sm", [128, 16], F32))
            ostg = es.enter_context(_sbuf_unique("ostg", [128, 2, T], BF16))
            qTB, kvB = Buf("qT"), [Buf(f"kv{i}") for i in range(6)]
            kvalB, selvB, selbB, ematB, m8B, cvsB, cbB = Buf(), Buf(), Buf(), Buf(), Buf(), Buf(), Buf()
            grpW = [Buf() for _ in range(5)]
            RsB, RwB, RcB, hidB, gxB, kcmpB = Buf("Rs"), Buf("Rw"), Buf("Rc"), Buf("hid"), Buf("gx"), Buf("kcmp")
            mskB, oaccB, obfB, impB, selTB, tkB, smB = Buf(), Buf(), Buf(), Buf(), Buf(), Buf(), Buf()
            etB = [Buf() for _ in range(6)]
            emtB = [Buf() for _ in range(8)]
            pvsB = [Buf() for _ in range(2)]
            pctr, sbctr = [0], [0]
            ostgB = [Buf(), Buf()]
            ectr = [0]
            K.dma(sp, kvals[:], kval[:, :], w=[kvalB])
            K.dma(sp, Rc[:, 128:161], rcc[:, :], w=[RcB])
            K.dma(sp, selvs[:], selv[:, :], w=[selvB])
            K.dma(sp, selbs[:], selb[:, :], w=[selbB])
            K.dma(sp, emats[:], emat[:, :], w=[ematB])
            K.dma(sp, qT[:], p_q[0:4].rearrange("h p t -> p h t"), w=[qTB])
            for kd in (3, 5, 0, 1, 2, 4):
                K.dma(sp, kvT[:, kd, :], p_kv[kd * 4], w=[kvB[kd]])
            K.dma(pool, w1k[:], w1k_t[:, :], w=[grpW[0]], max_dma_last_dim=8192)
            K.dma(pool, w1v[:], w1v_t[:, :], w=[grpW[1]], max_dma_last_dim=8192)
            K.dma(pool, w2k[:], w2k_t[:, :], w=[grpW[2]])
            K.dma(pool, w2v[:], w2v_t[:, :], w=[grpW[3]])
            K.dma(pool, posb[:], posT[:, :], w=[grpW[4]])
            K.dma(sp, m8s[:].rearrange("p a b -> p (a b)"), m8[:, :], w=[m8B])
            K.dma(sp, cvs[:], cvis[:, :], w=[cvsB])
            K.op(dve, lambda: nc.vector.memset(hid[:], 0.0), w=[hidB])
            K.op(dve, lambda: nc.vector.memset(kcmpT[:], 0.0), w=[kcmpB])
            K.op(dve, lambda: nc.vector.memset(Rc[:, 0:128], 0.0), w=[RcB])
            for jt in range(16):
                K.op(dve, lambda: nc.vector.tensor_copy(out=Rs[:, jt, 128:129], in_=kvals[:, jt:jt + 1]), r=[kvalB], w=[RsB])
                K.op(dve, lambda: nc.vector.memset(Rw[:, jt, 128:129], 1.0), w=[RwB])
            for i in range(NT):
                K.op(pe, lambda: nc.tensor.transpose(out=banks[7][:, 0:128], in_=gatesT[:, i * 128:(i + 1) * 128],
                                                     identity=idf[:]), r=[gatesTB, constB], w=[bankB[7]])
                K.op(dve, lambda: nc.vector.tensor_copy(out=gates[:, i * 48:(i + 1) * 48], in_=banks[7][:, 0:48]),
                     r=[bankB[7]], w=[gatesB])
            def post_group(qg, hh, head, br, first, with_imp, ncol):
                pi = pctr[0] % 2
                pctr[0] += 1
                pv = pvs[:, pi * 4:(pi + 1) * 4, :]
                pB = pvsB[pi]
                for qt in range(4):
                    K.op(dve, lambda: nc.vector.tensor_copy(out=pv[:, qt, 0:ncol], in_=banks[2 + qt][:, 0:ncol]),
                         r=[bankB[2 + qt]], w=[pB])
                K.op(dve, lambda: nc.vector.tensor_scalar_max(out=sm[:, 0:4], in0=pv[:, :, 128], scalar1=1e-30),
                     r=[pB], w=[smB])
                K.op(dve, lambda: nc.vector.reciprocal(out=sm[:, 4:8], in_=sm[:, 0:4]), w=[smB])
                gv = gates[:].rearrange("p (t c) -> p t c", c=48)[:, qg * 4:(qg + 1) * 4, head * 3 + br]
                K.op(dve, lambda: nc.vector.tensor_tensor(out=sm[:, 8:12], in0=sm[:, 4:8], in1=gv, op=ALU.mult),
                     r=[gatesB], w=[smB])
                dst = oacc[:, qg * 4:(qg + 1) * 4, hh * 128:(hh + 1) * 128]
                wbc = sm[:, 8:12].unsqueeze(2).to_broadcast([128, 4, 128])
                if first:
                    K.op(dve, lambda: nc.vector.tensor_tensor(out=dst, in0=pv[:, :, 0:128], in1=wbc, op=ALU.mult),
                         r=[pB, smB], w=[oaccB])
                else:
                    K.op(dve, lambda: nc.vector.tensor_tensor(out=ptmp[:, :, 0:128], in0=pv[:, :, 0:128], in1=wbc, op=ALU.mult),
                         r=[pB, smB], w=[ptmpB])
                    K.op(dve, lambda: nc.vector.tensor_tensor(out=dst, in0=dst, in1=ptmp[:, :, 0:128], op=ALU.add),
                         r=[ptmpB], w=[oaccB])
                if with_imp:
                    idst = imp[:, qg * 4:(qg + 1) * 4, :]
                    rbc = sm[:, 4:8].unsqueeze(2).to_broadcast([128, 4, 32])
                    if hh == 0:
                        K.op(dve, lambda: nc.vector.tensor_tensor(out=idst, in0=pv[:, :, 129:161], in1=rbc, op=ALU.mult),
                             r=[pB, smB], w=[impB])
                    else:
                        K.op(dve, lambda: nc.vector.tensor_tensor(out=ptmp[:, :, 128:160], in0=pv[:, :, 129:161], in1=rbc,
                                                                  op=ALU.mult), r=[pB, smB], w=[ptmpB])
                        K.op(dve, lambda: nc.vector.tensor_tensor(out=idst, in0=idst, in1=ptmp[:, :, 128:160], op=ALU.add),
                             r=[ptmpB], w=[impB])

            SBK = [0, 1, 6, 7]
            NE = 8
            DEPTH = 4

            def attend(hh, head, qg, jts, kT, kB, R, RB, ncol, mask_of, br, first, with_imp, qrange_of=None):
                n = len(jts)
                rng = [qrange_of(jt) if qrange_of is not None else (0, 512) for jt in jts]
                first_ji, last_ji = {}, {}
                for ji, (lo, hi) in enumerate(rng):
                    for qt in range(lo // 128, hi // 128):
                        first_ji.setdefault(qt, ji)
                        last_ji[qt] = ji
                assert sorted(first_ji) == [0, 1, 2, 3]

                def front(ji):
                    jt = jts[ji]
                    lo, hi = rng[ji]
                    sb = SBK[sbctr[0] % 4]
                    sbctr[0] += 1
                    mk, mkB = mask_of(jt)
                    fns = [lambda: nc.tensor.matmul(banks[sb][:, lo:hi], kT(jt), G["qT"][:, hh, qg * 512 + lo:qg * 512 + hi],
                                                    start=True, stop=False),
                           lambda: nc.tensor.matmul(banks[sb][:, lo:hi], idb[:], mk[:, lo:hi], start=False, stop=True)]
                    K.group(pe, fns, r=[kB, G["qTB"], mkB, constB], w=[bankB[sb]])
                    ei = ectr[0] % NE
                    ectr[0] += 1
                    K.op(act, lambda: nc.scalar.activation(out=emt[:, ei, lo:hi], in_=banks[sb][:, lo:hi], func=AF.Exp,
                                                           scale=SCALE), r=[bankB[sb]], w=[emtB[ei]])
                    return ei

                eis = [front(ji) for ji in range(min(DEPTH, n))]
                for ji in range(n):
                    ei = eis[ji]
                    jt = jts[ji]
                    lo, hi = rng[ji]
                    qts = list(range(lo // 128, hi // 128))
                    fns = [lambda qt=qt, ei=ei, jt=jt, ji=ji: nc.tensor.matmul(
                        banks[2 + qt][:, 0:ncol], emt[:, ei, qt * 128:(qt + 1) * 128], R(jt),
                        start=(ji == first_ji[qt]), stop=(ji == last_ji[qt])) for qt in qts]
                    K.group(pe, fns, r=[emtB[ei], RB], w=[bankB[2 + qt] for qt in qts])
                    if ji + DEPTH < n:
                        eis.append(front(ji + DEPTH))
                post_group(qg, hh, head, br, first, with_imp, ncol)

            G = {}
            qTs = [qT[:], wslots[0][:].rearrange("p (h t) -> p h t", h=4)]
            qTBs = [qTB, Buf("qT1")]
            kalt = wslots[1][:].rearrange("p (a t) -> p a t", a=2)
            kaltB = [Buf(), Buf()]
            Rss = [Rs[:], wslots[2][:, 0:16 * 129].rearrange("p (j c) -> p j c", c=129)]
            Rws = [Rw[:], wslots[3][:, 0:16 * 129].rearrange("p (j c) -> p j c", c=129)]
            RsBs, RwBs = [RsB, Buf()], [RwB, Buf()]
            for jt in range(16):
                K.op(dve, lambda: nc.vector.tensor_copy(out=Rss[1][:, jt, 128:129], in_=kvals[:, jt:jt + 1]), r=[kvalB], w=[RsBs[1]])
                K.op(dve, lambda: nc.vector.memset(Rws[1][:, jt, 128:129], 1.0), w=[RwBs[1]])

            def kvsrc(par, kd):
                if par == 1 and kd in (2, 4):
                    return kalt[:, (kd - 2) // 2, :], kaltB[(kd - 2) // 2]
                return kvT[:, kd, :], kvB[kd]

            def loads(g):
                par = g % 2
                K.dma(sp, qTs[par], p_q[4 * g:4 * g + 4].rearrange("h p t -> p h t"), w=[qTBs[par]])
                for kd in range(6):
                    t_, b_ = kvsrc(par, kd)
                    K.dma(sp, t_, p_kv[kd * 4 + g], w=[b_])

            def compute_cb():
                for a, w1 in enumerate((w1k, w1v)):
                    for hc in range(2):
                        col = a * 2 + hc
                        fns = [lambda l=l, w1=w1, hc=hc, col=col, a=a: nc.tensor.matmul(
                            banks[7][:, col:col + 1], w1[:, l * 256 + hc * 128:l * 256 + hc * 128 + 128],
                            posb[:, a * 32 + l:a * 32 + l + 1], start=(l == 0), stop=(l == 31)) for l in range(32)]
                        K.group(pe, fns, r=grpW, w=[bankB[7]])
                        K.op(dve, lambda: nc.vector.tensor_copy(out=cb[:, col:col + 1], in_=banks[7][:, col:col + 1]),
                             r=[bankB[7]], w=[cbB])


            def stage1(g, between=None):
                par = g % 2
                for kd, R, RB in ((3, Rss[par], RsBs[par]), (5, Rws[par], RwBs[par])):
                    vsrc, vB = kvsrc(par, kd)
                    for jb in range(2):
                        pst = banks[6 + (jb % 2)][:].bitcast(BF16)
                        fns = [lambda j=j, jb=jb, vsrc=vsrc, pst=pst: nc.tensor.transpose(
                            out=pst[:, j * 128:(j + 1) * 128], in_=vsrc[:, (jb * 8 + j) * 128:(jb * 8 + j + 1) * 128],
                            identity=idb[:]) for j in range(8)]
                        K.group(pe, fns, r=[vB, constB], w=[bankB[6 + (jb % 2)]])
                        K.op(act, lambda: nc.scalar.copy(out=R[:, jb * 8:(jb + 1) * 8, 0:128],
                                                         in_=pst.rearrange("p (a b) -> p a b", a=8)),
                             r=[bankB[6 + (jb % 2)]], w=[RB])
                if between is not None:
                    between()
                for a, (kd, w1, w2) in enumerate(((0, w1k, w2k), (1, w1v, w2v))):
                    csrc, cBf = kvsrc(par, kd)
                    v3 = csrc.rearrange("p (n s) -> p n s", s=16)
                    for hc in range(2):
                        fns = []
                        for l in range(32):
                            rhs = v3[:, 0:127, l] if l < 16 else v3[:, 1:128, l - 16]
                            fns.append(lambda l=l, rhs=rhs, w1=w1, hc=hc: nc.tensor.matmul(
                                banks[7][:, 0:127], w1[:, l * 256 + hc * 128:l * 256 + hc * 128 + 128], rhs,
                                start=(l == 0), stop=(l == 31)))
                        K.group(pe, fns, r=[cBf] + grpW, w=[bankB[7]])
                        x_ = gx[:, 0, 0:127]
                        t_ = gx[:, 1, 0:127]
                        K.op(act, lambda: nc.scalar.activation(out=x_, in_=banks[7][:, 0:127], func=AF.Identity,
                                                               bias=cb[:, a * 2 + hc:a * 2 + hc + 1]), r=[bankB[7], cbB], w=[gxB])
                        K.op(dve, lambda: nc.vector.tensor_tensor(out=t_, in0=x_, in1=x_, op=ALU.mult), w=[gxB])
                        K.op(dve, lambda: nc.vector.tensor_scalar(out=t_, in0=t_, scalar1=0.044715, scalar2=1.0,
                                                                  op0=ALU.mult, op1=ALU.add), w=[gxB])
                        K.op(dve, lambda: nc.vector.tensor_tensor(out=t_, in0=t_, in1=x_, op=ALU.mult), w=[gxB])
                        K.op(act, lambda: nc.scalar.activation(out=t_, in_=t_, func=AF.Tanh, scale=0.7978845608028654), w=[gxB])
                        K.op(dve, lambda: nc.vector.tensor_scalar(out=t_, in0=t_, scalar1=1.0, scalar2=0.5,
                                                                  op0=ALU.add, op1=ALU.mult), w=[gxB])
                        K.op(dve, lambda: nc.vector.tensor_tensor(out=hid[:, a * 2 + hc, 0:127], in0=t_, in1=x_, op=ALU.mult),
                             r=[gxB], w=[hidB])
                    if a == 0:
                        fns = [lambda hc=hc: nc.tensor.matmul(banks[7][:, 128:255], w2k[:, hc * 128:(hc + 1) * 128],
                                                              hid[:, hc, 0:127], start=(hc == 0), stop=(hc == 1)) for hc in range(2)]
                        K.group(pe, fns, r=[hidB] + grpW, w=[bankB[7]])
                        K.op(act, lambda: nc.scalar.copy(out=kcmpT[:, 0:127], in_=banks[7][:, 128:255]), r=[bankB[7]], w=[kcmpB])
                    else:
                        fns = [lambda hc=hc: nc.tensor.matmul(banks[7][0:127, 256:384], hid[:, 2 + hc, 0:127],
                                                              w2v[:, hc * 128:(hc + 1) * 128], start=(hc == 0), stop=(hc == 1))
                               for hc in range(2)]
                        K.group(pe, fns, r=[hidB] + grpW, w=[bankB[7]])
                        K.op(act, lambda: nc.scalar.copy(out=Rc[0:127, 0:128], in_=banks[7][0:127, 256:384]), r=[bankB[7]], w=[RcB])

            stage1(0, between=compute_cb)
            K.op(dve, lambda: nc.vector.tensor_scalar(out=m8c[:], in0=m8s[:, 4:8, :], scalar1=-MASK_BIG, scalar2=None,
                                                      op0=ALU.add), r=[m8B], w=[m8cB])
            for g in range(4):
                par = g % 2
                G["qT"], G["qTB"] = qTs[par], qTBs[par]
                if g + 1 < 4:
                    loads(g + 1)
                for hh in range(4):
                    for qg in range(2):
                        attend(hh, 4 * g + hh, qg, [0], lambda jt: kcmpT[:, :], kcmpB, lambda jt: Rc[:, :], RcB, 161,
                               lambda jt: (cvs[:, qg * 512:(qg + 1) * 512], cvsB), 0, True, True)
                for i in range(NT):
                    sc = tk[:, 0, :]
                    sc2 = tk[:, 1, :]
                    K.op(dve, lambda: nc.vector.tensor_tensor(out=sc, in0=imp[:, i, :], in1=selvs[:, i * 32:(i + 1) * 32],
                                                              op=ALU.mult), r=[impB, selvB, selbB], w=[tkB])
                    K.op(dve, lambda: nc.vector.tensor_tensor(out=sc, in0=sc, in1=selbs[:, i * 32:(i + 1) * 32], op=ALU.add), w=[tkB])
                    K.op(dve, lambda: nc.vector.max(out=mx[:, 0:8], in_=sc), w=[tkB])
                    K.op(dve, lambda: nc.vector.match_replace(out=sc2, in_to_replace=mx[:, 0:8], in_values=sc, imm_value=-3.0e4), w=[tkB])
                    K.op(dve, lambda: nc.vector.max(out=mx[:, 8:16], in_=sc2), w=[tkB])
                    K.op(dve, lambda: nc.vector.tensor_scalar(out=selm8[:, i, :], in0=sc, scalar1=mx[:, 15:16], scalar2=None,
                                                              op0=ALU.is_ge), r=[tkB], w=[selmB])
                if g + 1 < 4:
                    stage1(g + 1)
                ksT, ksB = kvsrc(par, 2)
                kwT, kwB = kvsrc(par, 4)
                Rs_, RsB_, Rw_, RwB_ = Rss[par], RsBs[par], Rws[par], RwBs[par]
                for qg in range(2):
                    jt0 = 4 + 4 * qg
                    for hh in range(4):
                        attend(hh, 4 * g + hh, qg, list(range(jt0, jt0 + 8)), lambda jt: kwT[:, jt * 128:(jt + 1) * 128], kwB,
                               lambda jt: Rw_[:, jt, :], RwB_, 129, lambda jt: (m8s[:, jt - jt0, :], m8B), 2, False, False,
                               qrange_of=lambda jt: ((0, (jt - jt0 + 1) * 128) if jt - jt0 <= 3 else ((jt - jt0 - 4) * 128, 512)))
                for i in range(NT):
                    pst = banks[6][:].bitcast(BF16)
                    K.op(pe, lambda: nc.tensor.transpose(out=pst[0:32, 0:128], in_=selm8[:, i, :], identity=idb[:]),
                         r=[selmB, constB], w=[bankB[6]])
                    K.op(act, lambda: nc.scalar.copy(out=selT[:, i * 128:(i + 1) * 128], in_=pst[0:32, 0:128]), r=[bankB[6]], w=[selTB])
                for qg in range(2):
                    njt = 12 + 4 * qg
                    for jt in range(njt):
                        bk = 6 + (jt % 2)
                        K.op(pe, lambda: nc.tensor.matmul(banks[bk][:, 0:512], emats[:, jt * 128:(jt + 1) * 128],
                                                          selT[:, qg * 512:(qg + 1) * 512], start=True, stop=True),
                             r=[selTB, ematB], w=[bankB[bk]])
                        dl = jt * 128 - (1024 + qg * 512)
                        if dl >= 0:
                            K.op(dve, lambda: nc.vector.scalar_tensor_tensor(out=msk[:, jt, :], in0=banks[bk][:, 0:512],
                                                                             scalar=MASK_BIG, in1=m8c[:, dl // 128, :],
                                                                             op0=ALU.mult, op1=ALU.add),
                                 r=[bankB[bk], m8cB], w=[mskB])
                        else:
                            K.op(dve, lambda: nc.vector.tensor_scalar(out=msk[:, jt, :], in0=banks[bk][:, 0:512], scalar1=-1.0,
                                                                      scalar2=MASK_BIG, op0=ALU.add, op1=ALU.mult),
                                 r=[bankB[bk]], w=[mskB])
                    for hh in range(4):
                        attend(hh, 4 * g + hh, qg, list(range(njt)), lambda jt: ksT[:, jt * 128:(jt + 1) * 128], ksB,
                               lambda jt: Rs_[:, jt, :], RsB_, 129, lambda jt: (msk[:, jt, :], mskB), 1, False, False,
                               qrange_of=lambda jt: (max(0, jt * 128 - (1024 + qg * 512)), 512))
                K.op(dve, lambda: nc.vector.tensor_copy(out=obf[:, 0:4, :], in_=oacc[:, 0:4, :]), r=[oaccB], w=[obfB])
                K.op(act, lambda: nc.scalar.copy(out=obf[:, 4:8, :], in_=oacc[:, 4:8, :]), r=[oaccB], w=[obfB])
                for hh in range(4):
                    bk = 6 + (hh % 2)
                    pst = banks[bk][:].bitcast(BF16)
                    fns = [lambda i=i, hh=hh, pst=pst: nc.tensor.transpose(
                        out=pst[:, i * 128:(i + 1) * 128], in_=obf[:, i, hh * 128:(hh + 1) * 128], identity=idb[:]) for i in range(NT)]
                    K.group(pe, fns, r=[obfB, constB], w=[bankB[bk]])
                    K.op(act, lambda: nc.scalar.copy(out=ostg[:, hh % 2, :], in_=pst), r=[bankB[bk]], w=[ostgB[hh % 2]])
                    K.dma(sp, p_attn[4 * g + hh], ostg[:, hh % 2, :], r=[ostgB[hh % 2]])
            K.barrier()

    if lvl >= order.index("B"):
        phase_B()

    def phase_C():
        with ExitStack() as es:
            sbt = lambda n, s, d: es.enter_context(_sbuf_unique(n, s, d))
            yall = sbt("yall", [128, 16, T], F32)
            ubuf = sbt("ubuf", [128, 2, 1152], F32)
            sq = sbt("sq", [128, 2, T], F32)
            mean = sbt("mean", [128, T], F32)
            rstd = sbt("rstd", [128, T], F32)
            onesf = sbt("onesf", [128, 128], F32)
            cps = sbt("cps", [128, 16, 34], F32)
            cstg = sbt("cstg", [128, 2, T], BF16)
            yB = [Buf() for _ in range(16)]
            ubB, sqB, cstB = [Buf(), Buf()], [Buf(), Buf()], [Buf(), Buf()]
            cB, mB = Buf(), Buf()
            K.op(dve, lambda: nc.vector.memset(onesf[:], 1.0), w=[cB])
            K.dma(sp, cps[:], convp[:, :, :], w=[cB])
            for c in range(16):
                u_, uB_ = ubuf[:, c % 2, :], ubB[c % 2]
                K.dma(sp, u_, p_glu[c], w=[uB_])
                y_ = yall[:, c, :]
                K.op(dve, lambda: nc.vector.tensor_scalar(out=y_, in0=u_[:, 98:98 + T], scalar1=cps[:, c, 0:1],
                                                          scalar2=cps[:, c, 31:32], op0=ALU.mult, op1=ALU.add),
                     r=[uB_, cB], w=[yB[c]])
                for j in range(1, 31):
                    K.op(dve, lambda: nc.vector.scalar_tensor_tensor(out=y_, in0=u_[:, 98 + j:98 + j + T], scalar=cps[:, c, j:j + 1],
                                                                     in1=y_, op0=ALU.mult, op1=ALU.add), r=[uB_], w=[yB[c]])
                s_, sB_ = sq[:, c % 2, :], sqB[c % 2]
                K.op(act, lambda: nc.scalar.activation(out=s_, in_=y_, func=AF.Square), r=[yB[c]], w=[sB_])
                for tg in range(2):
                    K.op(pe, lambda: nc.tensor.matmul(banks[tg][:, 0:512], onesf[:], y_[:, tg * 512:(tg + 1) * 512],
                                                      start=(c == 0), stop=(c == 15)), r=[yB[c], cB], w=[bankB[tg]])
                    K.op(pe, lambda: nc.tensor.matmul(banks[2 + tg][:, 0:512], onesf[:], s_[:, tg * 512:(tg + 1) * 512],
                                                      start=(c == 0), stop=(c == 15)), r=[sB_, cB], w=[bankB[2 + tg]])
            for tg in range(2):
                sl = slice(tg * 512, (tg + 1) * 512)
                K.op(act, lambda: nc.scalar.activation(out=mean[:, sl], in_=banks[tg][:, 0:512], func=AF.Identity, scale=1.0 / 2048),
                     r=[bankB[tg]], w=[mB])
                K.op(dve, lambda: nc.vector.tensor_tensor(out=rstd[:, sl], in0=mean[:, sl], in1=mean[:, sl], op=ALU.mult), w=[mB])
                K.op(dve, lambda: nc.vector.scalar_tensor_tensor(out=rstd[:, sl], in0=banks[2 + tg][:, 0:512], scalar=1.0 / 2048,
                                                                 in1=rstd[:, sl], op0=ALU.mult, op1=ALU.subtract),
                     r=[bankB[2 + tg]], w=[mB])
                K.op(act, lambda: nc.scalar.activation(out=rstd[:, sl], in_=rstd[:, sl], func=AF.Sqrt, bias=epsn[:, 1:2]),
                     r=[constB], w=[mB])
                K.op(dve, lambda: nc.vector.reciprocal(out=rstd[:, sl], in_=rstd[:, sl]), w=[mB])
            for c in range(16):
                y_ = yall[:, c, :]
                if c % 2 == 0:
                    K.op(dve, lambda: nc.vector.tensor_tensor(out=y_, in0=y_, in1=mean[:], op=ALU.subtract), r=[mB], w=[yB[c]])
                else:
                    K.op(pool, lambda: nc.gpsimd.tensor_tensor(out=y_, in0=y_, in1=mean[:], op=ALU.subtract), r=[mB], w=[yB[c]])
                K.op(dve, lambda: nc.vector.tensor_tensor(out=y_, in0=y_, in1=rstd[:], op=ALU.mult), r=[mB], w=[yB[c]])
                K.op(act, lambda: nc.scalar.activation(out=cstg[:, c % 2, :], in_=y_, func=AF.Silu, scale=cps[:, c, 32:33],
                                                       bias=cps[:, c, 33:34]), r=[yB[c], cB], w=[cstB[c % 2]])
                K.dma(sp, p_conv[c], cstg[:, c % 2, :], r=[cstB[c % 2]])
            K.barrier()

    ssq = nc.alloc_sbuf_tensor("ssq", [128, NT, 8], F32)
    ssqB = Buf("ssq")

    def tm_proj(wsrc_of, lhsT, lhsB, nkb, evac, pre_cg=None):
        for cg in range(8):
            if pre_cg is not None:
                pre_cg(cg)
            for kb in range(nkb):
                wt, wb = load_w(wsrc_of(cg, kb))
                for tt in range(NT):
                    fns = [lambda k8=k8, tt=tt, wt=wt, kb=kb: nc.tensor.matmul(
                        banks[tt][:, 0:512], lhsT[:, kb * 8 + k8, tt * 128:(tt + 1) * 128], wt[:, k8 * 512:(k8 + 1) * 512],
                        start=(kb == 0 and k8 == 0), stop=(kb == nkb - 1 and k8 == 7)) for k8 in range(8)]
                    K.group(pe, fns, r=[wb, lhsB], w=[bankB[tt]])
            for tt in range(NT):
                evac(cg, tt, banks[tt], bankB[tt])

    def phase_DE():
        with ExitStack() as es:
            sbt = lambda n, s, d: es.enter_context(_sbuf_unique(n, s, d))
            mergedT = sbt("mergedT", [128, KC, T], BF16)
            mgB = Buf("merged")
            with ExitStack() as es2:
                sb2 = lambda n, s, d: es2.enter_context(_sbuf_unique(n, s, d))
                convT = sb2("convT", [128, 16, T], BF16)
                attnT = sb2("attnT", [128, 16, T], BF16)
                gat = sb2("gat", [128, 2, 2, T], BF16)
                tmp = sb2("tmpD", [128, 4, 512], F32)
                cvBs, atBs = [Buf() for _ in range(4)], [Buf() for _ in range(4)]
                gaB = [[Buf(), Buf()], [Buf(), Buf()]]
                tmpB = [Buf() for _ in range(4)]
                for c4 in range(4):
                    K.dma(sp, convT[:, c4 * 4:(c4 + 1) * 4, :], p_conv[c4 * 4:(c4 + 1) * 4].rearrange("c p t -> p c t"), w=[cvBs[c4]])
                    K.dma(sp, attnT[:, c4 * 4:(c4 + 1) * 4, :], p_attn[c4 * 4:(c4 + 1) * 4].rearrange("c p t -> p c t"), w=[atBs[c4]])
                tctr = 0
                for c in range(32):
                    K.dma(sp, gat[:, c % 2, 0, :], p_ga[c], w=[gaB[c % 2][0]])
                    K.dma(sp, gat[:, c % 2, 1, :], p_gb[c], w=[gaB[c % 2][1]])
                    ws = []
                    for src_ in (w_co_t[c], w_ao_t[c]):
                        i = wctr[0] % 4
                        wctr[0] += 1
                        K.dma(pool, wslots[i][:, 0:2048], src_, w=[wslotB[i]], max_dma_last_dim=8192)
                        ws.append((wslots[i], wslotB[i]))
                    for tg in range(2):
                        sl = slice(tg * 512, (tg + 1) * 512)
                        for a, (rhs, rB) in enumerate(((convT, cvBs), (attnT, atBs))):
                            bk = (tg * 2 + a) % 4
                            wt, wb = ws[a]
                            fns = [lambda kc=kc, wt=wt, rhs=rhs, bk=bk: nc.tensor.matmul(
                                banks[bk][:, 0:512], wt[:, kc * 128:(kc + 1) * 128], rhs[:, kc, sl],
                                start=(kc == 0), stop=(kc == 15)) for kc in range(16)]
                            K.group(pe, fns, r=[wb] + rB, w=[bankB[bk]])
                        t1, t2 = tctr % 4, (tctr + 1) % 4
                        tctr += 2
                        K.op(dve, lambda: nc.vector.tensor_tensor(out=tmp[:, t1, :], in0=banks[(tg * 2) % 4][:, 0:512],
                                                                  in1=gat[:, c % 2, 0, sl], op=ALU.mult),
                             r=[bankB[(tg * 2) % 4], gaB[c % 2][0]], w=[tmpB[t1]])
                        K.op(dve, lambda: nc.vector.tensor_tensor(out=tmp[:, t2, :], in0=banks[(tg * 2 + 1) % 4][:, 0:512],
                                                                  in1=gat[:, c % 2, 1, sl], op=ALU.mult),
                             r=[bankB[(tg * 2 + 1) % 4], gaB[c % 2][1]], w=[tmpB[t2]])
                        K.op(dve, lambda: nc.vector.tensor_tensor(out=mergedT[:, c, sl], in0=tmp[:, t1, :], in1=tmp[:, t2, :],
                                                                  op=ALU.add), r=[tmpB[t1], tmpB[t2]], w=[mgB])
                K.barrier()
            if lvl < order.index("E"):
                return
            with ExitStack() as es3:
                sb3 = lambda n, s, d: es3.enter_context(_sbuf_unique(n, s, d))
                zst = sb3("zst", [128, 4, 512], F32)
                junk = sb3("junk", [128, 2, 512], BF16)
                zB = [Buf() for _ in range(4)]
                jB = [Buf(), Buf()]
                zc = [0]

                def evacE(cg, tt, bank, bb):
                    i = zc[0] % 4
                    zc[0] += 1
                    K.op(dve, lambda: nc.vector.tensor_copy(out=zst[:, i, :], in_=bank[:, 0:512]), r=[bb], w=[zB[i]])
                    K.op(act, lambda: nc.scalar.activation(out=junk[:, i % 2, :], in_=zst[:, i, :], func=AF.Square,
                                                           accum_out=ssq[:, tt, cg:cg + 1]), r=[zB[i]], w=[jB[i % 2], ssqB])
                    K.dma(sp, p_z[tt * 128:(tt + 1) * 128, cg * 512:(cg + 1) * 512], zst[:, i, :], r=[zB[i]])

                tm_proj(lambda cg, kb: w_out_t[cg, kb], mergedT, mgB, 4, evacE)
                K.barrier()

    def make_pre(zsrc, gP, gPb, zts, ztBs, st2, st2B, store_to):
        def pre(i, x_t, x_b):
            zt, ztB = zts[i % len(zts)], ztBs[i % len(zts)]
            s2 = st2[:, (i % 2) * 4:(i % 2) * 4 + 4]
            s2B = st2B[i % 2]
            K.dma(sp, zt[:], zsrc[i * 128:(i + 1) * 128, :], w=[ztB])
            K.op(dve, lambda: nc.vector.tensor_reduce(out=s2[:, 0:1], in_=ssq[:, i, :], axis=AX.X, op=ALU.add),
                 r=[ssqB], w=[s2B])
            K.op(act, lambda: nc.scalar.activation(out=s2[:, 1:2], in_=s2[:, 0:1], func=AF.Sqrt, scale=1.0 / D,
                                                   bias=epsn[:, 0:1]), r=[constB], w=[s2B])
            K.op(dve, lambda: nc.vector.reciprocal(out=s2[:, 2:3], in_=s2[:, 1:2]), w=[s2B])
            K.op(dve, lambda: nc.vector.scalar_tensor_tensor(out=zt[:], in0=zt[:], scalar=s2[:, 2:3], in1=gP, op0=ALU.mult,
                                                             op1=ALU.mult), r=[s2B, gPb], w=[ztB])
            K.op(dve, lambda: nc.vector.tensor_tensor(out=x_t[:], in0=x_t[:], in1=zt[:], op=ALU.add), r=[ztB], w=[x_b])
            if store_to is not None:
                K.dma(pool, store_to[i * 128:(i + 1) * 128, :], x_t[:], r=[x_b])
        return pre

    def phase_F():
        with ExitStack() as es:
            sbt = lambda n, s, d: es.enter_context(_sbuf_unique(n, s, d))
            hT = sbt("hT", [128, KC, T], BF16)
            hTB = [Buf(f"hT{i}") for i in range(4)]
            with ExitStack() as es2:
                sb2 = lambda n, s, d: es2.enter_context(_sbuf_unique(n, s, d))
                xt0, xt1 = sb2("xt0", [128, D], F32), sb2("xt1", [128, D], F32)
                zt = sb2("zt", [128, D], F32)
                gP, gM = sb2("gP", [128, D], F32), sb2("gM", [128, D], F32)
                ub = sb2("ub", [128, D], BF16)
                ubx = sb2("ubx", [128, D], BF16)
                st, st2 = sb2("st", [128, 8], F32), sb2("st2", [128, 8], F32)
                gPb, gMb, ubB, ztB, st2B = Buf(), Buf(), Buf(), Buf(), [Buf(), Buf()]
                K.dma(sp, gP[:], gains[1:2, :].to_broadcast([128, D]), w=[gPb])
                K.dma(sp, gM[:], gains[2:3, :].to_broadcast([128, D]), w=[gMb])
                pre = make_pre(p_z, gP[:], gPb, [zt], [ztB], st2, st2B, p_x1)
                norm_T(xo, gM[:], gMb, hT, hTB, NT, [xt0, xt1], [Buf(), Buf()], [ub, ubx], [ubB, Buf()], st, [Buf(), Buf()], pre=pre)
                K.barrier()
            with ExitStack() as es3:
                sb3 = lambda n, s, d: es3.enter_context(_sbuf_unique(n, s, d))
                actq = sb3("actq", [128, 32, T], BF16)
                rl = sb3("rl", [128, 4, 512], F32)
                part = sb3("part", [128, NT, 512], F32)
                ost = sb3("ost", [128, 4, 512], F32)
                junk = sb3("junk2", [128, 2, 512], BF16)
                aqB = Buf("actq")
                rlB = [Buf() for _ in range(4)]
                partB = [Buf() for _ in range(NT)]
                ostB = [Buf() for _ in range(4)]
                jB = [Buf(), Buf()]
                accB = [[Buf() for _ in range(8)] for _ in range(NT)]
                ctr = [0, 0]
                for qf in range(4):
                    def evacF(ci, gi, bank, bb, lo, n):
                        i = ctr[0] % 4
                        ctr[0] += 1
                        K.op(act, lambda: nc.scalar.activation(out=rl[:, i, :], in_=bank[:, 0:512], func=AF.Relu), r=[bb], w=[rlB[i]])
                        K.op(dve, lambda: nc.vector.tensor_tensor(out=actq[:, ci, lo:lo + n], in0=rl[:, i, :], in1=rl[:, i, :],
                                                                  op=ALU.mult), r=[rlB[i]], w=[aqB])
                    proj_fm([w_up_t[qf * 32 + f] for f in range(32)], hT, hTB, KC, [(0, 512), (512, 512)], evacF)

                    def evacG(cg, tt, bank, bb, qf=qf):
                        i = ctr[1] % 4
                        ctr[1] += 1
                        dst = p_acc[tt * 128:(tt + 1) * 128, cg * 512:(cg + 1) * 512]
                        if qf == 0:
                            K.op(dve, lambda: nc.vector.tensor_copy(out=ost[:, i, :], in_=bank[:, 0:512]), r=[bb], w=[ostB[i]])
                        else:
                            K.op(dve, lambda: nc.vector.tensor_tensor(out=ost[:, i, :], in0=bank[:, 0:512], in1=part[:, tt, :],
                                                                      op=ALU.add), r=[bb, partB[tt]], w=[ostB[i]])
                        if qf == 3:
                            K.op(act, lambda: nc.scalar.activation(out=junk[:, i % 2, :], in_=ost[:, i, :], func=AF.Square,
                                                                   accum_out=ssq[:, tt, cg:cg + 1]), r=[ostB[i]], w=[jB[i % 2], ssqB])
                        K.dma(sp, dst, ost[:, i, :], r=[ostB[i]], w=[accB[tt][cg]])
                    def preG(cg, qf=qf):
                        if qf > 0:
                            for tt in range(NT):
                                K.dma(sp, part[:, tt, :], p_acc[tt * 128:(tt + 1) * 128, cg * 512:(cg + 1) * 512],
                                      r=[accB[tt][cg]], w=[partB[tt]])
                    tm_proj(lambda cg, kb, qf=qf: w_dn_t[qf, cg, kb], actq, aqB, 4, evacG, pre_cg=preG)
                K.barrier()
        with ExitStack() as es4:
            sb4 = lambda n, s, d: es4.enter_context(_sbuf_unique(n, s, d))
            xt = [sb4("xf0", [128, D], F32), sb4("xf1", [128, D], F32), sb4("xf2", [128, D], F32)]
            zts = [sb4("zf0", [128, D], F32), sb4("zf1", [128, D], F32), sb4("zf2", [128, D], F32)]
            gP = sb4("gF", [128, D], F32)
            st2 = sb4("stf", [128, 8], F32)
            gPb, ztBs, st2B = Buf(), [Buf(), Buf(), Buf()], [Buf(), Buf()]
            xB = [Buf(), Buf(), Buf()]
            K.dma(sp, gP[:], gains[3:4, :].to_broadcast([128, D]), w=[gPb])
            pre = make_pre(p_acc, gP[:], gPb, zts, ztBs, st2, st2B, out)
            for i in range(NT):
                K.dma(sp, xt[i % 3][:], p_x1[i * 128:(i + 1) * 128, :], w=[xB[i % 3]])
                pre(i, xt[i % 3], xB[i % 3])
            K.barrier()

    if lvl >= order.index("D"):
        phase_DE()
    if lvl >= order.index("F"):
        phase_F()

    K.barrier()
    nc.all_engine_barrier()
    cm.__exit__(None, None, None)
    return nc


def _col_index():
    idx = []
    for i in range(16):
        idx += list(range(128 * i, 128 * i + 128))
        idx += list(range(2048 + 128 * i, 2048 + 128 * i + 128))
    idx += list(range(4096, 6144))
    idx += list(range(6144, 9216))
    idx += list(range(9264, 13360))
    idx += list(range(13360, 17456))
    idx += list(range(9216, 9264))
    return np.asarray(idx, dtype=np.int64)


def _tile_cols(w, ncols_chunk=128):
    Kd, N = w.shape
    a = w.reshape(Kd // 128, 128, N // 128, 128).transpose(2, 1, 0, 3)
    return np.ascontiguousarray(a).reshape(N // 128, 128, (Kd // 128) * 128)


def _shared_inputs(inp):
    f32 = np.float32
    w_in = np.asarray(inp["w_in"], f32)[0]
    wp = np.zeros((D, NCH_IN * 128), f32)
    ci = _col_index()
    wp[:, :ci.size] = w_in[:, ci]
    sh = {}
    sh["w_in_t"] = _tile_cols(wp)
    del wp
    sh["w_co_t"] = _tile_cols(np.asarray(inp["w_conv_out"], f32)[0])
    sh["w_ao_t"] = _tile_cols(np.asarray(inp["w_attn_out"], f32)[0])
    w_out = np.asarray(inp["w_out"], f32)[0]
    sh["w_out_t"] = np.ascontiguousarray(
        w_out.reshape(4, 8, 128, 8, 512).transpose(3, 0, 2, 1, 4)).reshape(8, 4, 128, D)
    sh["w_up_t"] = _tile_cols(np.asarray(inp["w_up"], f32)[0])
    w_dn = np.asarray(inp["w_down"], f32)[0]
    sh["w_dn_t"] = np.ascontiguousarray(
        w_dn.reshape(4, 4, 8, 128, 8, 512).transpose(0, 4, 1, 3, 2, 5)).reshape(4, 8, 4, 128, D)
    sh["gains"] = np.ascontiguousarray(np.stack([np.asarray(inp[k], f32)[0] for k in
                                                 ("norm_mix_pre", "norm_mix_post", "norm_mlp_pre", "norm_mlp_post")]))
    cp = np.zeros((128, 16, 34), f32)
    cp[:, :, 0:31] = np.asarray(inp["w_dw"], f32)[0, :, 0, :].reshape(31, 16, 128).transpose(2, 1, 0)
    cp[:, :, 31] = np.asarray(inp["b_dw"], f32)[0].reshape(16, 128).T
    cp[:, :, 32] = np.asarray(inp["ln_conv_g"], f32)[0].reshape(16, 128).T
    cp[:, :, 33] = np.asarray(inp["ln_conv_b"], f32)[0].reshape(16, 128).T
    sh["convp"] = cp
    for nm, k1, k2 in (("k", "w_cmp_k1", "w_cmp_k2"), ("v", "w_cmp_v1", "w_cmp_v2")):
        w1 = np.asarray(inp[k1], f32)[0]
        sh[f"w1{nm}_t"] = np.ascontiguousarray(w1.reshape(32, 128, 256).transpose(1, 0, 2)).reshape(128, 8192)
        w2 = np.asarray(inp[k2], f32)[0]
        sh[f"w2{nm}_t"] = np.ascontiguousarray(w2.reshape(2, 128, 128).transpose(1, 0, 2)).reshape(128, 256)
    sh["posT"] = np.ascontiguousarray(np.concatenate([np.asarray(inp["pos_cmp_k"], f32)[0].T,
                                                      np.asarray(inp["pos_cmp_v"], f32)[0].T], axis=1))
    p = np.arange(128)[:, None, None]
    dl = (np.arange(8)[None, :, None] - 4) * 128
    qq = np.arange(512)[None, None, :]
    j = dl + p
    sh["m8"] = ((((j <= qq) & (j > qq - 512)).astype(np.float32) - 1.0) * MASK_BIG).astype(NPBF).reshape(128, 8 * 512)
    n = np.arange(128)[:, None]
    m = np.arange(32)[None, :]
    rc = np.zeros((128, 33), np.float32)
    rc[:127, 0] = 1.0
    rc[:, 1:] = ((n >= 4 * m - 1) & (n <= 4 * m + 3) & (n < 127))
    sh["rcc"] = rc.astype(NPBF)
    sh["emat"] = (np.arange(2048)[None, :] // 64 == np.arange(32)[:, None]).astype(NPBF)
    sh["identb"] = np.eye(128, dtype=np.float32).astype(NPBF)
    sh["identf"] = np.eye(128, dtype=np.float32)
    return sh


def _half_inputs(h):
    t = np.arange(T)
    t_real = t + 1024 * h
    n = np.arange(128)[:, None]
    cv = (16 * n + 31 <= t[None, :] + 1024) & (n <= 126)
    if h == 0:
        cv &= (n >= 64)
    m = np.arange(32)[None, :]
    jr = m - 16 * (1 - h)
    tr = t_real[:, None]
    valid = (jr >= 0) & (64 * jr <= tr)
    cur = tr // 64
    forced = (jr == 0) | (jr == cur) | (jr == cur - 1)
    sv = (valid & ~forced).astype(np.float32)
    sb = np.where(forced & valid, 1e4, np.where(valid, 0.0, -1e4)).astype(np.float32)
    lay = lambda a: np.ascontiguousarray(a.reshape(NT, 128, 32).transpose(1, 0, 2)).reshape(128, NT * 32)
    kv = np.ones((128, 16), np.float32)
    if h == 0:
        kv[:, :8] = 0.0
    cvb = ((cv.astype(np.float32) - 1.0) * MASK_BIG).astype(NPBF)
    return {"cvis": cvb, "selv": lay(sv), "selb": lay(sb), "kval": kv.astype(NPBF)}


def make_in_maps(inp):
    sh = _shared_inputs(inp)
    x = np.asarray(inp["x"], np.float32)
    halves = [_half_inputs(0), _half_inputs(1)]
    zeros = np.zeros((T, D), np.float32)
    maps = []
    for c in range(8):
        b, h = c // 2, c % 2
        mp = dict(sh)
        mp.update(halves[h])
        mp["xo"] = np.ascontiguousarray(x[b, h * T:(h + 1) * T])
        mp["xc"] = np.ascontiguousarray(x[b, 0:T]) if h == 1 else zeros
        maps.append(mp)
    return maps


def kernel(**inputs):
    maps = make_in_maps(inputs)
    nc = build_nc()
    res = run_bass_kernel_spmd(nc, maps, core_ids=list(range(8)))
    outp = np.zeros((4, 2 * T, D), np.float32)
    for c in range(8):
        b, h = c // 2, c % 2
        outp[b, h * T:(h + 1) * T] = np.asarray(res.results[c]["out"], np.float32)
    return outp
```

```python
from contextlib import ExitStack
import numpy as np
import ml_dtypes
import concourse.bass as bass
import concourse.mybir as mybir
from concourse.bass_utils import run_bass_kernel_spmd

F32 = mybir.dt.float32
BF16 = mybir.dt.bfloat16
AF = mybir.ActivationFunctionType
ALU = mybir.AluOpType
AX = mybir.AxisListType
NPBF = ml_dtypes.bfloat16

D = 4096
T = 1024
NT = 8
KC = 32
DFF = 16384
NCH_IN = 137
NORM_EPS = 1e-6
LN_EPS = 1e-5
SCALE = 128 ** -0.5
MASK_BIG = 32768.0
CH_GLU = 0
CH_Q = 32
CH_KV = 48
CH_GA = 72
CH_GB = 104
CH_NG = 136


class Sem:
    def __init__(self, nc, name):
        self.h = nc.alloc_semaphore(name)
        self.n = 0


class Buf:
    __slots__ = ("name", "w", "rs")

    def __init__(self, name=""):
        self.name = name
        self.w = None
        self.rs = {}


class Q:
    def __init__(self, nc, eng, name):
        self.eng = eng
        self.name = name
        self.sem = Sem(nc, "s_" + name)
        self.seen = {}

    def wait(self, t):
        if t is None:
            return
        s, v = t
        if self.seen.get(id(s), 0) >= v:
            return
        self.eng.wait_ge(s.h, v)
        self.seen[id(s)] = v


class KB:
    def __init__(self, nc):
        self.nc = nc
        self.pe = Q(nc, nc.tensor, "pe")
        self.act = Q(nc, nc.scalar, "act")
        self.dve = Q(nc, nc.vector, "dve")
        self.pool = Q(nc, nc.gpsimd, "pool")
        self.sp = Q(nc, nc.sync, "sp")
        self.qs = [self.pe, self.act, self.dve, self.pool, self.sp]
        self.dsems = {id(self.sp): [Sem(nc, f"dsp{i}") for i in range(24)],
                      id(self.pool): [Sem(nc, f"dpl{i}") for i in range(24)],
                      id(self.act): [Sem(nc, f"dac{i}") for i in range(8)]}
        self.dnext = {k: 0 for k in self.dsems}

    def _deps(self, q, r, w):
        for b in r:
            q.wait(b.w)
        for b in w:
            q.wait(b.w)
            for t in b.rs.values():
                q.wait(t)

    def _commit(self, t, r, w):
        for b in r:
            b.rs[id(t[0])] = t
        for b in w:
            b.w = t
            b.rs = {}

    def op(self, q, fn, r=(), w=()):
        self._deps(q, r, w)
        inst = fn()
        q.sem.n += 1
        inst.then_inc(q.sem.h, 1)
        t = (q.sem, q.sem.n)
        self._commit(t, r, w)
        return t

    def group(self, q, fns, r=(), w=()):
        self._deps(q, r, w)
        inst = None
        for fn in fns:
            inst = fn()
        q.sem.n += 1
        inst.then_inc(q.sem.h, 1)
        t = (q.sem, q.sem.n)
        self._commit(t, r, w)
        return t

    def dma(self, q, out, in_, r=(), w=(), **kw):
        self._deps(q, r, w)
        ring = self.dsems[id(q)]
        i = self.dnext[id(q)]
        self.dnext[id(q)] = (i + 1) % len(ring)
        s = ring[i]
        if s.n:
            q.wait((s, s.n))
        inst = q.eng.dma_start(out=out, in_=in_, **kw)
        s.n += 16
        inst.then_inc(s.h, 16)
        t = (s, s.n)
        self._commit(t, r, w)
        return t

    def barrier(self):
        ts = [(q.sem, q.sem.n) for q in self.qs if q.sem.n]
        for ring in self.dsems.values():
            ts += [(s, s.n) for s in ring if s.n]
        for q in self.qs:
            for t in ts:
                q.wait(t)


def build_nc(debug=False, upto="all"):
    nc = bass.Bass("TRN2", target_bir_lowering=False)
    order = ["A", "B", "C", "D", "E", "F", "all"]
    lvl = order.index(upto)

    def din(name, shape, dt=F32):
        return nc.dram_tensor(name, list(shape), dt, kind="ExternalInput").ap()

    def dscr(name, shape, dt=F32):
        kind = "ExternalOutput" if debug else "Internal"
        return nc.dram_tensor(name, list(shape), dt, kind=kind).ap()

    xo = din("xo", [T, D])
    xc = din("xc", [T, D])
    w_in_t = din("w_in_t", [NCH_IN, 128, D])
    w_co_t = din("w_co_t", [32, 128, 16 * 128])
    w_ao_t = din("w_ao_t", [32, 128, 16 * 128])
    w_out_t = din("w_out_t", [8, 4, 128, D])
    w_up_t = din("w_up_t", [128, 128, D])
    w_dn_t = din("w_dn_t", [4, 8, 4, 128, D])
    gains = din("gains", [4, D])
    convp = din("convp", [128, 16, 34])
    w1k_t = din("w1k_t", [128, 32 * 256])
    w1v_t = din("w1v_t", [128, 32 * 256])
    w2k_t = din("w2k_t", [128, 2 * 128])
    w2v_t = din("w2v_t", [128, 2 * 128])
    posT = din("posT", [128, 64])
    m8 = din("m8", [128, 8 * 512], BF16)
    cvis = din("cvis", [128, T], BF16)
    selv = din("selv", [128, NT * 32])
    selb = din("selb", [128, NT * 32])
    rcc = din("rcc", [128, 33], BF16)
    kval = din("kval", [128, 16], BF16)
    emat = din("emat", [32, 2048], BF16)
    identb = din("identb", [128, 128], BF16)
    identf = din("identf", [128, 128])
    out = nc.dram_tensor("out", [T, D], F32, kind="ExternalOutput").ap()

    p_glu = dscr("p_glu", [16, 128, 1152])
    p_q = dscr("p_q", [16, 128, T], BF16)
    p_kv = dscr("p_kv", [24, 128, 2048], BF16)
    p_ga = dscr("p_ga", [32, 128, T], BF16)
    p_gb = dscr("p_gb", [32, 128, T], BF16)
    p_attn = dscr("p_attn", [16, 128, T], BF16)
    p_conv = dscr("p_conv", [16, 128, T], BF16)
    p_z = dscr("p_z", [T, D])
    p_x1 = dscr("p_x1", [T, D])
    p_acc = dscr("p_acc", [T, D])
    p_y = dscr("p_y", [16, 128, T])

    cm = nc.cleanup_on_exit()
    cm.__enter__()
    _orig_sbuf_tensor = nc.sbuf_tensor
    _uid = [0]

    def _sbuf_unique(name, shape, dt):
        _uid[0] += 1
        return _orig_sbuf_tensor(f"{name}_{_uid[0]}", shape, dt)
    K = KB(nc)
    pe, act, dve, pool, sp = K.pe, K.act, K.dve, K.pool, K.sp

    banks = [nc.alloc_psum_tensor(f"bank{i}", [128, 512], F32) for i in range(8)]
    bankB = [Buf(f"bank{i}") for i in range(8)]
    wslots = [nc.alloc_sbuf_tensor(f"wslot{i}", [128, D], BF16) for i in range(4)]
    wslotB = [Buf(f"wslot{i}") for i in range(4)]
    wctr = [0]
    idb = nc.alloc_sbuf_tensor("idb", [128, 128], BF16)
    idf = nc.alloc_sbuf_tensor("idf", [128, 128], F32)
    gatesT = nc.alloc_sbuf_tensor("gatesT", [128, T], F32)
    gates = nc.alloc_sbuf_tensor("gates", [128, NT * 48], F32)
    constB = Buf("const")
    gatesTB = Buf("gatesT")
    gatesB = Buf("gates")
    K.dma(sp, idb[:], identb[:, :], w=[constB])
    K.dma(sp, idf[:], identf[:, :], w=[constB])

    def load_w(src):
        i = wctr[0] % 4
        wctr[0] += 1
        K.dma(pool, wslots[i][:], src, w=[wslotB[i]], max_dma_last_dim=8192)
        return wslots[i], wslotB[i]

    def norm_T(src, gB, gBb, dstT, dstB, ntiles, xt, xtB, ubs, ubBs, st, stB, pre=None):
        for i in range(ntiles):
            x_t, x_b = xt[i % 2], xtB[i % 2]
            K.dma(sp, x_t[:], src[i * 128:(i + 1) * 128, :], w=[x_b])
            if pre is not None:
                pre(i, x_t, x_b)
            s_t = st[:, (i % 2) * 4:(i % 2) * 4 + 4]
            s_b = stB[i % 2]
            ub, ubB = ubs[i % 2], ubBs[i % 2]
            K.op(act, lambda: nc.scalar.activation(out=ub[:], in_=x_t[:], func=AF.Square,
                                                   accum_out=s_t[:, 0:1]), r=[x_b], w=[ubB, s_b])
            K.op(act, lambda: nc.scalar.activation(out=s_t[:, 1:2], in_=s_t[:, 0:1], func=AF.Sqrt,
                                                   scale=1.0 / D, bias=epsn[:, 0:1]), r=[constB], w=[s_b])
            K.op(dve, lambda: nc.vector.reciprocal(out=s_t[:, 2:3], in_=s_t[:, 1:2]), w=[s_b])
            K.op(dve, lambda: nc.vector.scalar_tensor_tensor(out=ub[:], in0=x_t[:], scalar=s_t[:, 2:3],
                                                             op0=ALU.mult, in1=gB, op1=ALU.mult),
                 r=[x_b, s_b, gBb], w=[ubB])
            for g in range(4):
                bk = 4 + g
                pst = banks[bk][:].bitcast(BF16)
                fns = []
                for j in range(8):
                    kc = g * 8 + j
                    fns.append(lambda kc=kc, j=j, pst=pst: nc.tensor.transpose(
                        out=pst[:, j * 128:(j + 1) * 128], in_=ub[:, kc * 128:(kc + 1) * 128], identity=idb[:]))
                K.group(pe, fns, r=[ubB, constB], w=[bankB[bk]])
                dst = dstT[:, g * 8:(g + 1) * 8, i * 128:(i + 1) * 128]
                srcv = pst.rearrange("p (a b) -> p a b", a=8)
                dB_ = dstB[g] if isinstance(dstB, list) else dstB
                if g % 2 == 0:
                    K.op(dve, lambda: nc.vector.tensor_copy(out=dst, in_=srcv), r=[bankB[bk]], w=[dB_])
                else:
                    K.op(act, lambda: nc.scalar.copy(out=dst, in_=srcv), r=[bankB[bk]], w=[dB_])

    def proj_fm(wsrcs, rhsT, rhsB, nkc, tok_groups, evac, bank0=0, after_chunk=None):
        cnt = 0
        for ci, src in enumerate(wsrcs):
            if nkc == 32:
                wt, wb = load_w(src)
            else:
                i = wctr[0] % 4
                wctr[0] += 1
                K.dma(pool, wslots[i][:, 0:nkc * 128], src, w=[wslotB[i]], max_dma_last_dim=8192)
                wt, wb = wslots[i], wslotB[i]
            for gi, grp in enumerate(tok_groups):
                lo, n = grp[0], grp[1]
                rT, rB = (grp[2], grp[3]) if len(grp) == 4 else (rhsT, rhsB)
                bk = bank0 + (cnt % 4)
                cnt += 1
                fns = []
                for kc in range(nkc):
                    fns.append(lambda kc=kc, bk=bk, wt=wt, lo=lo, n=n, rT=rT: nc.tensor.matmul(
                        banks[bk][:, 0:n], wt[:, kc * 128:(kc + 1) * 128], rT[:, kc, lo:lo + n],
                        start=(kc == 0), stop=(kc == nkc - 1)))
                K.group(pe, fns, r=[wb] + (rB if isinstance(rB, list) else [rB]), w=[bankB[bk]])
                evac(ci, gi, banks[bk], bankB[bk], lo, n)
            if after_chunk is not None:
                after_chunk()

    epsn = nc.alloc_sbuf_tensor("epsn", [128, 2], F32)
    K.op(dve, lambda: nc.vector.memset(epsn[:, 0:1], NORM_EPS), w=[constB])
    K.op(dve, lambda: nc.vector.memset(epsn[:, 1:2], LN_EPS), w=[constB])

    onesf = nc.alloc_sbuf_tensor("onesf", [128, 128], F32)
    cps = nc.alloc_sbuf_tensor("cps", [128, 16, 34], F32)
    ccB = Buf("convconst")
    K.op(dve, lambda: nc.vector.memset(onesf[:], 1.0), w=[ccB])
    K.dma(sp, cps[:], convp[:, :, :], w=[ccB])
    stg = None
    with _sbuf_unique("uT", [128, KC, T], BF16) as uT, \
            _sbuf_unique("xt0", [128, D], F32) as xt0, _sbuf_unique("xt1", [128, D], F32) as xt1, \
            _sbuf_unique("gB", [128, D], F32) as gB, _sbuf_unique("ub", [128, D], BF16) as ub, \
            _sbuf_unique("ubx", [128, D], BF16) as ubx, \
            _sbuf_unique("st", [128, 8], F32) as st, \
            _sbuf_unique("sgb", [128, 4, 512], BF16) as sgb, \
            _sbuf_unique("sgf", [128, 6, 512], F32) as sgf, \
            _sbuf_unique("utail", [128, KC, 128], BF16) as utail:
        uTB, gBb, ubB = [Buf(f"uT{i}") for i in range(4)], Buf("gB"), Buf("ub")
        ubs, ubBs = [ub, ubx], [ubB, Buf("ubx")]
        xtB = [Buf("xt0"), Buf("xt1")]
        stB = [Buf("st0"), Buf("st1")]
        sgbB = [Buf(f"sgb{i}") for i in range(4)]
        sgfB = [Buf(f"sgf{i}") for i in range(6)]
        utailB = Buf("utail")
        sctr = [0, 0]
        act_only = [False]
        gluB = [Buf(f"glu{c}") for c in range(16)]
        K.dma(sp, gB[:], gains[0:1, :].to_broadcast([128, D]), w=[gBb])

        def evac_store(dst_of, func):
            def evac(ci, gi, bank, bb, lo, n):
                i = sctr[0] % 4
                sctr[0] += 1
                if func is None and (sctr[0] % 2 == 0) and not act_only[0]:
                    K.op(dve, lambda: nc.vector.tensor_copy(out=sgb[:, i, 0:n], in_=bank[:, 0:n]), r=[bb], w=[sgbB[i]])
                else:
                    f = func if func is not None else AF.Identity
                    K.op(act, lambda: nc.scalar.activation(out=sgb[:, i, 0:n], in_=bank[:, 0:n], func=f),
                         r=[bb], w=[sgbB[i]])
                K.dma(sp, dst_of(ci, lo, n), sgb[:, i, 0:n], r=[sgbB[i]])
            return evac

        def evac_glu(tok_off_of):
            hold = {}

            def evac(ci, gi, bank, bb, lo, n):
                tok_off = tok_off_of(gi)
                i = sctr[1] % 6
                sctr[1] += 1
                if ci % 2 == 0:
                    K.op(act, lambda: nc.scalar.activation(out=sgf[:, i, 0:n], in_=bank[:, 0:n], func=AF.Identity),
                         r=[bb], w=[sgfB[i]])
                    hold[gi] = i
                else:
                    ia = hold[gi]
                    K.op(act, lambda: nc.scalar.activation(out=sgf[:, i, 0:n], in_=bank[:, 0:n], func=AF.Sigmoid),
                         r=[bb], w=[sgfB[i]])
                    K.op(dve, lambda: nc.vector.tensor_tensor(out=sgf[:, i, 0:n], in0=sgf[:, i, 0:n], in1=sgf[:, ia, 0:n],
                                                              op=ALU.mult), r=[sgfB[ia]], w=[sgfB[i]])
                    K.dma(sp, p_glu[ci // 2, :, tok_off + lo:tok_off + lo + n], sgf[:, i, 0:n], r=[sgfB[i]], w=[gluB[ci // 2]])
            return evac

        def evac_gates(ci, gi, bank, bb, lo, n):
            K.op(act, lambda: nc.scalar.activation(out=gatesT[:, lo:lo + n], in_=bank[:, 0:n], func=AF.Sigmoid),
                 r=[bb], w=[gatesTB])

        tg2 = [(0, 512), (512, 512)]
        xts = [xt0, xt1]
        norm_T(xc, gB[:], gBb, uT, uTB, NT, xts, xtB, ubs, ubBs, st, stB)
        proj_fm([w_in_t[CH_KV + j] for j in range(16)], uT, uTB, KC, tg2,
                evac_store(lambda ci, lo, n: p_kv[ci, :, lo:lo + n], None))
        proj_fm([w_in_t[CH_KV + 16 + j] for j in range(8)], uT, uTB, KC, [(512, 512)],
                evac_store(lambda ci, lo, n: p_kv[16 + ci, :, lo:lo + n], None))
        K.op(dve, lambda: nc.vector.tensor_copy(out=utail[:], in_=uT[:, :, 896:1024]), r=uTB, w=[utailB])
        norm_T(xo, gB[:], gBb, uT, uTB, NT, xts, xtB, ubs, ubBs, st, stB)
        proj_fm([w_in_t[CH_GLU + j] for j in range(32)], uT, uTB, KC, tg2 + [(0, 128, utail, utailB)],
                evac_glu(lambda gi: 128 if gi < 2 else 0))
        def inherit(olds):
            b = Buf()
            for o in olds:
                for t in list(o.rs.values()) + ([o.w] if o.w is not None else []):
                    k = id(t[0])
                    if k not in b.rs or b.rs[k][1] < t[1]:
                        b.rs[k] = t
            return b


        def conv_gen():
            ubuf = xt0[:, 0:2304].rearrange("p (a b) -> p a b", a=2)
            ybuf = xt1[:, 0:3072].rearrange("p (a b) -> p a b", a=3)
            sqb = sgf[:, 0:4, :].rearrange("p a b -> p (a b)").rearrange("p (a b) -> p a b", a=2)
            mean = gB[:, 0:T]
            rstd = gB[:, T:2 * T]
            cstg = ub[:, 0:2 * T].rearrange("p (a b) -> p a b", a=2)
            ubB_ = [inherit([xtB[0]]), inherit([xtB[0]])]
            yB_ = [inherit([xtB[1]]) for _ in range(3)]
            sqB_ = [inherit(sgfB), inherit(sgfB)]
            mB_ = inherit([gBb])
            csB_ = [inherit([ubBs[0]]), inherit([ubBs[0]])]
            pyB = [Buf() for _ in range(16)]

            def load_u(c):
                K.dma(sp, ubuf[:, c % 2, :], p_glu[c], r=[gluB[c]], w=[ubB_[c % 2]])

            load_u(0)
            for u in range(18):
                if u + 1 < 16:
                    load_u(u + 1)
                if u < 16:
                    c = u
                    u_, y_ = ubuf[:, c % 2, :], ybuf[:, c % 3, :]
                    K.op(dve, lambda: nc.vector.tensor_scalar(out=y_, in0=u_[:, 98:98 + T], scalar1=cps[:, c, 0:1],
                                                              scalar2=cps[:, c, 31:32], op0=ALU.mult, op1=ALU.add),
                         r=[ubB_[c % 2], ccB], w=[yB_[c % 3]])
                    for j in range(1, 31):
                        K.op(dve, lambda: nc.vector.scalar_tensor_tensor(out=y_, in0=u_[:, 98 + j:98 + j + T],
                                                                         scalar=cps[:, c, j:j + 1], in1=y_, op0=ALU.mult,
                                                                         op1=ALU.add), r=[ubB_[c % 2]], w=[yB_[c % 3]])
                if 0 <= u - 1 < 16:
                    c = u - 1
                    K.op(act, lambda: nc.scalar.activation(out=sqb[:, c % 2, :], in_=ybuf[:, c % 3, :], func=AF.Square),
                         r=[yB_[c % 3]], w=[sqB_[c % 2]])
                if 0 <= u - 2 < 16:
                    c = u - 2
                    for tg in range(2):
                        sl = slice(tg * 512, (tg + 1) * 512)
                        K.op(pe, lambda: nc.tensor.matmul(banks[4 + tg][:, 0:512], onesf[:], ybuf[:, c % 3, sl],
                                                          start=(c == 0), stop=(c == 15)), r=[yB_[c % 3], ccB], w=[bankB[4 + tg]])
                        K.op(pe, lambda: nc.tensor.matmul(banks[6 + tg][:, 0:512], onesf[:], sqb[:, c % 2, sl],
                                                          start=(c == 0), stop=(c == 15)), r=[sqB_[c % 2], ccB], w=[bankB[6 + tg]])
                    K.dma(sp, p_y[c], ybuf[:, c % 3, :], r=[yB_[c % 3]], w=[pyB[c]])
                yield
            for tg in range(2):
                sl = slice(tg * 512, (tg + 1) * 512)
                K.op(act, lambda: nc.scalar.activation(out=mean[:, sl], in_=banks[4 + tg][:, 0:512], func=AF.Identity,
                                                       scale=1.0 / 2048), r=[bankB[4 + tg]], w=[mB_])
                K.op(dve, lambda: nc.vector.tensor_tensor(out=rstd[:, sl], in0=mean[:, sl], in1=mean[:, sl], op=ALU.mult), w=[mB_])
                K.op(dve, lambda: nc.vector.scalar_tensor_tensor(out=rstd[:, sl], in0=banks[6 + tg][:, 0:512], scalar=1.0 / 2048,
                                                                 in1=rstd[:, sl], op0=ALU.mult, op1=ALU.subtract),
                     r=[bankB[6 + tg]], w=[mB_])
            yield
            for tg in range(2):
                sl = slice(tg * 512, (tg + 1) * 512)
                K.op(act, lambda: nc.scalar.activation(out=rstd[:, sl], in_=rstd[:, sl], func=AF.Sqrt, bias=epsn[:, 1:2]),
                     r=[constB], w=[mB_])
            yield
            for tg in range(2):
                sl = slice(tg * 512, (tg + 1) * 512)
                K.op(dve, lambda: nc.vector.reciprocal(out=rstd[:, sl], in_=rstd[:, sl]), w=[mB_])

            def load_y(c):
                K.dma(sp, ybuf[:, c % 3, :], p_y[c], r=[pyB[c]], w=[yB_[c % 3]])

            load_y(0)
            yield
            for v in range(18):
                if v + 1 < 16:
                    load_y(v + 1)
                if v < 16:
                    c = v
                    y_ = ybuf[:, c % 3, :]
                    K.op(dve, lambda: nc.vector.tensor_tensor(out=y_, in0=y_, in1=mean, op=ALU.subtract), r=[mB_], w=[yB_[c % 3]])
                    K.op(dve, lambda: nc.vector.tensor_tensor(out=y_, in0=y_, in1=rstd, op=ALU.mult), r=[mB_], w=[yB_[c % 3]])
                if 0 <= v - 1 < 16:
                    c = v - 1
                    K.op(act, lambda: nc.scalar.activation(out=cstg[:, c % 2, :], in_=ybuf[:, c % 3, :], func=AF.Silu,
                                                           scale=cps[:, c, 32:33], bias=cps[:, c, 33:34]),
                         r=[yB_[c % 3], ccB], w=[csB_[c % 2]])
                if 0 <= v - 2 < 16:
                    c = v - 2
                    K.dma(sp, p_conv[c], cstg[:, c % 2, :], r=[csB_[c % 2]])
                yield

        cgen = conv_gen()
        cstate = [0, 0]

        def conv_tick():
            cstate[0] += 1
            period = 3 if cstate[1] < 18 else 2
            if cstate[0] % period == 0:
                cstate[1] += 1
                next(cgen, None)

        act_only[0] = True
        proj_fm([w_in_t[CH_Q + j] for j in range(16)], uT, uTB, KC, tg2,
                evac_store(lambda ci, lo, n: p_q[ci, :, lo:lo + n], None), after_chunk=conv_tick)
        proj_fm([w_in_t[CH_KV + j] for j in range(24)], uT, uTB, KC, tg2,
                evac_store(lambda ci, lo, n: p_kv[ci, :, 1024 + lo:1024 + lo + n], None), after_chunk=conv_tick)
        proj_fm([w_in_t[CH_GA + j] for j in range(32)], uT, uTB, KC, tg2,
                evac_store(lambda ci, lo, n: p_ga[ci, :, lo:lo + n], AF.Sigmoid), after_chunk=conv_tick)
        proj_fm([w_in_t[CH_GB + j] for j in range(32)], uT, uTB, KC, tg2,
                evac_store(lambda ci, lo, n: p_gb[ci, :, lo:lo + n], AF.Sigmoid), after_chunk=conv_tick)
        proj_fm([w_in_t[CH_NG]], uT, uTB, KC, tg2, evac_gates)
        for _ in cgen:
            pass
        K.barrier()

    def phase_B():
        with ExitStack() as es:
            w1k = es.enter_context(_sbuf_unique("w1k", [128, 8192], BF16))
            w1v = es.enter_context(_sbuf_unique("w1v", [128, 8192], BF16))
            w2k = es.enter_context(_sbuf_unique("w2k", [128, 256], BF16))
            w2v = es.enter_context(_sbuf_unique("w2v", [128, 256], BF16))
            posb = es.enter_context(_sbuf_unique("posb", [128, 64], BF16))
            m8s = es.enter_context(_sbuf_unique("m8s", [128, 8, 512], BF16))
            cvs = es.enter_context(_sbuf_unique("cvs", [128, T], BF16))
            selvs = es.enter_context(_sbuf_unique("selvs", [128, NT * 32], F32))
            selbs = es.enter_context(_sbuf_unique("selbs", [128, NT * 32], F32))
            emats = es.enter_context(_sbuf_unique("emats", [32, 2048], BF16))
            kvals = es.enter_context(_sbuf_unique("kvals", [128, 16], BF16))
            cb = es.enter_context(_sbuf_unique("cb", [128, 4], F32))
            qT = es.enter_context(_sbuf_unique("qT", [128, 4, T], BF16))
            kvT = es.enter_context(_sbuf_unique("kvT", [128, 6, 2048], BF16))
            Rs = es.enter_context(_sbuf_unique("Rs", [128, 16, 129], BF16))
            Rw = es.enter_context(_sbuf_unique("Rw", [128, 16, 129], BF16))
            Rc = es.enter_context(_sbuf_unique("Rc", [128, 161], BF16))
            hid = es.enter_context(_sbuf_unique("hid", [128, 4, 128], BF16))
            gx = es.enter_context(_sbuf_unique("gx", [128, 3, 128], F32))
            kcmpT = es.enter_context(_sbuf_unique("kcmpT", [128, 128], BF16))
            msk = es.enter_context(_sbuf_unique("msk", [128, 16, 512], BF16))
            pvs = es.enter_context(_sbuf_unique("pvs", [128, 8, 161], F32))
            selm8 = es.enter_context(_sbuf_unique("selm8", [128, 8, 32], BF16))
            selmB = Buf("selm8")
            ptmp = es.enter_context(_sbuf_unique("ptmp", [128, 4, 160], F32))
            ptmpB = Buf("ptmp")
            m8c = es.enter_context(_sbuf_unique("m8c", [128, 4, 512], BF16))
            m8cB = Buf("m8c")
            emt = es.enter_context(_sbuf_unique("emt", [128, 8, 512], BF16))
            oacc = es.enter_context(_sbuf_unique("oacc", [128, NT, 512], F32))
            obf = es.enter_context(_sbuf_unique("obf", [128, NT, 512], BF16))
            imp = es.enter_context(_sbuf_unique("imp", [128, NT, 32], F32))
            selT = es.enter_context(_sbuf_unique("selT", [32, T], BF16))
            tk = es.enter_context(_sbuf_unique("tk", [128, 2, 32], F32))
            mx = es.enter_context(_sbuf_unique("mx", [128, 16], F32))
            selm = es.enter_context(_sbuf_unique("selm", [128, 32], BF16))
            sm = es.enter_context(_sbuf_unique("sm", [128, 16], F32))
            ostg = es.enter_context(_sbuf_unique("ostg", [128, 2, T], BF16))
            qTB, kvB = Buf("qT"), [Buf(f"kv{i}") for i in range(6)]
            kvalB, selvB, selbB, ematB, m8B, cvsB, cbB = Buf(), Buf(), Buf(), Buf(), Buf(), Buf(), Buf()
            grpW = [Buf() for _ in range(5)]
            RsB, RwB, RcB, hidB, gxB, kcmpB = Buf("Rs"), Buf("Rw"), Buf("Rc"), Buf("hid"), Buf("gx"), Buf("kcmp")
            mskB, oaccB, obfB, impB, selTB, tkB, smB = Buf(), Buf(), Buf(), Buf(), Buf(), Buf(), Buf()
            etB = [Buf() for _ in range(6)]
            emtB = [Buf() for _ in range(8)]
            pvsB = [Buf() for _ in range(2)]
            pctr, sbctr = [0], [0]
            ostgB = [Buf(), Buf()]
            ectr = [0]
            K.dma(sp, kvals[:], kval[:, :], w=[kvalB])
            K.dma(sp, Rc[:, 128:161], rcc[:, :], w=[RcB])
            K.dma(sp, selvs[:], selv[:, :], w=[selvB])
            K.dma(sp, selbs[:], selb[:, :], w=[selbB])
            K.dma(sp, emats[:], emat[:, :], w=[ematB])
            K.dma(sp, qT[:], p_q[0:4].rearrange("h p t -> p h t"), w=[qTB])
            for kd in (3, 5, 0, 1, 2, 4):
                K.dma(sp, kvT[:, kd, :], p_kv[kd * 4], w=[kvB[kd]])
            K.dma(pool, w1k[:], w1k_t[:, :], w=[grpW[0]], max_dma_last_dim=8192)
            K.dma(pool, w1v[:], w1v_t[:, :], w=[grpW[1]], max_dma_last_dim=8192)
            K.dma(pool, w2k[:], w2k_t[:, :], w=[grpW[2]])
            K.dma(pool, w2v[:], w2v_t[:, :], w=[grpW[3]])
            K.dma(pool, posb[:], posT[:, :], w=[grpW[4]])
            K.dma(sp, m8s[:].rearrange("p a b -> p (a b)"), m8[:, :], w=[m8B])
            K.dma(sp, cvs[:], cvis[:, :], w=[cvsB])
            K.op(dve, lambda: nc.vector.memset(hid[:], 0.0), w=[hidB])
            K.op(dve, lambda: nc.vector.memset(kcmpT[:], 0.0), w=[kcmpB])
            K.op(dve, lambda: nc.vector.memset(Rc[:, 0:128], 0.0), w=[RcB])
            for jt in range(16):
                K.op(dve, lambda: nc.vector.tensor_copy(out=Rs[:, jt, 128:129], in_=kvals[:, jt:jt + 1]), r=[kvalB], w=[RsB])
                K.op(dve, lambda: nc.vector.memset(Rw[:, jt, 128:129], 1.0), w=[RwB])
            for i in range(NT):
                K.op(pe, lambda: nc.tensor.transpose(out=banks[7][:, 0:128], in_=gatesT[:, i * 128:(i + 1) * 128],
                                                     identity=idf[:]), r=[gatesTB, constB], w=[bankB[7]])
                K.op(dve, lambda: nc.vector.tensor_copy(out=gates[:, i * 48:(i + 1) * 48], in_=banks[7][:, 0:48]),
                     r=[bankB[7]], w=[gatesB])
            def post_group(qg, hh, head, br, first, with_imp, ncol):
                pi = pctr[0] % 2
                pctr[0] += 1
                pv = pvs[:, pi * 4:(pi + 1) * 4, :]
                pB = pvsB[pi]
                for qt in range(4):
                    K.op(dve, lambda: nc.vector.tensor_copy(out=pv[:, qt, 0:ncol], in_=banks[2 + qt][:, 0:ncol]),
                         r=[bankB[2 + qt]], w=[pB])
                K.op(dve, lambda: nc.vector.tensor_scalar_max(out=sm[:, 0:4], in0=pv[:, :, 128], scalar1=1e-30),
                     r=[pB], w=[smB])
                K.op(dve, lambda: nc.vector.reciprocal(out=sm[:, 4:8], in_=sm[:, 0:4]), w=[smB])
                gv = gates[:].rearrange("p (t c) -> p t c", c=48)[:, qg * 4:(qg + 1) * 4, head * 3 + br]
                K.op(dve, lambda: nc.vector.tensor_tensor(out=sm[:, 8:12], in0=sm[:, 4:8], in1=gv, op=ALU.mult),
                     r=[gatesB], w=[smB])
                dst = oacc[:, qg * 4:(qg + 1) * 4, hh * 128:(hh + 1) * 128]
                wbc = sm[:, 8:12].unsqueeze(2).to_broadcast([128, 4, 128])
                if first:
                    K.op(dve, lambda: nc.vector.tensor_tensor(out=dst, in0=pv[:, :, 0:128], in1=wbc, op=ALU.mult),
                         r=[pB, smB], w=[oaccB])
                else:
                    K.op(dve, lambda: nc.vector.tensor_tensor(out=ptmp[:, :, 0:128], in0=pv[:, :, 0:128], in1=wbc, op=ALU.mult),
                         r=[pB, smB], w=[ptmpB])
                    K.op(dve, lambda: nc.vector.tensor_tensor(out=dst, in0=dst, in1=ptmp[:, :, 0:128], op=ALU.add),
                         r=[ptmpB], w=[oaccB])
                if with_imp:
                    idst = imp[:, qg * 4:(qg + 1) * 4, :]
                    rbc = sm[:, 4:8].unsqueeze(2).to_broadcast([128, 4, 32])
                    if hh == 0:
                        K.op(dve, lambda: nc.vector.tensor_tensor(out=idst, in0=pv[:, :, 129:161], in1=rbc, op=ALU.mult),
                             r=[pB, smB], w=[impB])
                    else:
                        K.op(dve, lambda: nc.vector.tensor_tensor(out=ptmp[:, :, 128:160], in0=pv[:, :, 129:161], in1=rbc,
                                                                  op=ALU.mult), r=[pB, smB], w=[ptmpB])
                        K.op(dve, lambda: nc.vector.tensor_tensor(out=idst, in0=idst, in1=ptmp[:, :, 128:160], op=ALU.add),
                             r=[ptmpB], w=[impB])

            SBK = [0, 1, 6, 7]
            NE = 8
            DEPTH = 4

            def attend(hh, head, qg, jts, kT, kB, R, RB, ncol, mask_of, br, first, with_imp, qrange_of=None):
                n = len(jts)
                rng = [qrange_of(jt) if qrange_of is not None else (0, 512) for jt in jts]
                first_ji, last_ji = {}, {}
                for ji, (lo, hi) in enumerate(rng):
                    for qt in range(lo // 128, hi // 128):
                        first_ji.setdefault(qt, ji)
                        last_ji[qt] = ji
                assert sorted(first_ji) == [0, 1, 2, 3]

                def front(ji):
                    jt = jts[ji]
                    lo, hi = rng[ji]
                    sb = SBK[sbctr[0] % 4]
                    sbctr[0] += 1
                    mk, mkB = mask_of(jt)
                    fns = [lambda: nc.tensor.matmul(banks[sb][:, lo:hi], kT(jt), G["qT"][:, hh, qg * 512 + lo:qg * 512 + hi],
                                                    start=True, stop=False),
                           lambda: nc.tensor.matmul(banks[sb][:, lo:hi], idb[:], mk[:, lo:hi], start=False, stop=True)]
                    K.group(pe, fns, r=[kB, G["qTB"], mkB, constB], w=[bankB[sb]])
                    ei = ectr[0] % NE
                    ectr[0] += 1
                    K.op(act, lambda: nc.scalar.activation(out=emt[:, ei, lo:hi], in_=banks[sb][:, lo:hi], func=AF.Exp,
                                                           scale=SCALE), r=[bankB[sb]], w=[emtB[ei]])
                    return ei

                eis = [front(ji) for ji in range(min(DEPTH, n))]
                for ji in range(n):
                    ei = eis[ji]
                    jt = jts[ji]
                    lo, hi = rng[ji]
                    qts = list(range(lo // 128, hi // 128))
                    fns = [lambda qt=qt, ei=ei, jt=jt, ji=ji: nc.tensor.matmul(
                        banks[2 + qt][:, 0:ncol], emt[:, ei, qt * 128:(qt + 1) * 128], R(jt),
                        start=(ji == first_ji[qt]), stop=(ji == last_ji[qt])) for qt in qts]
                    K.group(pe, fns, r=[emtB[ei], RB], w=[bankB[2 + qt] for qt in qts])
                    if ji + DEPTH < n:
                        eis.append(front(ji + DEPTH))
                post_group(qg, hh, head, br, first, with_imp, ncol)

            G = {}
            qTs = [qT[:], wslots[0][:].rearrange("p (h t) -> p h t", h=4)]
            qTBs = [qTB, Buf("qT1")]
            kalt = wslots[1][:].rearrange("p (a t) -> p a t", a=2)
            kaltB = [Buf(), Buf()]
            Rss = [Rs[:], wslots[2][:, 0:16 * 129].rearrange("p (j c) -> p j c", c=129)]
            Rws = [Rw[:], wslots[3][:, 0:16 * 129].rearrange("p (j c) -> p j c", c=129)]
            RsBs, RwBs = [RsB, Buf()], [RwB, Buf()]
            for jt in range(16):
                K.op(dve, lambda: nc.vector.tensor_copy(out=Rss[1][:, jt, 128:129], in_=kvals[:, jt:jt + 1]), r=[kvalB], w=[RsBs[1]])
                K.op(dve, lambda: nc.vector.memset(Rws[1][:, jt, 128:129], 1.0), w=[RwBs[1]])

            def kvsrc(par, kd):
                if par == 1 and kd in (2, 4):
                    return kalt[:, (kd - 2) // 2, :], kaltB[(kd - 2) // 2]
                return kvT[:, kd, :], kvB[kd]

            def loads(g):
                par = g % 2
                K.dma(sp, qTs[par], p_q[4 * g:4 * g + 4].rearrange("h p t -> p h t"), w=[qTBs[par]])
                for kd in range(6):
                    t_, b_ = kvsrc(par, kd)
                    K.dma(sp, t_, p_kv[kd * 4 + g], w=[b_])

            def compute_cb():
                for a, w1 in enumerate((w1k, w1v)):
                    for hc in range(2):
                        col = a * 2 + hc
                        fns = [lambda l=l, w1=w1, hc=hc, col=col, a=a: nc.tensor.matmul(
                            banks[7][:, col:col + 1], w1[:, l * 256 + hc * 128:l * 256 + hc * 128 + 128],
                            posb[:, a * 32 + l:a * 32 + l + 1], start=(l == 0), stop=(l == 31)) for l in range(32)]
                        K.group(pe, fns, r=grpW, w=[bankB[7]])
                        K.op(dve, lambda: nc.vector.tensor_copy(out=cb[:, col:col + 1], in_=banks[7][:, col:col + 1]),
                             r=[bankB[7]], w=[cbB])


            def stage1(g, between=None):
                par = g % 2
                for kd, R, RB in ((3, Rss[par], RsBs[par]), (5, Rws[par], RwBs[par])):
                    vsrc, vB = kvsrc(par, kd)
                    for jb in range(2):
                        pst = banks[6 + (jb % 2)][:].bitcast(BF16)
                        fns = [lambda j=j, jb=jb, vsrc=vsrc, pst=pst: nc.tensor.transpose(
                            out=pst[:, j * 128:(j + 1) * 128], in_=vsrc[:, (jb * 8 + j) * 128:(jb * 8 + j + 1) * 128],
                            identity=idb[:]) for j in range(8)]
                        K.group(pe, fns, r=[vB, constB], w=[bankB[6 + (jb % 2)]])
                        K.op(act, lambda: nc.scalar.copy(out=R[:, jb * 8:(jb + 1) * 8, 0:128],
                                                         in_=pst.rearrange("p (a b) -> p a b", a=8)),
                             r=[bankB[6 + (jb % 2)]], w=[RB])
                if between is not None:
                    between()
                for a, (kd, w1, w2) in enumerate(((0, w1k, w2k), (1, w1v, w2v))):
                    csrc, cBf = kvsrc(par, kd)
                    v3 = csrc.rearrange("p (n s) -> p n s", s=16)
                    for hc in range(2):
                        fns = []
                        for l in range(32):
                            rhs = v3[:, 0:127, l] if l < 16 else v3[:, 1:128, l - 16]
                            fns.append(lambda l=l, rhs=rhs, w1=w1, hc=hc: nc.tensor.matmul(
                                banks[7][:, 0:127], w1[:, l * 256 + hc * 128:l * 256 + hc * 128 + 128], rhs,
                                start=(l == 0), stop=(l == 31)))
                        K.group(pe, fns, r=[cBf] + grpW, w=[bankB[7]])
                        x_ = gx[:, 0, 0:127]
                        t_ = gx[:, 1, 0:127]
                        K.op(act, lambda: nc.scalar.activation(out=x_, in_=banks[7][:, 0:127], func=AF.Identity,
                                                               bias=cb[:, a * 2 + hc:a * 2 + hc + 1]), r=[bankB[7], cbB], w=[gxB])
                        K.op(dve, lambda: nc.vector.tensor_tensor(out=t_, in0=x_, in1=x_, op=ALU.mult), w=[gxB])
                        K.op(dve, lambda: nc.vector.tensor_scalar(out=t_, in0=t_, scalar1=0.044715, scalar2=1.0,
                                                                  op0=ALU.mult, op1=ALU.add), w=[gxB])
                        K.op(dve, lambda: nc.vector.tensor_tensor(out=t_, in0=t_, in1=x_, op=ALU.mult), w=[gxB])
                        K.op(act, lambda: nc.scalar.activation(out=t_, in_=t_, func=AF.Tanh, scale=0.7978845608028654), w=[gxB])
                        K.op(dve, lambda: nc.vector.tensor_scalar(out=t_, in0=t_, scalar1=1.0, scalar2=0.5,
                                                                  op0=ALU.add, op1=ALU.mult), w=[gxB])
                        K.op(dve, lambda: nc.vector.tensor_tensor(out=hid[:, a * 2 + hc, 0:127], in0=t_, in1=x_, op=ALU.mult),
                             r=[gxB], w=[hidB])
                    if a == 0:
                        fns = [lambda hc=hc: nc.tensor.matmul(banks[7][:, 128:255], w2k[:, hc * 128:(hc + 1) * 128],
                                                              hid[:, hc, 0:127], start=(hc == 0), stop=(hc == 1)) for hc in range(2)]
                        K.group(pe, fns, r=[hidB] + grpW, w=[bankB[7]])
                        K.op(act, lambda: nc.scalar.copy(out=kcmpT[:, 0:127], in_=banks[7][:, 128:255]), r=[bankB[7]], w=[kcmpB])
                    else:
                        fns = [lambda hc=hc: nc.tensor.matmul(banks[7][0:127, 256:384], hid[:, 2 + hc, 0:127],
                                                              w2v[:, hc * 128:(hc + 1) * 128], start=(hc == 0), stop=(hc == 1))
                               for hc in range(2)]
                        K.group(pe, fns, r=[hidB] + grpW, w=[bankB[7]])
                        K.op(act, lambda: nc.scalar.copy(out=Rc[0:127, 0:128], in_=banks[7][0:127, 256:384]), r=[bankB[7]], w=[RcB])

            stage1(0, between=compute_cb)
            K.op(dve, lambda: nc.vector.tensor_scalar(out=m8c[:], in0=m8s[:, 4:8, :], scalar1=-MASK_BIG, scalar2=None,
                                                      op0=ALU.add), r=[m8B], w=[m8cB])
            for g in range(4):
                par = g % 2
                G["qT"], G["qTB"] = qTs[par], qTBs[par]
                if g + 1 < 4:
                    loads(g + 1)
                for hh in range(4):
                    for qg in range(2):
                        attend(hh, 4 * g + hh, qg, [0], lambda jt: kcmpT[:, :], kcmpB, lambda jt: Rc[:, :], RcB, 161,
                               lambda jt: (cvs[:, qg * 512:(qg + 1) * 512], cvsB), 0, True, True)
                for i in range(NT):
                    sc = tk[:, 0, :]
                    sc2 = tk[:, 1, :]
                    K.op(dve, lambda: nc.vector.tensor_tensor(out=sc, in0=imp[:, i, :], in1=selvs[:, i * 32:(i + 1) * 32],
                                                              op=ALU.mult), r=[impB, selvB, selbB], w=[tkB])
                    K.op(dve, lambda: nc.vector.tensor_tensor(out=sc, in0=sc, in1=selbs[:, i * 32:(i + 1) * 32], op=ALU.add), w=[tkB])
                    K.op(dve, lambda: nc.vector.max(out=mx[:, 0:8], in_=sc), w=[tkB])
                    K.op(dve, lambda: nc.vector.match_replace(out=sc2, in_to_replace=mx[:, 0:8], in_values=sc, imm_value=-3.0e4), w=[tkB])
                    K.op(dve, lambda: nc.vector.max(out=mx[:, 8:16], in_=sc2), w=[tkB])
                    K.op(dve, lambda: nc.vector.tensor_scalar(out=selm8[:, i, :], in0=sc, scalar1=mx[:, 15:16], scalar2=None,
                                                              op0=ALU.is_ge), r=[tkB], w=[selmB])
                if g + 1 < 4:
                    stage1(g + 1)
                ksT, ksB = kvsrc(par, 2)
                kwT, kwB = kvsrc(par, 4)
                Rs_, RsB_, Rw_, RwB_ = Rss[par], RsBs[par], Rws[par], RwBs[par]
                for qg in range(2):
                    jt0 = 4 + 4 * qg
                    for hh in range(4):
                        attend(hh, 4 * g + hh, qg, list(range(jt0, jt0 + 8)), lambda jt: kwT[:, jt * 128:(jt + 1) * 128], kwB,
                               lambda jt: Rw_[:, jt, :], RwB_, 129, lambda jt: (m8s[:, jt - jt0, :], m8B), 2, False, False,
                               qrange_of=lambda jt: ((0, (jt - jt0 + 1) * 128) if jt - jt0 <= 3 else ((jt - jt0 - 4) * 128, 512)))
                for i in range(NT):
                    pst = banks[6][:].bitcast(BF16)
                    K.op(pe, lambda: nc.tensor.transpose(out=pst[0:32, 0:128], in_=selm8[:, i, :], identity=idb[:]),
                         r=[selmB, constB], w=[bankB[6]])
                    K.op(act, lambda: nc.scalar.copy(out=selT[:, i * 128:(i + 1) * 128], in_=pst[0:32, 0:128]), r=[bankB[6]], w=[selTB])
                for qg in range(2):
                    njt = 12 + 4 * qg
                    for jt in range(njt):
                        bk = 6 + (jt % 2)
                        K.op(pe, lambda: nc.tensor.matmul(banks[bk][:, 0:512], emats[:, jt * 128:(jt + 1) * 128],
                                                          selT[:, qg * 512:(qg + 1) * 512], start=True, stop=True),
                             r=[selTB, ematB], w=[bankB[bk]])
                        dl = jt * 128 - (1024 + qg * 512)
                        if dl >= 0:
                            K.op(dve, lambda: nc.vector.scalar_tensor_tensor(out=msk[:, jt, :], in0=banks[bk][:, 0:512],
                                                                             scalar=MASK_BIG, in1=m8c[:, dl // 128, :],
                                                                             op0=ALU.mult, op1=ALU.add),
                                 r=[bankB[bk], m8cB], w=[mskB])
                        else:
                            K.op(dve, lambda: nc.vector.tensor_scalar(out=msk[:, jt, :], in0=banks[bk][:, 0:512], scalar1=-1.0,
                                                                      scalar2=MASK_BIG, op0=ALU.add, op1=ALU.mult),
                                 r=[bankB[bk]], w=[mskB])
                    for hh in range(4):
                        attend(hh, 4 * g + hh, qg, list(range(njt)), lambda jt: ksT[:, jt * 128:(jt + 1) * 128], ksB,
                               lambda jt: Rs_[:, jt, :], RsB_, 129, lambda jt: (msk[:, jt, :], mskB), 1, False, False,
                               qrange_of=lambda jt: (max(0, jt * 128 - (1024 + qg * 512)), 512))
                K.op(dve, lambda: nc.vector.tensor_copy(out=obf[:, 0:4, :], in_=oacc[:, 0:4, :]), r=[oaccB], w=[obfB])
                K.op(act, lambda: nc.scalar.copy(out=obf[:, 4:8, :], in_=oacc[:, 4:8, :]), r=[oaccB], w=[obfB])
                for hh in range(4):
                    bk = 6 + (hh % 2)
                    pst = banks[bk][:].bitcast(BF16)
                    fns = [lambda i=i, hh=hh, pst=pst: nc.tensor.transpose(
                        out=pst[:, i * 128:(i + 1) * 128], in_=obf[:, i, hh * 128:(hh + 1) * 128], identity=idb[:]) for i in range(NT)]
                    K.group(pe, fns, r=[obfB, constB], w=[bankB[bk]])
                    K.op(act, lambda: nc.scalar.copy(out=ostg[:, hh % 2, :], in_=pst), r=[bankB[bk]], w=[ostgB[hh % 2]])
                    K.dma(sp, p_attn[4 * g + hh], ostg[:, hh % 2, :], r=[ostgB[hh % 2]])
            K.barrier()

    if lvl >= order.index("B"):
        phase_B()

    def phase_C():
        with ExitStack() as es:
            sbt = lambda n, s, d: es.enter_context(_sbuf_unique(n, s, d))
            yall = sbt("yall", [128, 16, T], F32)
            ubuf = sbt("ubuf", [128, 2, 1152], F32)
            sq = sbt("sq", [128, 2, T], F32)
            mean = sbt("mean", [128, T], F32)
            rstd = sbt("rstd", [128, T], F32)
            onesf = sbt("onesf", [128, 128], F32)
            cps = sbt("cps", [128, 16, 34], F32)
            cstg = sbt("cstg", [128, 2, T], BF16)
            yB = [Buf() for _ in range(16)]
            ubB, sqB, cstB = [Buf(), Buf()], [Buf(), Buf()], [Buf(), Buf()]
            cB, mB = Buf(), Buf()
            K.op(dve, lambda: nc.vector.memset(onesf[:], 1.0), w=[cB])
            K.dma(sp, cps[:], convp[:, :, :], w=[cB])
            for c in range(16):
                u_, uB_ = ubuf[:, c % 2, :], ubB[c % 2]
                K.dma(sp, u_, p_glu[c], w=[uB_])
                y_ = yall[:, c, :]
                K.op(dve, lambda: nc.vector.tensor_scalar(out=y_, in0=u_[:, 98:98 + T], scalar1=cps[:, c, 0:1],
                                                          scalar2=cps[:, c, 31:32], op0=ALU.mult, op1=ALU.add),
                     r=[uB_, cB], w=[yB[c]])
                for j in range(1, 31):
                    K.op(dve, lambda: nc.vector.scalar_tensor_tensor(out=y_, in0=u_[:, 98 + j:98 + j + T], scalar=cps[:, c, j:j + 1],
                                                                     in1=y_, op0=ALU.mult, op1=ALU.add), r=[uB_], w=[yB[c]])
                s_, sB_ = sq[:, c % 2, :], sqB[c % 2]
                K.op(act, lambda: nc.scalar.activation(out=s_, in_=y_, func=AF.Square), r=[yB[c]], w=[sB_])
                for tg in range(2):
                    K.op(pe, lambda: nc.tensor.matmul(banks[tg][:, 0:512], onesf[:], y_[:, tg * 512:(tg + 1) * 512],
                                                      start=(c == 0), stop=(c == 15)), r=[yB[c], cB], w=[bankB[tg]])
                    K.op(pe, lambda: nc.tensor.matmul(banks[2 + tg][:, 0:512], onesf[:], s_[:, tg * 512:(tg + 1) * 512],
                                                      start=(c == 0), stop=(c == 15)), r=[sB_, cB], w=[bankB[2 + tg]])
            for tg in range(2):
                sl = slice(tg * 512, (tg + 1) * 512)
                K.op(act, lambda: nc.scalar.activation(out=mean[:, sl], in_=banks[tg][:, 0:512], func=AF.Identity, scale=1.0 / 2048),
                     r=[bankB[tg]], w=[mB])
                K.op(dve, lambda: nc.vector.tensor_tensor(out=rstd[:, sl], in0=mean[:, sl], in1=mean[:, sl], op=ALU.mult), w=[mB])
                K.op(dve, lambda: nc.vector.scalar_tensor_tensor(out=rstd[:, sl], in0=banks[2 + tg][:, 0:512], scalar=1.0 / 2048,
                                                                 in1=rstd[:, sl], op0=ALU.mult, op1=ALU.subtract),
                     r=[bankB[2 + tg]], w=[mB])
                K.op(act, lambda: nc.scalar.activation(out=rstd[:, sl], in_=rstd[:, sl], func=AF.Sqrt, bias=epsn[:, 1:2]),
                     r=[constB], w=[mB])
                K.op(dve, lambda: nc.vector.reciprocal(out=rstd[:, sl], in_=rstd[:, sl]), w=[mB])
            for c in range(16):
                y_ = yall[:, c, :]
                if c % 2 == 0:
                    K.op(dve, lambda: nc.vector.tensor_tensor(out=y_, in0=y_, in1=mean[:], op=ALU.subtract), r=[mB], w=[yB[c]])
                else:
                    K.op(pool, lambda: nc.gpsimd.tensor_tensor(out=y_, in0=y_, in1=mean[:], op=ALU.subtract), r=[mB], w=[yB[c]])
                K.op(dve, lambda: nc.vector.tensor_tensor(out=y_, in0=y_, in1=rstd[:], op=ALU.mult), r=[mB], w=[yB[c]])
                K.op(act, lambda: nc.scalar.activation(out=cstg[:, c % 2, :], in_=y_, func=AF.Silu, scale=cps[:, c, 32:33],
                                                       bias=cps[:, c, 33:34]), r=[yB[c], cB], w=[cstB[c % 2]])
                K.dma(sp, p_conv[c], cstg[:, c % 2, :], r=[cstB[c % 2]])
            K.barrier()

    ssq = nc.alloc_sbuf_tensor("ssq", [128, NT, 8], F32)
    ssqB = Buf("ssq")

    def tm_proj(wsrc_of, lhsT, lhsB, nkb, evac, pre_cg=None):
        for cg in range(8):
            if pre_cg is not None:
                pre_cg(cg)
            for kb in range(nkb):
                wt, wb = load_w(wsrc_of(cg, kb))
                for tt in range(NT):
                    fns = [lambda k8=k8, tt=tt, wt=wt, kb=kb: nc.tensor.matmul(
                        banks[tt][:, 0:512], lhsT[:, kb * 8 + k8, tt * 128:(tt + 1) * 128], wt[:, k8 * 512:(k8 + 1) * 512],
                        start=(kb == 0 and k8 == 0), stop=(kb == nkb - 1 and k8 == 7)) for k8 in range(8)]
                    K.group(pe, fns, r=[wb, lhsB[kb] if isinstance(lhsB, list) else lhsB], w=[bankB[tt]])
            for tt in range(NT):
                evac(cg, tt, banks[tt], bankB[tt])

    def phase_DE():
        with ExitStack() as es:
            sbt = lambda n, s, d: es.enter_context(_sbuf_unique(n, s, d))
            mergedT = sbt("mergedT", [128, KC, T], BF16)
            mgB = Buf("merged")
            with ExitStack() as es2:
                sb2 = lambda n, s, d: es2.enter_context(_sbuf_unique(n, s, d))
                convT = sb2("convT", [128, 16, T], BF16)
                attnT = sb2("attnT", [128, 16, T], BF16)
                gat = sb2("gat", [128, 2, 2, T], BF16)
                tmp = sb2("tmpD", [128, 4, 512], F32)
                cvBs, atBs = [Buf() for _ in range(4)], [Buf() for _ in range(4)]
                gaB = [[Buf(), Buf()], [Buf(), Buf()]]
                tmpB = [Buf() for _ in range(4)]
                for c4 in range(4):
                    K.dma(sp, convT[:, c4 * 4:(c4 + 1) * 4, :], p_conv[c4 * 4:(c4 + 1) * 4].rearrange("c p t -> p c t"), w=[cvBs[c4]])
                    K.dma(sp, attnT[:, c4 * 4:(c4 + 1) * 4, :], p_attn[c4 * 4:(c4 + 1) * 4].rearrange("c p t -> p c t"), w=[atBs[c4]])
                tctr = 0
                for c in range(32):
                    K.dma(sp, gat[:, c % 2, 0, :], p_ga[c], w=[gaB[c % 2][0]])
                    K.dma(sp, gat[:, c % 2, 1, :], p_gb[c], w=[gaB[c % 2][1]])
                    ws = []
                    for src_ in (w_co_t[c], w_ao_t[c]):
                        i = wctr[0] % 4
                        wctr[0] += 1
                        K.dma(pool, wslots[i][:, 0:2048], src_, w=[wslotB[i]], max_dma_last_dim=8192)
                        ws.append((wslots[i], wslotB[i]))
                    for tg in range(2):
                        sl = slice(tg * 512, (tg + 1) * 512)
                        for a, (rhs, rB) in enumerate(((convT, cvBs), (attnT, atBs))):
                            bk = (tg * 2 + a) % 4
                            wt, wb = ws[a]
                            fns = [lambda kc=kc, wt=wt, rhs=rhs, bk=bk: nc.tensor.matmul(
                                banks[bk][:, 0:512], wt[:, kc * 128:(kc + 1) * 128], rhs[:, kc, sl],
                                start=(kc == 0), stop=(kc == 15)) for kc in range(16)]
                            K.group(pe, fns, r=[wb] + rB, w=[bankB[bk]])
                        t1, t2 = tctr % 4, (tctr + 1) % 4
                        tctr += 2
                        K.op(dve, lambda: nc.vector.tensor_tensor(out=tmp[:, t1, :], in0=banks[(tg * 2) % 4][:, 0:512],
                                                                  in1=gat[:, c % 2, 0, sl], op=ALU.mult),
                             r=[bankB[(tg * 2) % 4], gaB[c % 2][0]], w=[tmpB[t1]])
                        K.op(dve, lambda: nc.vector.tensor_tensor(out=tmp[:, t2, :], in0=banks[(tg * 2 + 1) % 4][:, 0:512],
                                                                  in1=gat[:, c % 2, 1, sl], op=ALU.mult),
                             r=[bankB[(tg * 2 + 1) % 4], gaB[c % 2][1]], w=[tmpB[t2]])
                        K.op(dve, lambda: nc.vector.tensor_tensor(out=mergedT[:, c, sl], in0=tmp[:, t1, :], in1=tmp[:, t2, :],
                                                                  op=ALU.add), r=[tmpB[t1], tmpB[t2]], w=[mgB])
                K.barrier()
            if lvl < order.index("E"):
                return
            with ExitStack() as es3:
                sb3 = lambda n, s, d: es3.enter_context(_sbuf_unique(n, s, d))
                zst = sb3("zst", [128, 4, 512], F32)
                junk = sb3("junk", [128, 2, 512], BF16)
                zB = [Buf() for _ in range(4)]
                jB = [Buf(), Buf()]
                zc = [0]

                def evacE(cg, tt, bank, bb):
                    i = zc[0] % 4
                    zc[0] += 1
                    K.op(dve, lambda: nc.vector.tensor_copy(out=zst[:, i, :], in_=bank[:, 0:512]), r=[bb], w=[zB[i]])
                    K.op(act, lambda: nc.scalar.activation(out=junk[:, i % 2, :], in_=zst[:, i, :], func=AF.Square,
                                                           accum_out=ssq[:, tt, cg:cg + 1]), r=[zB[i]], w=[jB[i % 2], ssqB])
                    K.dma(sp, p_z[tt * 128:(tt + 1) * 128, cg * 512:(cg + 1) * 512], zst[:, i, :], r=[zB[i]])

                tm_proj(lambda cg, kb: w_out_t[cg, kb], mergedT, mgB, 4, evacE)
                K.barrier()

    def make_pre(zsrc, gP, gPb, zts, ztBs, st2, st2B, store_to):
        def pre(i, x_t, x_b):
            zt, ztB = zts[i % len(zts)], ztBs[i % len(zts)]
            s2 = st2[:, (i % 2) * 4:(i % 2) * 4 + 4]
            s2B = st2B[i % 2]
            K.dma(sp, zt[:], zsrc[i * 128:(i + 1) * 128, :], w=[ztB])
            K.op(dve, lambda: nc.vector.tensor_reduce(out=s2[:, 0:1], in_=ssq[:, i, :], axis=AX.X, op=ALU.add),
                 r=[ssqB], w=[s2B])
            K.op(act, lambda: nc.scalar.activation(out=s2[:, 1:2], in_=s2[:, 0:1], func=AF.Sqrt, scale=1.0 / D,
                                                   bias=epsn[:, 0:1]), r=[constB], w=[s2B])
            K.op(dve, lambda: nc.vector.reciprocal(out=s2[:, 2:3], in_=s2[:, 1:2]), w=[s2B])
            K.op(dve, lambda: nc.vector.scalar_tensor_tensor(out=zt[:], in0=zt[:], scalar=s2[:, 2:3], in1=gP, op0=ALU.mult,
                                                             op1=ALU.mult), r=[s2B, gPb], w=[ztB])
            K.op(dve, lambda: nc.vector.tensor_tensor(out=x_t[:], in0=x_t[:], in1=zt[:], op=ALU.add), r=[ztB], w=[x_b])
            if store_to is not None:
                K.dma(pool, store_to[i * 128:(i + 1) * 128, :], x_t[:], r=[x_b])
        return pre

    def phase_F():
        with ExitStack() as es:
            sbt = lambda n, s, d: es.enter_context(_sbuf_unique(n, s, d))
            hT = sbt("hT", [128, KC, T], BF16)
            hTB = [Buf(f"hT{i}") for i in range(4)]
            with ExitStack() as es2:
                sb2 = lambda n, s, d: es2.enter_context(_sbuf_unique(n, s, d))
                xt0, xt1 = sb2("xt0", [128, D], F32), sb2("xt1", [128, D], F32)
                zt = sb2("zt", [128, D], F32)
                gP, gM = sb2("gP", [128, D], F32), sb2("gM", [128, D], F32)
                ub = sb2("ub", [128, D], BF16)
                ubx = sb2("ubx", [128, D], BF16)
                st, st2 = sb2("st", [128, 8], F32), sb2("st2", [128, 8], F32)
                gPb, gMb, ubB, ztB, st2B = Buf(), Buf(), Buf(), Buf(), [Buf(), Buf()]
                K.dma(sp, gP[:], gains[1:2, :].to_broadcast([128, D]), w=[gPb])
                K.dma(sp, gM[:], gains[2:3, :].to_broadcast([128, D]), w=[gMb])
                pre = make_pre(p_z, gP[:], gPb, [zt], [ztB], st2, st2B, p_x1)
                norm_T(xo, gM[:], gMb, hT, hTB, NT, [xt0, xt1], [Buf(), Buf()], [ub, ubx], [ubB, Buf()], st, [Buf(), Buf()], pre=pre)
                K.barrier()
            with ExitStack() as es3:
                sb3 = lambda n, s, d: es3.enter_context(_sbuf_unique(n, s, d))
                actq = sb3("actq", [128, 32, T], BF16)
                rl = sb3("rl", [128, 4, 512], F32)
                part = sb3("part", [128, NT, 512], F32)
                ost = sb3("ost", [128, 4, 512], F32)
                junk = sb3("junk2", [128, 2, 512], BF16)
                aqB = [Buf(f"actq{i}") for i in range(4)]
                rlB = [Buf() for _ in range(4)]
                partB = [Buf() for _ in range(NT)]
                ostB = [Buf() for _ in range(4)]
                jB = [Buf(), Buf()]
                accB = [[Buf() for _ in range(8)] for _ in range(NT)]
                ctr = [0, 0]
                for qf in range(4):
                    def evacF(ci, gi, bank, bb, lo, n):
                        i = ctr[0] % 4
                        ctr[0] += 1
                        K.op(act, lambda: nc.scalar.activation(out=rl[:, i, :], in_=bank[:, 0:512], func=AF.Relu), r=[bb], w=[rlB[i]])
                        K.op(dve, lambda: nc.vector.tensor_tensor(out=actq[:, ci, lo:lo + n], in0=rl[:, i, :], in1=rl[:, i, :],
                                                                  op=ALU.mult), r=[rlB[i]], w=[aqB[ci // 8]])
                    proj_fm([w_up_t[qf * 32 + f] for f in range(32)], hT, hTB, KC, [(0, 512), (512, 512)], evacF)

                    def evacG(cg, tt, bank, bb, qf=qf):
                        i = ctr[1] % 4
                        ctr[1] += 1
                        dst = p_acc[tt * 128:(tt + 1) * 128, cg * 512:(cg + 1) * 512]
                        if qf == 0:
                            K.op(dve, lambda: nc.vector.tensor_copy(out=ost[:, i, :], in_=bank[:, 0:512]), r=[bb], w=[ostB[i]])
                        else:
                            K.op(dve, lambda: nc.vector.tensor_tensor(out=ost[:, i, :], in0=bank[:, 0:512], in1=part[:, tt, :],
                                                                      op=ALU.add), r=[bb, partB[tt]], w=[ostB[i]])
                        if qf == 3:
                            K.op(act, lambda: nc.scalar.activation(out=junk[:, i % 2, :], in_=ost[:, i, :], func=AF.Square,
                                                                   accum_out=ssq[:, tt, cg:cg + 1]), r=[ostB[i]], w=[jB[i % 2], ssqB])
                        K.dma(sp, dst, ost[:, i, :], r=[ostB[i]], w=[accB[tt][cg]])
                    def preG(cg, qf=qf):
                        if qf > 0:
                            for tt in range(NT):
                                K.dma(sp, part[:, tt, :], p_acc[tt * 128:(tt + 1) * 128, cg * 512:(cg + 1) * 512],
                                      r=[accB[tt][cg]], w=[partB[tt]])
                    tm_proj(lambda cg, kb, qf=qf: w_dn_t[qf, cg, kb], actq, aqB, 4, evacG, pre_cg=preG)
                K.barrier()
        with ExitStack() as es4:
            sb4 = lambda n, s, d: es4.enter_context(_sbuf_unique(n, s, d))
            xt = [sb4("xf0", [128, D], F32), sb4("xf1", [128, D], F32), sb4("xf2", [128, D], F32)]
            zts = [sb4("zf0", [128, D], F32), sb4("zf1", [128, D], F32), sb4("zf2", [128, D], F32)]
            gP = sb4("gF", [128, D], F32)
            st2 = sb4("stf", [128, 8], F32)
            gPb, ztBs, st2B = Buf(), [Buf(), Buf(), Buf()], [Buf(), Buf()]
            xB = [Buf(), Buf(), Buf()]
            K.dma(sp, gP[:], gains[3:4, :].to_broadcast([128, D]), w=[gPb])
            pre = make_pre(p_acc, gP[:], gPb, zts, ztBs, st2, st2B, out)
            for i in range(NT):
                K.dma(sp, xt[i % 3][:], p_x1[i * 128:(i + 1) * 128, :], w=[xB[i % 3]])
                pre(i, xt[i % 3], xB[i % 3])
            K.barrier()

    if lvl >= order.index("D"):
        phase_DE()
    if lvl >= order.index("F"):
        phase_F()

    K.barrier()
    nc.all_engine_barrier()
    cm.__exit__(None, None, None)
    return nc


def _col_index():
    idx = []
    for i in range(16):
        idx += list(range(128 * i, 128 * i + 128))
        idx += list(range(2048 + 128 * i, 2048 + 128 * i + 128))
    idx += list(range(4096, 6144))
    idx += list(range(6144, 9216))
    idx += list(range(9264, 13360))
    idx += list(range(13360, 17456))
    idx += list(range(9216, 9264))
    return np.asarray(idx, dtype=np.int64)


def _tile_cols(w, ncols_chunk=128):
    Kd, N = w.shape
    a = w.reshape(Kd // 128, 128, N // 128, 128).transpose(2, 1, 0, 3)
    return np.ascontiguousarray(a).reshape(N // 128, 128, (Kd // 128) * 128)


def _shared_inputs(inp):
    f32 = np.float32
    w_in = np.asarray(inp["w_in"], f32)[0]
    wp = np.zeros((D, NCH_IN * 128), f32)
    ci = _col_index()
    wp[:, :ci.size] = w_in[:, ci]
    sh = {}
    sh["w_in_t"] = _tile_cols(wp)
    del wp
    sh["w_co_t"] = _tile_cols(np.asarray(inp["w_conv_out"], f32)[0])
    sh["w_ao_t"] = _tile_cols(np.asarray(inp["w_attn_out"], f32)[0])
    w_out = np.asarray(inp["w_out"], f32)[0]
    sh["w_out_t"] = np.ascontiguousarray(
        w_out.reshape(4, 8, 128, 8, 512).transpose(3, 0, 2, 1, 4)).reshape(8, 4, 128, D)
    sh["w_up_t"] = _tile_cols(np.asarray(inp["w_up"], f32)[0])
    w_dn = np.asarray(inp["w_down"], f32)[0]
    sh["w_dn_t"] = np.ascontiguousarray(
        w_dn.reshape(4, 4, 8, 128, 8, 512).transpose(0, 4, 1, 3, 2, 5)).reshape(4, 8, 4, 128, D)
    sh["gains"] = np.ascontiguousarray(np.stack([np.asarray(inp[k], f32)[0] for k in
                                                 ("norm_mix_pre", "norm_mix_post", "norm_mlp_pre", "norm_mlp_post")]))
    cp = np.zeros((128, 16, 34), f32)
    cp[:, :, 0:31] = np.asarray(inp["w_dw"], f32)[0, :, 0, :].reshape(31, 16, 128).transpose(2, 1, 0)
    cp[:, :, 31] = np.asarray(inp["b_dw"], f32)[0].reshape(16, 128).T
    cp[:, :, 32] = np.asarray(inp["ln_conv_g"], f32)[0].reshape(16, 128).T
    cp[:, :, 33] = np.asarray(inp["ln_conv_b"], f32)[0].reshape(16, 128).T
    sh["convp"] = cp
    for nm, k1, k2 in (("k", "w_cmp_k1", "w_cmp_k2"), ("v", "w_cmp_v1", "w_cmp_v2")):
        w1 = np.asarray(inp[k1], f32)[0]
        sh[f"w1{nm}_t"] = np.ascontiguousarray(w1.reshape(32, 128, 256).transpose(1, 0, 2)).reshape(128, 8192)
        w2 = np.asarray(inp[k2], f32)[0]
        sh[f"w2{nm}_t"] = np.ascontiguousarray(w2.reshape(2, 128, 128).transpose(1, 0, 2)).reshape(128, 256)
    sh["posT"] = np.ascontiguousarray(np.concatenate([np.asarray(inp["pos_cmp_k"], f32)[0].T,
                                                      np.asarray(inp["pos_cmp_v"], f32)[0].T], axis=1))
    p = np.arange(128)[:, None, None]
    dl = (np.arange(8)[None, :, None] - 4) * 128
    qq = np.arange(512)[None, None, :]
    j = dl + p
    sh["m8"] = ((((j <= qq) & (j > qq - 512)).astype(np.float32) - 1.0) * MASK_BIG).astype(NPBF).reshape(128, 8 * 512)
    n = np.arange(128)[:, None]
    m = np.arange(32)[None, :]
    rc = np.zeros((128, 33), np.float32)
    rc[:127, 0] = 1.0
    rc[:, 1:] = ((n >= 4 * m - 1) & (n <= 4 * m + 3) & (n < 127))
    sh["rcc"] = rc.astype(NPBF)
    sh["emat"] = (np.arange(2048)[None, :] // 64 == np.arange(32)[:, None]).astype(NPBF)
    sh["identb"] = np.eye(128, dtype=np.float32).astype(NPBF)
    sh["identf"] = np.eye(128, dtype=np.float32)
    return sh


def _half_inputs(h):
    t = np.arange(T)
    t_real = t + 1024 * h
    n = np.arange(128)[:, None]
    cv = (16 * n + 31 <= t[None, :] + 1024) & (n <= 126)
    if h == 0:
        cv &= (n >= 64)
    m = np.arange(32)[None, :]
    jr = m - 16 * (1 - h)
    tr = t_real[:, None]
    valid = (jr >= 0) & (64 * jr <= tr)
    cur = tr // 64
    forced = (jr == 0) | (jr == cur) | (jr == cur - 1)
    sv = (valid & ~forced).astype(np.float32)
    sb = np.where(forced & valid, 1e4, np.where(valid, 0.0, -1e4)).astype(np.float32)
    lay = lambda a: np.ascontiguousarray(a.reshape(NT, 128, 32).transpose(1, 0, 2)).reshape(128, NT * 32)
    kv = np.ones((128, 16), np.float32)
    if h == 0:
        kv[:, :8] = 0.0
    cvb = ((cv.astype(np.float32) - 1.0) * MASK_BIG).astype(NPBF)
    return {"cvis": cvb, "selv": lay(sv), "selb": lay(sb), "kval": kv.astype(NPBF)}


def make_in_maps(inp):
    sh = _shared_inputs(inp)
    x = np.asarray(inp["x"], np.float32)
    halves = [_half_inputs(0), _half_inputs(1)]
    zeros = np.zeros((T, D), np.float32)
    maps = []
    for c in range(8):
        b, h = c // 2, c % 2
        mp = dict(sh)
        mp.update(halves[h])
        mp["xo"] = np.ascontiguousarray(x[b, h * T:(h + 1) * T])
        mp["xc"] = np.ascontiguousarray(x[b, 0:T]) if h == 1 else zeros
        maps.append(mp)
    return maps


def kernel(**inputs):
    maps = make_in_maps(inputs)
    nc = build_nc()
    res = run_bass_kernel_spmd(nc, maps, core_ids=list(range(8)))
    outp = np.zeros((4, 2 * T, D), np.float32)
    for c in range(8):
        b, h = c // 2, c % 2
        outp[b, h * T:(h + 1) * T] = np.asarray(res.results[c]["out"], np.float32)
    return outp
```
